# Optimizing a Trainium2 kernel written in Bass

```python
import jax, jax.numpy as jnp
from jax import lax
import numpy as np

D_MODEL = 1024
BATCH = 2
SEQ = 16384
DEPTH = 4

MEM_LEN = 256
MIX_WIDTH = D_MODEL
D_FF = ((8 * D_MODEL // 3 + 255) // 256) * 256
XATTN_HEADS = 4
XATTN_HEAD_DIM = D_MODEL // XATTN_HEADS
POOL_WINDOWS = (2, 4, 8, 16)
POOL_GROUPS = 4
POOL_WIDTH = MIX_WIDTH // 2
POOL_GROUP_DIM = POOL_WIDTH // POOL_GROUPS
SGU_WIDTH = MIX_WIDTH // 2
SGU_HEADS = 4
SGU_CHUNK = 128
EVEN_IN = POOL_WIDTH + 2 * SGU_WIDTH
RWKV_WIDTH = MIX_WIDTH // 2
RWKV_HEAD_DIM = 64
RWKV_HEADS = RWKV_WIDTH // RWKV_HEAD_DIM
DECAY_RANK = 64
A_RANK = 64
GATE_RANK = 128
RWKV_IN = 3 * RWKV_WIDTH + DECAY_RANK + A_RANK + GATE_RANK
RWKV_GN_EPS = 64e-5
LRU_WIDTH = MIX_WIDTH // 2
LRU_BLOCKS = 8
LRU_BLOCK_DIM = LRU_WIDTH // LRU_BLOCKS
LRU_C = 8.0
CONV_WIDTH = 4
ODD_IN = RWKV_IN + 2 * LRU_WIDTH

N_EVEN = (DEPTH + 1) // 2
N_ODD = DEPTH // 2
LN_EPS = 1e-5
DEEPNORM_ALPHA = (2 * DEPTH) ** 0.25
DEEPNORM_BETA = (8 * DEPTH) ** -0.25
MACARON_WEIGHT = 0.5

kernel_name = "hybrid_pool_sgu_rwkv7_rglru_macaron_deepnorm"


def layer_norm(x, g, b, eps=LN_EPS):
    xf = x.astype(jnp.float32)
    mu = jnp.mean(xf, -1, keepdims=True)
    var = jnp.mean(jnp.square(xf - mu), -1, keepdims=True)
    return ((xf - mu) * lax.rsqrt(var + eps)).astype(x.dtype) * g + b


def shift_right(y):
    return jnp.pad(y[:, :-1], ((0, 0), (1, 0), (0, 0)))


def swiglu_ffn(x, w_in, w_out):
    gate, up = jnp.split(x @ w_in, 2, axis=-1)
    return (jax.nn.silu(gate) * up) @ w_out


def multiscale_pool(xa, pool_w, pool_scale):
    B, S, _ = xa.shape
    xg = xa.reshape(B, S, POOL_GROUPS, POOL_GROUP_DIM)
    csum = jnp.cumsum(xg.astype(jnp.float32), axis=1)
    pos = jnp.arange(1, S + 1, dtype=jnp.float32)[None, :, None]
    means = []
    for g, win in enumerate(POOL_WINDOWS):
        c = csum[:, :, g]
        lagged = jnp.pad(c[:, :-win], ((0, 0), (win, 0), (0, 0)))
        means.append((c - lagged) / jnp.minimum(pos, win))
    pooled = jnp.stack(means, axis=2).astype(xa.dtype) - xg
    y = jnp.einsum('bsgc,gcd->bsgd', pooled, pool_w).reshape(B, S, POOL_WIDTH)
    return y * pool_scale


def spatial_gating(u, v, ln_g, ln_b, w_s, b_s):
    B, S, W = v.shape
    v = layer_norm(v, ln_g, ln_b)
    vc = v.reshape(B, S // SGU_CHUNK, SGU_CHUNK, SGU_HEADS, W // SGU_HEADS)
    ws = jnp.tril(w_s)
    mixed = jnp.einsum('hts,bnshd->bnthd', ws, vc) + b_s.T[None, None, :, :, None]
    return u * mixed.reshape(B, S, W)


def even_mixer(x, w_in, w_out, pool_w, pool_scale, sgu_ln_g, sgu_ln_b, sgu_w, sgu_b):
    h = x @ w_in
    xa, u, v = jnp.split(h, [POOL_WIDTH, POOL_WIDTH + SGU_WIDTH], axis=-1)
    ya = multiscale_pool(xa, pool_w, pool_scale)
    yb = spatial_gating(jax.nn.gelu(u), jax.nn.gelu(v), sgu_ln_g, sgu_ln_b, sgu_w, sgu_b)
    return jnp.concatenate([ya, yb], axis=-1) @ w_out


def rwkv7_step(state, inp):
    r, w, k, v, a, b = inp
    sa = jnp.einsum('bhij,bhj->bhi', state, a)
    state = (state * w[:, :, None, :] + sa[..., None] * b[:, :, None, :]
             + v[..., None] * k[:, :, None, :])
    return state, jnp.einsum('bhij,bhj->bhi', state, r)


def rwkv7_time_mix(r, k, v, wd, ad, gd, w0, w_up, a0, a_up, g_up, k_k, k_a, r_k, gn_g, gn_b):
    B, S, W = r.shape
    H, N = RWKV_HEADS, RWKV_HEAD_DIM
    f32 = jnp.float32
    heads = lambda t: t.reshape(B, S, H, N)
    logw = -jax.nn.softplus(-(w0 + jnp.tanh(wd) @ w_up)) - 0.5
    decay = jnp.exp(-jnp.exp(logw.astype(f32)))
    a = jax.nn.sigmoid(a0 + ad @ a_up)
    g = jax.nn.sigmoid(gd) @ g_up
    kk = heads(k * k_k).astype(f32)
    kk = kk * lax.rsqrt(jnp.maximum(jnp.sum(kk * kk, -1, keepdims=True), 1e-24))
    k = k * (1 + (a - 1) * k_a)
    rh, kh, vh = heads(r), heads(k), heads(v)
    seq_first = lambda t: jnp.moveaxis(t.astype(f32), 1, 0)
    xs = (seq_first(rh), seq_first(heads(decay)), seq_first(kh), seq_first(vh),
          seq_first(-kk), seq_first(kk * heads(a).astype(f32)))
    state0 = jnp.zeros((B, H, N, N), f32)
    _, y = lax.scan(rwkv7_step, state0, xs)
    y = jnp.moveaxis(y, 0, 1)
    mu = jnp.mean(y, -1, keepdims=True)
    var = jnp.mean(jnp.square(y - mu), -1, keepdims=True)
    y = ((y - mu) * lax.rsqrt(var + RWKV_GN_EPS)).reshape(B, S, W).astype(r.dtype) * gn_g + gn_b
    bonus = jnp.sum(rh * kh * r_k, -1, keepdims=True) * vh
    return (y + bonus.reshape(B, S, W)) * g


def _linear_recurrence_combine(c1, c2):
    a1, b1 = c1
    a2, b2 = c2
    return a1 * a2, a2 * b1 + b2


def rglru_branch(xr, gate, conv_w, conv_b, w_a, b_a, w_x, b_x, lam):
    B, S, W = xr.shape
    f32 = jnp.float32
    xc = lax.conv_general_dilated(xr, conv_w[:, None, :], window_strides=(1,),
                                  padding=[(CONV_WIDTH - 1, 0)],
                                  dimension_numbers=('NWC', 'WIO', 'NWC'),
                                  feature_group_count=W) + conv_b
    xb = xc.reshape(B, S, LRU_BLOCKS, LRU_BLOCK_DIM)
    rec = jax.nn.sigmoid(jnp.einsum('bshi,hij->bshj', xb, w_a).reshape(B, S, W) + b_a)
    inp = jax.nn.sigmoid(jnp.einsum('bshi,hij->bshj', xb, w_x).reshape(B, S, W) + b_x)
    log_a = -LRU_C * rec.astype(f32) * jax.nn.softplus(-lam.astype(f32))
    a = jnp.exp(log_a)
    bx = jnp.sqrt(-jnp.expm1(2 * log_a)) * (inp * xc).astype(f32)
    _, h = lax.associative_scan(_linear_recurrence_combine, (a, bx), axis=1)
    return h.astype(xr.dtype) * jax.nn.gelu(gate)


def odd_mixer(x, w_in, w_out, mu, w0, w_up, a0, a_up, g_up, k_k, k_a, r_k, gn_g, gn_b,
              conv_w, conv_b, w_a, b_a, w_x, b_x, lam):
    h = x @ w_in
    hc, hd = h[..., :RWKV_IN], h[..., RWKV_IN:]
    hc = hc + mu * (shift_right(hc) - hc)
    sizes = [RWKV_WIDTH, RWKV_WIDTH, RWKV_WIDTH, DECAY_RANK, A_RANK, GATE_RANK]
    r, k, v, wd, ad, gd = jnp.split(hc, np.cumsum(sizes)[:-1].tolist(), axis=-1)
    yc = rwkv7_time_mix(r, k, v, wd, ad, gd, w0, w_up, a0, a_up, g_up, k_k, k_a, r_k, gn_g, gn_b)
    gate, xr = jnp.split(hd, 2, axis=-1)
    yd = rglru_branch(xr, gate, conv_w, conv_b, w_a, b_a, w_x, b_x, lam)
    return jnp.concatenate([yc, yd], axis=-1) @ w_out


def memory_cross_attention(x, mem, w_q, w_kv, w_o):
    B, S, _ = x.shape
    M = mem.shape[1]
    q = (x @ w_q).reshape(B, S, XATTN_HEADS, XATTN_HEAD_DIM)
    k, v = jnp.split(mem @ w_kv, 2, axis=-1)
    k = k.reshape(B, M, XATTN_HEADS, XATTN_HEAD_DIM)
    v = v.reshape(B, M, XATTN_HEADS, XATTN_HEAD_DIM)
    s = jnp.einsum('bshd,bmhd->bhsm', q, k, preferred_element_type=jnp.float32) * (XATTN_HEAD_DIM ** -0.5)
    p = jax.nn.softmax(s, axis=-1).astype(x.dtype)
    o = jnp.einsum('bhsm,bmhd->bshd', p, v).reshape(B, S, D_MODEL)
    return o @ w_o


def setup_inputs(seed: int = 0) -> dict:
    key = jax.random.key(seed)
    keys = jax.random.split(key, 64)
    ks = iter([keys[i] for i in range(64)])

    def nrm(shape, scale):
        return jax.random.normal(next(ks), shape, jnp.float32) * scale

    def unif(shape, lo, hi):
        return jax.random.uniform(next(ks), shape, jnp.float32, lo, hi)

    L, E, O = DEPTH, N_EVEN, N_ODD
    D, F = D_MODEL, D_FF
    beta = DEEPNORM_BETA
    x = nrm((BATCH, SEQ, D), 1.0)
    mem = nrm((BATCH, MEM_LEN, D), 1.0)
    ffn1_w_in = nrm((L, D, 2 * F), D ** -0.5)
    ffn1_w_out = nrm((L, F, D), F ** -0.5 * beta)
    ffn2_w_in = nrm((L, D, 2 * F), D ** -0.5)
    ffn2_w_out = nrm((L, F, D), F ** -0.5 * beta)
    ln_g = 1.0 + nrm((L, 4, D), 0.02)
    ln_b = nrm((L, 4, D), 0.02)
    xattn_w_q = nrm((L, D, D), D ** -0.5)
    xattn_w_kv = jnp.concatenate([nrm((L, D, D), D ** -0.5), nrm((L, D, D), D ** -0.5 * beta)], axis=-1)
    xattn_w_o = nrm((L, D, D), D ** -0.5 * beta)
    even_w_in = nrm((E, D, EVEN_IN), D ** -0.5)
    even_w_out = nrm((E, MIX_WIDTH, D), MIX_WIDTH ** -0.5 * beta)
    pool_w = nrm((E, POOL_GROUPS, POOL_GROUP_DIM, POOL_GROUP_DIM), POOL_GROUP_DIM ** -0.5)
    pool_scale = 1.0 + nrm((E, POOL_WIDTH), 0.1)
    sgu_ln_g = 1.0 + nrm((E, SGU_WIDTH), 0.02)
    sgu_ln_b = nrm((E, SGU_WIDTH), 0.02)
    sgu_w = nrm((E, SGU_HEADS, SGU_CHUNK, SGU_CHUNK), 0.02)
    sgu_b = 1.0 + nrm((E, SGU_HEADS, SGU_CHUNK), 0.01)
    odd_w_in = nrm((O, D, ODD_IN), D ** -0.5)
    odd_w_out = nrm((O, MIX_WIDTH, D), MIX_WIDTH ** -0.5 * beta)
    rwkv_mu = unif((O, RWKV_IN), 0.0, 1.0)
    rwkv_w0 = unif((O, RWKV_WIDTH), -6.0, -1.0)
    rwkv_w_up = nrm((O, DECAY_RANK, RWKV_WIDTH), 0.1 * DECAY_RANK ** -0.5)
    rwkv_a0 = nrm((O, RWKV_WIDTH), 0.1)
    rwkv_a_up = nrm((O, A_RANK, RWKV_WIDTH), 0.1 * A_RANK ** -0.5)
    rwkv_g_up = nrm((O, GATE_RANK, RWKV_WIDTH), GATE_RANK ** -0.5)
    rwkv_k_k = 0.85 + nrm((O, RWKV_WIDTH), 0.02)
    rwkv_k_a = 1.0 + nrm((O, RWKV_WIDTH), 0.02)
    rwkv_r_k = nrm((O, RWKV_HEADS, RWKV_HEAD_DIM), 0.1)
    rwkv_gn_g = 1.0 + nrm((O, RWKV_WIDTH), 0.02)
    rwkv_gn_b = nrm((O, RWKV_WIDTH), 0.02)
    lru_conv_w = nrm((O, CONV_WIDTH, LRU_WIDTH), CONV_WIDTH ** -0.5)
    lru_conv_b = nrm((O, LRU_WIDTH), 0.02)
    lru_w_a = nrm((O, LRU_BLOCKS, LRU_BLOCK_DIM, LRU_BLOCK_DIM), LRU_BLOCK_DIM ** -0.5)
    lru_b_a = nrm((O, LRU_WIDTH), 0.02)
    lru_w_x = nrm((O, LRU_BLOCKS, LRU_BLOCK_DIM, LRU_BLOCK_DIM), LRU_BLOCK_DIM ** -0.5)
    lru_b_x = nrm((O, LRU_WIDTH), 0.02)
    a_pow_c = unif((O, LRU_WIDTH), 0.9, 0.999)
    lru_lambda = -jnp.log(jnp.expm1(-jnp.log(a_pow_c) / LRU_C))
    return {
        "x": x, "mem": mem,
        "ffn1_w_in": ffn1_w_in, "ffn1_w_out": ffn1_w_out,
        "ffn2_w_in": ffn2_w_in, "ffn2_w_out": ffn2_w_out,
        "ln_g": ln_g, "ln_b": ln_b,
        "xattn_w_q": xattn_w_q, "xattn_w_kv": xattn_w_kv, "xattn_w_o": xattn_w_o,
        "even_w_in": even_w_in, "even_w_out": even_w_out,
        "pool_w": pool_w, "pool_scale": pool_scale,
        "sgu_ln_g": sgu_ln_g, "sgu_ln_b": sgu_ln_b, "sgu_w": sgu_w, "sgu_b": sgu_b,
        "odd_w_in": odd_w_in, "odd_w_out": odd_w_out,
        "rwkv_mu": rwkv_mu, "rwkv_w0": rwkv_w0, "rwkv_w_up": rwkv_w_up,
        "rwkv_a0": rwkv_a0, "rwkv_a_up": rwkv_a_up, "rwkv_g_up": rwkv_g_up,
        "rwkv_k_k": rwkv_k_k, "rwkv_k_a": rwkv_k_a, "rwkv_r_k": rwkv_r_k,
        "rwkv_gn_g": rwkv_gn_g, "rwkv_gn_b": rwkv_gn_b,
        "lru_conv_w": lru_conv_w, "lru_conv_b": lru_conv_b,
        "lru_w_a": lru_w_a, "lru_b_a": lru_b_a, "lru_w_x": lru_w_x, "lru_b_x": lru_b_x,
        "lru_lambda": lru_lambda,
    }


def reference(x, mem, ffn1_w_in, ffn1_w_out, ffn2_w_in, ffn2_w_out, ln_g, ln_b,
              xattn_w_q, xattn_w_kv, xattn_w_o, even_w_in, even_w_out, pool_w, pool_scale,
              sgu_ln_g, sgu_ln_b, sgu_w, sgu_b, odd_w_in, odd_w_out, rwkv_mu, rwkv_w0,
              rwkv_w_up, rwkv_a0, rwkv_a_up, rwkv_g_up, rwkv_k_k, rwkv_k_a, rwkv_r_k,
              rwkv_gn_g, rwkv_gn_b, lru_conv_w, lru_conv_b, lru_w_a, lru_b_a, lru_w_x,
              lru_b_x, lru_lambda):
    def post_norm(h, sub, l, j):
        return layer_norm(DEEPNORM_ALPHA * h + sub, ln_g[l, j], ln_b[l, j])

    for l in range(DEPTH):
        x = post_norm(x, MACARON_WEIGHT * swiglu_ffn(x, ffn1_w_in[l], ffn1_w_out[l]), l, 0)
        e = l // 2
        if l % 2 == 0:
            mix = even_mixer(x, even_w_in[e], even_w_out[e], pool_w[e], pool_scale[e],
                             sgu_ln_g[e], sgu_ln_b[e], sgu_w[e], sgu_b[e])
        else:
            mix = odd_mixer(x, odd_w_in[e], odd_w_out[e], rwkv_mu[e], rwkv_w0[e], rwkv_w_up[e],
                            rwkv_a0[e], rwkv_a_up[e], rwkv_g_up[e], rwkv_k_k[e], rwkv_k_a[e],
                            rwkv_r_k[e], rwkv_gn_g[e], rwkv_gn_b[e], lru_conv_w[e], lru_conv_b[e],
                            lru_w_a[e], lru_b_a[e], lru_w_x[e], lru_b_x[e], lru_lambda[e])
        x = post_norm(x, mix, l, 1)
        x = post_norm(x, memory_cross_attention(x, mem, xattn_w_q[l], xattn_w_kv[l], xattn_w_o[l]), l, 2)
        x = post_norm(x, MACARON_WEIGHT * swiglu_ffn(x, ffn2_w_in[l], ffn2_w_out[l]), l, 3)
    return x
```

```python
import contextlib
import numpy as np
import ml_dtypes
import concourse.bass as bass
import concourse.mybir as mybir
from concourse.bass_utils import run_bass_kernel_spmd

F32 = mybir.dt.float32
BF16 = mybir.dt.bfloat16
AF = mybir.ActivationFunctionType
ALU = mybir.AluOpType
AX = mybir.AxisListType
NPBF = ml_dtypes.bfloat16

NCORES = 8
D = 1024
DFF = 2816
DEPTH = 4
MEM = 256
ALPHA = (2 * DEPTH) ** 0.25
LN_EPS = 1e-5
GN_EPS = 64e-5
HALO = 16
ENGS = ("pe", "act", "dve", "pool", "sp")


class Reg:
    _n = 0

    def __init__(self, name=""):
        Reg._n += 1
        self.id = Reg._n
        self.name = name
        self.lw = None
        self.rd = {}
        self.dsem = None
        self.dcnt = 0


class V:
    def __init__(self, ap, r):
        self.ap = ap
        self.r = r

    def __getitem__(self, idx):
        return V(self.ap[idx], self.r)

    def re(self, pat, **kw):
        return V(self.ap.rearrange(pat, **kw), self.r)


class Buf:
    def __init__(self, P, name, shape, dt, psum=False, reg=None):
        self.t = (P.ps if psum else P.sb)(name, shape, dt)
        self.r = reg or Reg(name)

    def __getitem__(self, idx):
        return V(self.t[idx], self.r)


def _regs(*vs):
    return [v.r for v in vs if isinstance(v, V)]


def _a(v):
    return v.ap if isinstance(v, V) else v


class Prog:
    def __init__(self):
        self.nc = bass.Bass("TRN2", target_bir_lowering=False)
        self.es = contextlib.ExitStack()
        self.q = {e: [] for e in ENGS}
        self.seq = {e: 0 for e in ENGS}
        self.seen = {e: {} for e in ENGS}
        self.sem = {e: self.es.enter_context(self.nc.semaphore("s_" + e)) for e in ENGS}
        self.ninst = 0

    def sb(self, name, shape, dt):
        return self.es.enter_context(self.nc.sbuf_tensor(name, list(shape), dt))

    def ps(self, name, shape, dt=F32):
        return self.es.enter_context(self.nc.psum_tensor(name, list(shape), dt))

    def dram(self, name, shape, dt, kind="ExternalInput"):
        return self.nc.dram_tensor(name, list(shape), dt, kind=kind).ap()

    def _deps(self, reads, writes):
        toks = {}

        def add(t):
            if t is None:
                return
            k, v = t
            if toks.get(k, 0) < v:
                toks[k] = v
        for r in reads:
            add(r.lw)
        for w in writes:
            add(w.lw)
            for k, v in w.rd.items():
                add((k, v))
        return toks

    def _emit_waits(self, eng, toks, skip_self=False):
        seen = self.seen[eng]
        for k, v in toks.items():
            if skip_self and k == eng:
                continue
            if seen.get(k, 0) >= v:
                continue
            seen[k] = v
            semh = self.sem[k] if isinstance(k, str) else k[1]
            self.q[eng].append(lambda e, s=semh, v=v: e.wait_ge(s, v))

    def _commit(self, tok, reads, writes):
        k, v = tok
        for w in writes:
            w.lw = tok
            w.rd = {}
        for r in reads:
            if r in writes:
                continue
            if r.rd.get(k, 0) < v:
                r.rd[k] = v

    def op(self, eng, fn, reads=(), writes=(), skip_self=False):
        reads = list(reads)
        writes = list(writes)
        self._emit_waits(eng, self._deps(reads, writes), skip_self)
        self.seq[eng] += 1
        v = self.seq[eng]
        semh = self.sem[eng]
        self.q[eng].append(lambda e, fn=fn, s=semh: fn(e).then_inc(s, 1))
        self._commit((eng, v), reads, writes)
        self.ninst += 1

    def dma(self, eng, out, in_, **kw):
        reads = _regs(in_)
        writes = _regs(out)
        self._emit_waits(eng, self._deps(reads, writes))
        r = writes[0] if writes else reads[0]
        if r.dsem is None:
            r.dsem = self.es.enter_context(self.nc.semaphore("d%d" % r.id))
        r.dcnt += 16
        semh = r.dsem
        o, i = _a(out), _a(in_)
        self.q[eng].append(lambda e, o=o, i=i, s=semh, kw=kw: e.dma_start(out=o, in_=i, **kw).then_inc(s, 16))
        self._commit((("d", semh, r.id), r.dcnt), reads, writes)
        self.ninst += 1

    def wait_all(self, eng, regs):
        toks = {}
        for r in regs:
            if r.lw is not None:
                k, v = r.lw
                toks[k] = max(toks.get(k, 0), v)
            for k, v in r.rd.items():
                toks[k] = max(toks.get(k, 0), v)
        self._emit_waits(eng, toks)

    def tt(self, eng, out, a, b, op):
        self.op(eng, lambda e: e.tensor_tensor(out=out.ap, in0=a.ap, in1=b.ap, op=op),
                reads=_regs(a, b), writes=[out.r])

    def ts(self, eng, out, a, s1, op0, s2=None, op1=None, accum=None):
        kw = {}
        if op1 is not None:
            kw["op1"] = op1
        if accum is not None:
            kw["accum_out"] = accum.ap
        self.op(eng, lambda e: e.tensor_scalar(out=out.ap, in0=a.ap, scalar1=_a(s1), scalar2=_a(s2), op0=op0, **kw),
                reads=_regs(a, s1, s2), writes=[out.r] + _regs(accum))

    def stt(self, eng, out, a, s, b, op0, op1):
        self.op(eng, lambda e: e.scalar_tensor_tensor(out=out.ap, in0=a.ap, scalar=_a(s), in1=b.ap, op0=op0, op1=op1),
                reads=_regs(a, s, b), writes=[out.r])

    def act(self, out, a, func, bias=None, scale=None, accum=None):
        kw = {}
        if bias is not None:
            kw["bias"] = _a(bias)
        if scale is not None:
            kw["scale"] = _a(scale)
        if accum is not None:
            kw["accum_out"] = accum.ap
        self.op("act", lambda e: e.activation(out=out.ap, in_=a.ap, func=func, **kw),
                reads=_regs(a, bias, scale), writes=[out.r] + _regs(accum))

    def cp(self, eng, out, a):
        if eng == "act":
            self.op(eng, lambda e: e.copy(out=out.ap, in_=a.ap), reads=[a.r], writes=[out.r])
        else:
            self.op(eng, lambda e: e.tensor_copy(out=out.ap, in_=a.ap), reads=[a.r], writes=[out.r])

    def memset(self, eng, out, val):
        self.op(eng, lambda e: e.memset(out.ap, val), writes=[out.r])

    def mm(self, out, lhsT, rhs, start, stop):
        self.op("pe", lambda e: e.matmul(out.ap, lhsT=lhsT.ap, rhs=rhs.ap, start=start, stop=stop),
                reads=_regs(lhsT, rhs), writes=[out.r], skip_self=True)

    def tr(self, out, a, ident):
        self.op("pe", lambda e: e.transpose(out.ap, a.ap, ident.ap), reads=_regs(a, ident), writes=[out.r],
                skip_self=True)

    def recip(self, out, a):
        self.op("dve", lambda e: e.reciprocal(out=out.ap, in_=a.ap), reads=[a.r], writes=[out.r])

    def finish(self):
        nc = self.nc
        with nc.Block() as block:
            @block.sync
            def _(e):
                for f in self.q["sp"]:
                    f(e)

            @block.scalar
            def _(e):
                for f in self.q["act"]:
                    f(e)

            @block.vector
            def _(e):
                for f in self.q["dve"]:
                    f(e)

            @block.gpsimd
            def _(e):
                for f in self.q["pool"]:
                    f(e)

            @block.tensor
            def _(e):
                for f in self.q["pe"]:
                    f(e)
        self.es.close()
        return nc


def run(nc, in_maps):
    return run_bass_kernel_spmd(nc, in_maps, core_ids=list(range(NCORES))).results


CAST_F = 8192


def build_cast(nt):
    P = Prog()
    src = P.dram("src", [nt, 128, CAST_F], F32)
    dst = P.dram("dst", [nt, 128, CAST_F], BF16, "ExternalOutput")
    ins = [Buf(P, "ci%d" % i, [128, CAST_F], F32) for i in range(2)]
    outs = [Buf(P, "co%d" % i, [128, CAST_F], BF16) for i in range(2)]
    h = CAST_F // 2
    for i in range(nt):
        a, o = ins[i % 2], outs[i % 2]
        P.dma("sp", a[:], src[i])
        P.cp("dve", o[:, :h], a[:, :h])
        P.cp("act", o[:, h:], a[:, h:])
        P.dma("sp", dst[i], o[:])
    P.wait_all("sp", [b.r for b in outs])
    return P.finish()


def cast_weights(arrs):
    names = list(arrs)
    flat = np.concatenate([np.ascontiguousarray(arrs[n]).reshape(-1) for n in names])
    per = NCORES * 128 * CAST_F
    nt = -(-flat.size // per)
    pad = np.zeros(nt * per, np.float32)
    pad[:flat.size] = flat
    src = pad.reshape(NCORES, nt, 128, CAST_F)
    res = run(build_cast(nt), [{"src": src[c]} for c in range(NCORES)])
    out = np.concatenate([np.asarray(r["dst"]).reshape(-1) for r in res])
    d = {}
    off = 0
    for n in names:
        sz = arrs[n].size
        d[n] = out[off:off + sz].reshape(arrs[n].shape)
        off += sz
    return d


def wblocks(w, nb):
    K, N = w.shape
    return np.ascontiguousarray(w.reshape(K // 128, 128, N // nb, nb).transpose(2, 1, 0, 3))


def pvec(v):
    return np.ascontiguousarray(v.reshape(-1, 128).T)


def ffn_in_blocks(w_in):
    g, u = w_in[:, :DFF], w_in[:, DFF:]
    K = w_in.shape[0]
    inter = np.stack([g.reshape(K, 22, 128), u.reshape(K, 22, 128)], axis=2).reshape(K, 2 * DFF)
    return wblocks(inter, 512)


class TP:
    def __init__(self, T):
        P = self.P = Prog()
        self.T = T
        self.xT = Buf(P, "xT", [128, 8, T], F32)
        self.xb = Buf(P, "xb", [128, 8, T], BF16)
        self.zT = Buf(P, "zT", [128, 8, T], F32)
        self.hT = Buf(P, "hT", [128, 22, T], BF16)
        self.wq = [Buf(P, "w%d" % i, [128, 5632], BF16) for i in range(3)]
        self.wi = 0
        self.pq = [Buf(P, "ps%d" % i, [128, 512], F32, psum=True) for i in range(7)]
        self.pi = 0
        self.ptr = Buf(P, "ptr", [128, 1024], BF16, psum=True)
        self.sc = [Buf(P, "sc%d" % i, [128, T], F32) for i in range(6)]
        self.lnm = Buf(P, "lnm", [128, T], F32)
        self.lnr = Buf(P, "lnr", [128, T], F32)
        self.si = 0
        self.sm = [Buf(P, "sm%d" % i, [128, 8], F32) for i in range(12)]
        self.smi = 0
        self.ones = Buf(P, "ones", [128, 128], F32)
        P.memset("pool", self.ones[:], 1.0)
        self.lng = self.const("lng", [128, 4, 8])
        self.lnb = self.const("lnb", [128, 4, 8])

    def const(self, name, shape, dt=F32):
        d = self.P.dram(name, shape, dt)
        b = Buf(self.P, "c_" + name, shape, dt)
        self.P.dma("sp", b[:], d)
        return b

    def psum(self):
        b = self.pq[self.pi % len(self.pq)]
        self.pi += 1
        return b

    def scr(self):
        b = self.sc[self.si % len(self.sc)]
        self.si += 1
        return b

    def small(self):
        b = self.sm[self.smi % len(self.sm)]
        self.smi += 1
        return b

    def wload(self, blk, Kc, NB):
        b = self.wq[self.wi % len(self.wq)]
        self.wi += 1
        v = b[:, :Kc * NB].re("p (k n) -> p k n", k=Kc)
        self.P.dma("sp", v, blk)
        return v

    def gemm_fm(self, wdram, nblk, Kc, NB, rhs_fn, T, epi):
        per = NB // 128
        for b in range(nblk):
            w = self.wload(wdram[b], Kc, NB)
            for c in range(per):
                ps = self.psum()
                for k in range(Kc):
                    self.P.mm(ps[:, :T], w[:, k, c * 128:(c + 1) * 128], rhs_fn(k), k == 0, k == Kc - 1)
                epi(b * per + c, ps)

    def res_epi(self, m, T):
        def epi(oc, ps):
            self.P.stt("dve", self.zT[:, oc, :T], self.xT[:, oc, :T], ALPHA / m, ps[:, :T], ALU.mult, ALU.add)
        return epi

    def gelu(self, out, x, T):
        P = self.P
        a = self.scr()
        b = self.scr()
        P.act(a[:, :T], x, AF.Square)
        P.ts("dve", a[:, :T], a[:, :T], 0.044715, ALU.mult, 1.0, ALU.add)
        P.tt("dve", b[:, :T], a[:, :T], x, ALU.mult)
        P.act(b[:, :T], b[:, :T], AF.Sigmoid, scale=1.5957691216057308)
        P.tt("dve", out, b[:, :T], x, ALU.mult)

    def layernorm(self, j, m, T):
        P = self.P
        ps_s = self.psum()
        ps_q = self.psum()
        for k in range(8):
            P.mm(ps_s[:, :T], self.ones[:, :], self.zT[:, k, :T], k == 0, k == 7)
        sqs = [self.scr(), self.scr()]
        for k in range(8):
            sq = sqs[k % 2]
            P.act(sq[:, :T], self.zT[:, k, :T], AF.Square)
            P.mm(ps_q[:, :T], self.ones[:, :], sq[:, :T], k == 0, k == 7)
        mean = self.lnm
        msq = self.scr()
        rstd = self.lnr
        P.ts("dve", mean[:, :T], ps_s[:, :T], 1.0 / D, ALU.mult)
        P.tt("pool", msq[:, :T], mean[:, :T], mean[:, :T], ALU.mult)
        P.stt("dve", rstd[:, :T], ps_q[:, :T], 1.0 / D, msq[:, :T], ALU.mult, ALU.subtract)
        P.ts("dve", rstd[:, :T], rstd[:, :T], LN_EPS / (m * m), ALU.add)
        P.act(rstd[:, :T], rstd[:, :T], AF.Sqrt)
        P.recip(rstd[:, :T], rstd[:, :T])
        for k in range(8):
            t1 = self.scr()
            e1, e2 = ("dve", "pool") if k % 2 == 0 else ("pool", "dve")
            P.tt(e1, t1[:, :T], self.zT[:, k, :T], mean[:, :T], ALU.subtract)
            P.tt(e2, t1[:, :T], t1[:, :T], rstd[:, :T], ALU.mult)
            P.act(self.xT[:, k, :T], t1[:, :T], AF.Identity, bias=self.lnb[:, j, k:k + 1], scale=self.lng[:, j, k:k + 1])
            P.cp("pool", self.xb[:, k, :T], self.xT[:, k, :T])

    def ffn(self, w_in, w_out, j, T):
        P = self.P
        for b in range(11):
            w = self.wload(w_in[b], 8, 512)
            for pr in range(2):
                jj = 2 * b + pr
                pg = self.psum()
                pu = self.psum()
                for k in range(8):
                    P.mm(pg[:, :T], w[:, k, pr * 256:pr * 256 + 128], self.xb[:, k, :T], k == 0, k == 7)
                for k in range(8):
                    P.mm(pu[:, :T], w[:, k, pr * 256 + 128:pr * 256 + 256], self.xb[:, k, :T], k == 0, k == 7)
                sg = self.scr()
                P.act(sg[:, :T], pg[:, :T], AF.Silu)
                P.tt("dve", self.hT[:, jj, :T], sg[:, :T], pu[:, :T], ALU.mult)
        self.gemm_fm(w_out, 4, 22, 256, lambda k: self.hT[:, k, :T], T, self.res_epi(0.5, T))
        self.layernorm(j, 0.5, T)

    def xattn_setup(self):
        P = self.P
        self.xa_wq = P.dram("xwq", [2, 128, 8, 512], BF16)
        self.xa_wkv = P.dram("xwkv", [4, 128, 8, 512], BF16)
        self.xa_wo = P.dram("xwo", [2, 128, 8, 512], BF16)
        memd = P.dram("memT", [128, 8, MEM], F32)
        memf = self.zT
        P.dma("sp", memf[:, :, :MEM], memd)
        self.ident = self.const("ident", [128, 128], BF16)
        memb = Buf(P, "memb", [128, 8, MEM], BF16)
        P.cp("dve", memb[:], memf[:, :, :MEM])
        self.kT = Buf(P, "kT", [128, 8, MEM], BF16)
        self.vtok = Buf(P, "vtok", [128, 2, D], BF16)
        self.qT = Buf(P, "qT", [128, 8, self.T], BF16)
        self.pTa = Buf(P, "pTa", [128, 2, self.T], BF16)
        self.eb = [Buf(P, "eb%d" % i, [128, MEM], F32) for i in range(2)]
        self.pb = [Buf(P, "pb%d" % i, [128, MEM], BF16) for i in range(2)]

        def kepi(oc, ps):
            P.cp("act", self.kT[:, oc, :], ps[:, :MEM])
        self.gemm_fm(self.xa_wkv, 2, 8, 512, lambda k: memb[:, k, :], MEM, kepi)
        for b in range(2):
            w = self.wload(self.xa_wkv[2 + b], 8, 512)
            for mc in range(2):
                ps = self.psum()
                for k in range(8):
                    P.mm(ps[:, :512], memb[:, k, mc * 128:(mc + 1) * 128], w[:, k, :], k == 0, k == 7)
                P.cp("act", self.vtok[:, mc, b * 512:(b + 1) * 512], ps[:, :512])

    def xattn(self, j, T):
        P = self.P

        def qepi(oc, ps):
            P.cp("act", self.qT[:, oc, :T], ps[:, :T])
        self.gemm_fm(self.xa_wq, 2, 8, 512, lambda k: self.xb[:, k, :T], T, qepi)
        ntc = T // 128
        n = 0
        for h in range(4):
            for tc in range(ntc):
                ps = self.psum()
                for dc in range(2):
                    P.mm(ps[:, :MEM], self.qT[:, 2 * h + dc, tc * 128:(tc + 1) * 128], self.kT[:, 2 * h + dc, :], dc == 0, dc == 1)
                mx = self.small()
                P.op("dve", lambda e, o=mx[:, 0:1].ap, i=ps[:, :MEM].ap: e.tensor_reduce(out=o, in_=i, axis=AX.X, op=ALU.max),
                     reads=[ps.r], writes=[mx.r])
                P.ts("dve", mx[:, 1:2], mx[:, 0:1], -1.0 / 16.0, ALU.mult)
                e_ = self.eb[n % 2]
                p_ = self.pb[n % 2]
                n += 1
                P.act(e_[:, :], ps[:, :MEM], AF.Exp, bias=mx[:, 1:2], scale=1.0 / 16.0, accum=mx[:, 2:3])
                P.recip(mx[:, 3:4], mx[:, 2:3])
                P.ts("dve", p_[:, :], e_[:, :], mx[:, 3:4], ALU.mult)
                for mc in range(2):
                    P.tr(self.ptr[:, mc * 128:(mc + 1) * 128], p_[:, mc * 128:(mc + 1) * 128], self.ident[:, :])
                P.cp("pool" if False else "act", self.pTa[:, :, tc * 128:(tc + 1) * 128],
                     self.ptr[:, 0:256].re("p (m t) -> p m t", m=2))
            for dc in range(2):
                po = self.psum()
                for mc in range(2):
                    P.mm(po[:, :T], self.vtok[:, mc, h * 256 + dc * 128:h * 256 + dc * 128 + 128], self.pTa[:, mc, :T], mc == 0, mc == 1)
                P.cp("act", self.hT[:, 2 * h + dc, :T], po[:, :T])
        self.gemm_fm(self.xa_wo, 2, 8, 512, lambda k: self.hT[:, k, :T], T, self.res_epi(1.0, T))
        self.layernorm(j, 1.0, T)

    def even_setup(self):
        P = self.P
        T = self.T
        self.e_wi = P.dram("ewi", [3, 128, 8, 512], BF16)
        self.e_wo = P.dram("ewo", [2, 128, 8, 512], BF16)
        self.poolw = self.const("poolw", [128, 4, 128], BF16)
        self.pscale = self.const("pscale", [128, 4])
        self.sgG = self.const("sgG", [128, 512])
        self.sgB = self.const("sgB", [128, 512])
        wsT = self.const("wsT", [128, 4, 128], BF16)
        mask = self.const("trimask", [128, 128], BF16)
        self.wsTm = Buf(P, "wsTm", [128, 4, 128], BF16)
        for h in range(4):
            P.tt("dve", self.wsTm[:, h, :], wsT[:, h, :], mask[:, :], ALU.mult)
        self.BS = self.const("sgBS", [128, 4, T])
        self.corr = self.const("corr", [128, 4, HALO])
        self.hmask = self.const("hmask", [128, 1])
        self.xa = [Buf(P, "xa%d" % i, [128, 4, HALO + T], F32) for i in range(2)]
        self.sA = [Buf(P, "sA%d" % i, [128, HALO + T], F32) for i in range(2)]
        self.pooled = Buf(P, "pooled", [128, 4, T], BF16)
        self.gu = Buf(P, "gu", [128, 4, T], F32)
        self.gv = [Buf(P, "gv%d" % i, [128, 512], F32) for i in range(2)]
        self.vnb = Buf(P, "vnb", [128, T // 128, 512], BF16)
        self.bst = Buf(P, "bst", [128, 8], F32)
        self.gtmp = [Buf(P, "gtmp%d" % i, [128, 512], F32) for i in range(2)]

    def even_halo(self):
        P = self.P
        w = self.wload(self.e_wi[0], 8, 512)
        for g in range(4):
            ps = self.psum()
            for k in range(8):
                P.mm(ps[:, :HALO], w[:, k, g * 128:(g + 1) * 128], self.xb[:, k, :HALO], k == 0, k == 7)
            P.ts("dve", self.xa[0][:, g, 0:HALO], ps[:, :HALO], self.hmask[:, 0:1], ALU.mult)

    def even_mixer(self, i, j):
        P = self.P
        T = self.T
        xa = self.xa[i % 2]
        if i > 0:
            P.cp("pool", xa[:, :, 0:HALO], self.xa[(i - 1) % 2][:, :, T:T + HALO])
        w = self.wload(self.e_wi[0], 8, 512)
        for g in range(4):
            ps = self.psum()
            for k in range(8):
                P.mm(ps[:, :T], w[:, k, g * 128:(g + 1) * 128], self.xb[:, k, :T], k == 0, k == 7)
            P.cp("act", xa[:, g, HALO:HALO + T], ps[:, :T])
        w = self.wload(self.e_wi[1], 8, 512)
        for c in range(4):
            ps = self.psum()
            for k in range(8):
                P.mm(ps[:, :T], w[:, k, c * 128:(c + 1) * 128], self.xb[:, k, :T], k == 0, k == 7)
            self.gelu(self.gu[:, c, :T], ps[:, :T], T)
        w = self.wload(self.e_wi[2], 8, 512)
        for tc in range(T // 128):
            ps = self.psum()
            for k in range(8):
                P.mm(ps[:, :512], self.xb[:, k, tc * 128:(tc + 1) * 128], w[:, k, :], k == 0, k == 7)
            gv = self.gv[tc % 2]
            self.gelu512(gv, ps)
            st = self.bst
            P.op("dve", lambda e, o=st[:, 0:6].ap, a=gv[:, :].ap: e.bn_stats(out=o, in_=a), reads=[gv.r], writes=[st.r])
            mv = self.small()
            P.op("dve", lambda e, o=mv[:, 0:2].ap, a=st[:, 0:6].ap: e.bn_aggr(out=o, in_=a), reads=[st.r], writes=[mv.r])
            P.ts("dve", mv[:, 2:3], mv[:, 1:2], LN_EPS, ALU.add)
            P.act(mv[:, 2:3], mv[:, 2:3], AF.Sqrt)
            P.recip(mv[:, 3:4], mv[:, 2:3])
            P.ts("dve", gv[:, :], gv[:, :], mv[:, 0:1], ALU.subtract, mv[:, 3:4], ALU.mult)
            P.tt("pool", gv[:, :], gv[:, :], self.sgG[:, :], ALU.mult)
            P.tt("dve", self.vnb[:, tc, :], gv[:, :], self.sgB[:, :], ALU.add)
        for h in range(4):
            ps = self.psum()
            for tc in range(T // 128):
                P.mm(ps[:, tc * 128:(tc + 1) * 128], self.vnb[:, tc, h * 128:(h + 1) * 128], self.wsTm[:, h, :], True, True)
            t = self.scr()
            P.tt("dve", t[:, :T], ps[:, :T], self.BS[:, h, :T], ALU.add)
            P.tt("pool", self.hT[:, 4 + h, :T], t[:, :T], self.gu[:, h, :T], ALU.mult)
        W = HALO + T
        for g, win in enumerate((2, 4, 8, 16)):
            cur = xa[:, g, :]
            src = cur
            lo, sh, step = 0, 1, 0
            while sh < win:
                dst = self.sA[step % 2][:, :]
                lo2 = lo + sh
                P.tt("pool" if step % 2 else "dve", dst[:, lo2:W], src[:, lo2:W], src[:, lo2 - sh:W - sh], ALU.add)
                src, lo, sh, step = dst, lo2, sh * 2, step + 1
            if i == 0:
                P.tt("dve", src[:, HALO:2 * HALO], src[:, HALO:2 * HALO], self.corr[:, g, :], ALU.mult)
            P.stt("dve", self.pooled[:, g, :T], src[:, HALO:W], 1.0 / win, cur[:, HALO:W], ALU.mult, ALU.subtract)
            ps = self.psum()
            P.mm(ps[:, :T], self.poolw[:, g, :], self.pooled[:, g, :T], True, True)
            P.act(self.hT[:, g, :T], ps[:, :T], AF.Identity, scale=self.pscale[:, g:g + 1])
        self.gemm_fm(self.e_wo, 2, 8, 512, lambda k: self.hT[:, k, :T], T, self.res_epi(1.0, T))
        self.layernorm(j, 1.0, T)

    def gelu512(self, out, ps):
        P = self.P
        a = self.gtmp[0]
        b = self.gtmp[1]
        x = ps[:, :512]
        P.act(a[:, :], x, AF.Square)
        P.ts("dve", a[:, :], a[:, :], 0.044715, ALU.mult, 1.0, ALU.add)
        P.tt("dve", b[:, :], a[:, :], x, ALU.mult)
        P.act(b[:, :], b[:, :], AF.Sigmoid, scale=1.5957691216057308)
        P.tt("dve", out[:, :], b[:, :], x, ALU.mult)


def load_x(tp, src, T):
    P = tp.P
    P.dma("sp", tp.xT[:, :, :T], src)
    P.cp("pool", tp.xb[:, 0:4, :T], tp.xT[:, 0:4, :T])
    P.cp("act", tp.xb[:, 4:8, :T], tp.xT[:, 4:8, :T])


def build_even(NT, T):
    tp = TP(T)
    P = tp.P
    xin = P.dram("xin", [NT, 128, 8, T], F32)
    xh = P.dram("xh", [128, 8, HALO], F32)
    xout = P.dram("xout", [NT, 128, 8, T], F32, "ExternalOutput")
    w1i = P.dram("w1i", [11, 128, 8, 512], BF16)
    w1o = P.dram("w1o", [4, 128, 22, 256], BF16)
    w2i = P.dram("w2i", [11, 128, 8, 512], BF16)
    w2o = P.dram("w2o", [4, 128, 22, 256], BF16)
    tp.xattn_setup()
    tp.even_setup()
    load_x(tp, xh, HALO)
    tp.ffn(w1i, w1o, 0, HALO)
    tp.even_halo()
    for i in range(NT):
        load_x(tp, xin[i], T)
        tp.ffn(w1i, w1o, 0, T)
        tp.even_mixer(i, 1)
        tp.xattn(2, T)
        tp.ffn(w2i, w2o, 3, T)
        P.dma("sp", xout[i], tp.xT[:, :, :T])
    P.wait_all("sp", [tp.xT.r])
    return P.finish()


def to_fm(xs, T):
    n = xs.shape[0] // T
    F = xs.shape[1]
    return np.ascontiguousarray(xs.reshape(n, T, F // 128, 128).transpose(0, 3, 2, 1))


def from_fm(t):
    n, _, Fc, T = t.shape
    return np.ascontiguousarray(np.asarray(t).transpose(0, 3, 2, 1).reshape(n * T, Fc * 128))


def halo_fm(x, b, t0):
    if t0 == 0:
        return np.zeros((128, 8, HALO), np.float32)
    h = x[b, t0 - HALO:t0]
    return np.ascontiguousarray(h.reshape(HALO, 8, 128).transpose(2, 1, 0))


def common_maps(inp, bf, l, x, T):
    B, S, _ = x.shape
    cps = NCORES // B
    tpc = S // cps
    maps = []
    for c in range(NCORES):
        b, t0 = c // cps, (c % cps) * tpc
        m = {
            "xin": to_fm(x[b, t0:t0 + tpc], T),
            "xh": halo_fm(x, b, t0),
            "lng": np.ascontiguousarray(inp["ln_g"][l].reshape(4, 8, 128).transpose(2, 0, 1)),
            "lnb": np.ascontiguousarray(inp["ln_b"][l].reshape(4, 8, 128).transpose(2, 0, 1)),
        }
        maps.append(m)
    return maps, cps, tpc


def xattn_maps(maps, inp, bf, l, cps):
    for c, m in enumerate(maps):
        b = c // cps
        m["xwq"] = wblocks(bf["xattn_w_q"][l], 512)
        m["xwkv"] = wblocks(bf["xattn_w_kv"][l], 512)
        m["xwo"] = wblocks(bf["xattn_w_o"][l], 512)
        m["memT"] = np.ascontiguousarray(inp["mem"][b].reshape(MEM, 8, 128).transpose(2, 1, 0))
        m["ident"] = np.eye(128, dtype=np.float32).astype(NPBF)


def gather_x(res, key, B, S, cps, tpc):
    out = np.empty((B, S, D), np.float32)
    for c in range(NCORES):
        b, t0 = c // cps, (c % cps) * tpc
        out[b, t0:t0 + tpc] = from_fm(res[c][key])
    return out


def run_even(inp, bf, l, x, T):
    e = l // 2
    maps, cps, tpc = common_maps(inp, bf, l, x, T)
    xattn_maps(maps, inp, bf, l, cps)
    tri = (np.arange(128)[:, None] <= np.arange(128)[None, :]).astype(np.float32).astype(NPBF)
    for c, m in enumerate(maps):
        start = (c % cps) == 0
        m["w1i"] = ffn_in_blocks(bf["ffn1_w_in"][l])
        m["w1o"] = wblocks(bf["ffn1_w_out"][l], 256)
        m["w2i"] = ffn_in_blocks(bf["ffn2_w_in"][l])
        m["w2o"] = wblocks(bf["ffn2_w_out"][l], 256)
        m["ewi"] = wblocks(bf["even_w_in"][e], 512)
        m["ewo"] = wblocks(bf["even_w_out"][e], 512)
        m["poolw"] = np.ascontiguousarray(bf["pool_w"][e].transpose(1, 0, 2))
        m["pscale"] = pvec(inp["pool_scale"][e])
        m["sgG"] = np.ascontiguousarray(np.broadcast_to(inp["sgu_ln_g"][e][None, :], (128, 512)))
        m["sgB"] = np.ascontiguousarray(np.broadcast_to(inp["sgu_ln_b"][e][None, :], (128, 512)))
        m["wsT"] = np.ascontiguousarray(bf["sgu_w"][e].transpose(2, 0, 1))
        m["trimask"] = tri
        m["sgBS"] = np.ascontiguousarray(np.tile(inp["sgu_b"][e][None, :, :], (128, 1, T // 128)))
        corr = np.ones((128, 4, HALO), np.float32)
        if start:
            pos = np.arange(1, HALO + 1, dtype=np.float32)
            for g, win in enumerate((2, 4, 8, 16)):
                corr[:, g, :] = win / np.minimum(pos, win)
        m["corr"] = corr
        m["hmask"] = np.full((128, 1), 0.0 if start else 1.0, np.float32)
    B, S, _ = x.shape
    res = run(build_even(tpc // T, T), maps)
    return gather_x(res, "xout", B, S, cps, tpc)


EXPM05 = float(np.exp(-0.5))


def build_odd_pre(NT, T):
    tp = TP(T)
    P = tp.P
    W = HALO + T
    xin = P.dram("xin", [NT, 128, 8, T], F32)
    xh = P.dram("xh", [128, 8, HALO], F32)
    x1out = P.dram("x1out", [NT, 128, 8, T], F32, "ExternalOutput")
    qout = P.dram("qout", [NT, 10, 128, 4, T], F32, "ExternalOutput")
    w1i = P.dram("w1i", [11, 128, 8, 512], BF16)
    w1o = P.dram("w1o", [4, 128, 22, 256], BF16)
    owi = P.dram("owi", [11, 128, 8, 256], BF16)
    mu = tp.const("mu", [128, 14])
    lora = tp.const("lora", [128, 512])
    gup = tp.const("gup", [128, 512])
    bd64 = tp.const("bd64", [128, 128])
    pv = tp.const("pv", [128, 12, 4])
    W0, A0, KK, KA, CB, BA, BX, LAM = range(8)
    cw = tp.const("convw", [128, 4, 4])
    wabd = tp.const("wabd", [128, 4, 128])
    wxbd = tp.const("wxbd", [128, 4, 128])
    hmask = tp.const("hmask", [128, 1])
    omka = Buf(P, "omka", [128, 4], F32)
    P.ts("dve", omka[:, :], pv[:, KA, :], -1.0, ALU.mult, 1.0, ALU.add)
    m8sp = Buf(P, "m8sp", [128, 4], F32)
    P.act(m8sp[:, :], pv[:, LAM, :], AF.Exp, scale=-1.0)
    P.ts("dve", m8sp[:, :], m8sp[:, :], 1.0, ALU.add)
    P.act(m8sp[:, :], m8sp[:, :], AF.Ln)
    P.ts("dve", m8sp[:, :], m8sp[:, :], -8.0, ALU.mult)
    hb = Buf(P, "hb", [128, 22, W], F32)
    hh = Buf(P, "hh", [128, 22, 3], F32)
    abuf = Buf(P, "abuf", [128, 4, T], F32)
    osts = [Buf(P, "ost%d" % i, [128, 4, T], F32) for i in range(3)]
    ocnt = [0]

    def ost():
        b = osts[ocnt[0] % 3]
        ocnt[0] += 1
        return b

    def hgemm(Tn):
        def epi(oc, ps):
            P.cp("act", hb[:, oc, HALO:HALO + Tn], ps[:, :Tn])
        tp.gemm_fm(owi, 11, 8, 256, lambda k: tp.xb[:, k, :Tn], Tn, epi)

    load_x(tp, xh, HALO)
    tp.ffn(w1i, w1o, 0, HALO)
    hgemm(HALO)
    P.ts("dve", hh[:, :, :], hb[:, :, 2 * HALO - 3:2 * HALO], hmask[:, 0:1], ALU.mult)

    for i in range(NT):
        load_x(tp, xin[i], T)
        tp.ffn(w1i, w1o, 0, T)
        P.dma("sp", x1out[i], tp.xT[:, :, :T])
        hgemm(T)
        P.cp("pool", hb[:, :, HALO - 3:HALO], hh[:, :, :])
        P.cp("pool", hh[:, :, :], hb[:, :, W - 3:W])
        for c in range(14):
            d = tp.scr()
            e1 = "dve" if c % 2 == 0 else "pool"
            P.tt(e1, d[:, :T], hb[:, c, HALO - 1:W - 1], hb[:, c, HALO:W], ALU.subtract)
            P.stt("dve", hb[:, c, HALO:W], d[:, :T], mu[:, c:c + 1], hb[:, c, HALO:W], ALU.mult, ALU.add)
        o_r, o_v = ost(), ost()
        P.cp("pool", o_r[:, :, :T], hb[:, 0:4, HALO:W])
        P.dma("sp", qout[i, 0], o_r[:, :, :T])
        P.cp("pool", o_v[:, :, :T], hb[:, 8:12, HALO:W])
        P.dma("sp", qout[i, 2], o_v[:, :, :T])
        tw = tp.scr()
        P.act(tw[0:64, :T], hb[0:64, 12, HALO:W], AF.Tanh)
        sgd = tp.scr()
        P.act(sgd[:, :T], hb[:, 13, HALO:W], AF.Sigmoid)
        o_ld = ost()
        o_g = ost()
        for c in range(4):
            ps = tp.psum()
            P.mm(ps[:, :T], lora[0:64, c * 128:(c + 1) * 128], tw[0:64, :T], True, True)
            P.act(o_ld[:, c, :T], ps[:, :T], AF.Sigmoid, bias=pv[:, W0, c:c + 1])
            P.ts("pool", o_ld[:, c, :T], o_ld[:, c, :T], -EXPM05, ALU.mult)
            ps = tp.psum()
            P.mm(ps[:, :T], lora[64:128, c * 128:(c + 1) * 128], hb[64:128, 12, HALO:W], True, True)
            P.act(abuf[:, c, :T], ps[:, :T], AF.Sigmoid, bias=pv[:, A0, c:c + 1])
            ps = tp.psum()
            P.mm(ps[:, :T], gup[:, c * 128:(c + 1) * 128], sgd[:, :T], True, True)
            P.cp("act", o_g[:, c, :T], ps[:, :T])
        P.dma("sp", qout[i, 3], o_ld[:, :, :T])
        P.dma("sp", qout[i, 8], o_g[:, :, :T])
        o_al, o_be = ost(), ost()
        for c in range(4):
            k = hb[:, 4 + c, HALO:W]
            kr = tp.scr()
            sq = tp.scr()
            P.ts("dve", kr[:, :T], k, pv[:, KK, c:c + 1], ALU.mult)
            P.act(sq[:, :T], kr[:, :T], AF.Square)
            ps = tp.psum()
            P.mm(ps[:, :T], bd64[:, :], sq[:, :T], True, True)
            P.ts("dve", sq[:, :T], ps[:, :T], 1e-24, ALU.max)
            P.act(sq[:, :T], sq[:, :T], AF.Sqrt)
            P.recip(sq[:, :T], sq[:, :T])
            P.tt("pool", kr[:, :T], kr[:, :T], sq[:, :T], ALU.mult)
            P.ts("pool", o_al[:, c, :T], kr[:, :T], -1.0, ALU.mult)
            P.tt("dve", o_be[:, c, :T], kr[:, :T], abuf[:, c, :T], ALU.mult)
        P.dma("sp", qout[i, 4], o_al[:, :, :T])
        P.dma("sp", qout[i, 5], o_be[:, :, :T])
        o_k = ost()
        for c in range(4):
            t = tp.scr()
            P.ts("dve", t[:, :T], abuf[:, c, :T], pv[:, KA, c:c + 1], ALU.mult, omka[:, c:c + 1], ALU.add)
            P.tt("pool", o_k[:, c, :T], t[:, :T], hb[:, 4 + c, HALO:W], ALU.mult)
        P.dma("sp", qout[i, 1], o_k[:, :, :T])
        o_a, o_bx, o_gg = ost(), ost(), ost()
        for c in range(4):
            xc = tp.scr()
            P.ts("dve", xc[:, :T], hb[:, 18 + c, HALO - 3:W - 3], cw[:, 0, c:c + 1], ALU.mult)
            for tap in range(1, 4):
                P.stt("dve", xc[:, :T], hb[:, 18 + c, HALO - 3 + tap:W - 3 + tap], cw[:, tap, c:c + 1], xc[:, :T], ALU.mult, ALU.add)
            P.ts("pool", xc[:, :T], xc[:, :T], pv[:, CB, c:c + 1], ALU.add)
            ps = tp.psum()
            P.mm(ps[:, :T], wabd[:, c, :], xc[:, :T], True, True)
            la = tp.scr()
            P.act(la[:, :T], ps[:, :T], AF.Sigmoid, bias=pv[:, BA, c:c + 1])
            P.ts("pool", la[:, :T], la[:, :T], m8sp[:, c:c + 1], ALU.mult)
            ps = tp.psum()
            P.mm(ps[:, :T], wxbd[:, c, :], xc[:, :T], True, True)
            ix = tp.scr()
            P.act(ix[:, :T], ps[:, :T], AF.Sigmoid, bias=pv[:, BX, c:c + 1])
            P.tt("pool", ix[:, :T], ix[:, :T], xc[:, :T], ALU.mult)
            P.act(o_a[:, c, :T], la[:, :T], AF.Exp)
            th = tp.scr()
            P.act(th[:, :T], la[:, :T], AF.Tanh)
            a2 = xc
            P.tt("dve", a2[:, :T], o_a[:, c, :T], o_a[:, c, :T], ALU.mult)
            P.stt("dve", th[:, :T], a2[:, :T], 1.0, th[:, :T], ALU.add, ALU.mult)
            P.ts("dve", th[:, :T], th[:, :T], -1.0, ALU.mult, 0.0, ALU.max)
            P.act(th[:, :T], th[:, :T], AF.Sqrt)
            P.tt("dve", o_bx[:, c, :T], th[:, :T], ix[:, :T], ALU.mult)
            tp.gelu(o_gg[:, c, :T], hb[:, 14 + c, HALO:W], T)
        P.dma("sp", qout[i, 6], o_a[:, :, :T])
        P.dma("sp", qout[i, 7], o_bx[:, :, :T])
        P.dma("sp", qout[i, 9], o_gg[:, :, :T])
    P.wait_all("sp", [b.r for b in osts] + [tp.xT.r])
    return P.finish()


_REC_STOP = 99
_REC_VAR = 0


def build_rec(S, SEG=512):
    P = Prog()
    NCH = S // 64
    QS = SEG // 64
    NSEG = S // SEG
    fm = P.dram("fm", [4, 128, S], F32)
    ld = P.dram("ld", [64, NCH, 128], F32)
    bk = P.dram("bk", [128, NCH, 128], F32)
    vv = P.dram("vv", [64, NCH, 128], F32)
    la = P.dram("la", [128, S], F32)
    lb = P.dram("lb", [128, S], F32)
    yout = P.dram("yout", [64, NCH, 128], F32, "ExternalOutput")
    hout = P.dram("hout", [128, S], F32, "ExternalOutput")

    def const(name, shape):
        d = P.dram(name, shape, F32)
        b = Buf(P, "c_" + name, shape, F32)
        P.dma("sp", b[:], d)
        return b
    tri2 = const("tri2", [64, 128])
    trigt2 = const("trigt2", [64, 128])
    mask_sc = const("mask_sc", [128, 256])
    mask_l = const("mask_l", [64, 128])
    ident2 = const("ident2", [64, 128])

    LSEG = min(S, 2048)
    abuf = [Buf(P, "la%d" % i, [128, LSEG], F32) for i in range(2)]
    bbuf = [Buf(P, "lb%d" % i, [128, LSEG], F32) for i in range(2)]
    hbuf = [Buf(P, "lh%d" % i, [128, LSEG], F32) for i in range(2)]
    for s in range(S // LSEG):
        a_, b_, h_ = abuf[s % 2], bbuf[s % 2], hbuf[s % 2]
        P.dma("sp", a_[:], la[:, s * LSEG:(s + 1) * LSEG])
        P.dma("sp", b_[:], lb[:, s * LSEG:(s + 1) * LSEG])
        if s == 0:
            P.op("dve", lambda e, o=h_[:].ap, a=a_[:].ap, b=b_[:].ap: e.tensor_tensor_scan(
                out=o, data0=a, data1=b, initial=0.0, op0=ALU.mult, op1=ALU.add), reads=[a_.r, b_.r], writes=[h_.r])
        else:
            pr = hbuf[(s - 1) % 2]
            P.op("dve", lambda e, o=h_[:].ap, a=a_[:].ap, b=b_[:].ap, i=pr[:, LSEG - 1:LSEG].ap: e.tensor_tensor_scan(
                out=o, data0=a, data1=b, initial=i, op0=ALU.mult, op1=ALU.add), reads=[a_.r, b_.r, pr.r], writes=[h_.r])
        P.dma("sp", hout[:, s * LSEG:(s + 1) * LSEG], h_[:])

    pq = [Buf(P, "ps%d" % i, [128, 512], F32, psum=True) for i in range(8)]
    pi = [0]

    def psum():
        b = pq[pi[0] % 8]
        pi[0] += 1
        return b
    fmb = [Buf(P, "fmb%d" % i, [128, 4, SEG], F32) for i in range(2)]
    ldb = [Buf(P, "ldb%d" % i, [64, QS, 128], F32) for i in range(2)]
    bkb = [Buf(P, "bkb%d" % i, [128, QS, 128], F32) for i in range(2)]
    uvb = [Buf(P, "uvb%d" % i, [128, QS, 128], F32) for i in range(2)]
    yb = [Buf(P, "yb%d" % i, [64, QS, 128], F32) for i in range(2)]
    Hsb = Buf(P, "Hs", [128, 128], F32)
    P.memset("pool", Hsb[:], 0.0)
    Hs = Hsb[:, 0:64]

    def mk(name, shape, n=2):
        return [Buf(P, "%s%d" % (name, i), shape, F32) for i in range(n)]
    e1s, e2s, e3s = mk("e1", [128, 128]), mk("e2", [128, 64]), mk("e3", [128, 128])
    ARs, BKts, BKPs = mk("AR", [128, 128]), mk("BKt", [128, 128]), mk("BKP", [128, 128])
    SCs, L0s = mk("SC", [128, 256]), mk("L0", [64, 128])
    NTs, Lbs = mk("NT", [64, 256], 4), mk("Lb", [64, 128], 4)
    TTfs, XVs, X0s, O1s = mk("TTf", [64, 128]), mk("XV", [64, 128]), mk("X0", [64, 128]), mk("O1", [64, 128])

    def h2(v, w):
        return v.re("p (h c) -> p h c", h=2)

    for n in range(NCH):
        sg, q = divmod(n, QS)
        sb_ = sg % 2
        if q == 0:
            t0 = sg * SEG
            P.dma("sp", fmb[sb_][:], fm[:, :, t0:t0 + SEG].rearrange("f p t -> p f t"))
            P.dma("sp", ldb[sb_][:], ld[:, sg * QS:(sg + 1) * QS, :])
            P.dma("sp", bkb[sb_][:], bk[:, sg * QS:(sg + 1) * QS, :])
            P.dma("sp", uvb[sb_][64:128, :, :], vv[:, sg * QS:(sg + 1) * QS, :])
        if _REC_STOP <= 0:
            continue
        F = fmb[sb_]
        tsl = slice(q * 64, (q + 1) * 64)
        LD2 = ldb[sb_][:, q, :]
        UV = uvb[sb_]
        pb = n % 2
        e1, e2, e3, AR, BKt, BKP, SC, L0 = e1s[pb], e2s[pb], e3s[pb], ARs[pb], BKts[pb], BKPs[pb], SCs[pb], L0s[pb]
        TTf, XV, X0, O1 = TTfs[pb], XVs[pb], X0s[pb], O1s[pb]
        ps1 = psum()
        P.mm(ps1[:, 0:128], LD2, tri2[:, :], True, True)
        ps2 = psum()
        P.mm(ps2[:, 0:128], trigt2[:, :], LD2, True, True)
        P.act(e1[:, :], ps1[:, 0:128], AF.Exp)
        P.act(e2[:, :], ps1[:, 0:64], AF.Exp, scale=-1.0)
        P.act(e3[:, :], ps2[:, 0:128], AF.Exp)
        P.tt("dve", AR[:, 0:64], F[:, 1, tsl], e1[:, 64:128], ALU.mult)
        P.tt("pool", AR[:, 64:128], F[:, 0, tsl], e1[:, 0:64], ALU.mult)
        P.tt("dve", BKt[:, 0:64], F[:, 2, tsl], e2[:, :], ALU.mult)
        P.tt("pool", BKt[:, 64:128], F[:, 3, tsl], e2[:, :], ALU.mult)
        P.tt("pool", BKP[:, :], bkb[sb_][:, q, :], e3[:, :], ALU.mult)
        if _REC_STOP <= 1:
            continue
        for h in range(2):
            hs = slice(h * 64, (h + 1) * 64)
            psS = psum()
            psL = psum()
            P.mm(psS[:, 0:128], BKt[hs, :], AR[hs, :], True, True)
            P.mm(psL[0:64, 0:64], AR[hs, 0:64], BKt[hs, 0:64], True, True)
            P.tt("dve", SC[:, h * 128:(h + 1) * 128], psS[:, 0:128], mask_sc[:, 0:128], ALU.mult)
            P.tt("dve", L0[:, hs], psL[0:64, 0:64], mask_l[:, 0:64], ALU.mult)
        if _REC_STOP <= 2:
            continue
        SCh = h2(SC[0:64, :], 128)
        NT1, L1 = NTs[0], Lbs[0]
        psA = psum()
        psB = psum()
        for h in range(2):
            hs = slice(h * 64, (h + 1) * 64)
            P.mm(psA[0:64, hs], L0[:, hs], SC[0:64, h * 128:h * 128 + 64], True, True)
            P.mm(psB[0:64, hs], SC[0:64, h * 128:h * 128 + 64], L0[:, hs], True, True)
        P.cp("act", h2(NT1[:, :], 128)[:, :, 0:64], h2(psA[0:64, 0:128], 64))
        P.tt("dve", h2(NT1[:, :], 128)[:, :, 64:128], h2(ident2[:, :], 64), SCh[:, :, 0:64], ALU.add)
        P.cp("act", L1[:, :], psB[0:64, 0:128])
        for k in range(1, 5):
            cur, nxt, Lc, Ln = NTs[(k - 1) % 4], NTs[k % 4], Lbs[(k - 1) % 4], Lbs[k % 4]
            psA = psum()
            psB = psum()
            for h in range(2):
                hs = slice(h * 64, (h + 1) * 64)
                P.mm(psA[0:64, h * 128:(h + 1) * 128], Lc[:, hs], cur[:, h * 128:(h + 1) * 128], True, True)
                P.mm(psB[0:64, hs], cur[:, h * 128:h * 128 + 64], Lc[:, hs], True, True)
            P.cp("act", h2(nxt[:, :], 128)[:, :, 0:64], h2(psA[0:64, 0:256], 128)[:, :, 0:64])
            P.tt("dve", h2(nxt[:, :], 128)[:, :, 64:128], h2(cur[:, :], 128)[:, :, 64:128],
                 h2(psA[0:64, 0:256], 128)[:, :, 64:128], ALU.add)
            P.cp("act", Ln[:, :], psB[0:64, 0:128])
        cur, Lc = NTs[0], Lbs[0]
        psA = psum()
        for h in range(2):
            hs = slice(h * 64, (h + 1) * 64)
            P.mm(psA[0:64, hs], Lc[:, hs], cur[:, h * 128 + 64:(h + 1) * 128], True, True)
        P.tt("dve", h2(TTf[:, :], 64), h2(cur[:, :], 128)[:, :, 64:128], h2(psA[0:64, 0:128], 64), ALU.add)
        if _REC_STOP <= 3:
            continue
        psX = psum()
        for h in range(2):
            hs = slice(h * 64, (h + 1) * 64)
            P.mm(psX[0:64, hs], SC[64:128, h * 128:h * 128 + 64], UV[64:128, q, hs], True, True)
        P.cp("act", XV[:, :], psX[0:64, 0:128])
        if _REC_STOP <= 4:
            continue
        for h in range(2):
            hs = slice(h * 64, (h + 1) * 64)
            psH = psum()
            psO1 = psum()
            P.mm(psH[0:64, 0:64], AR[hs, 0:64], Hs[hs, :], True, True)
            P.mm(psO1[0:64, 0:64], AR[hs, 64:128], Hs[hs, :], True, True)
            P.tt("dve", X0[:, hs], psH[0:64, 0:64], XV[:, hs], ALU.add)
            P.cp("act", O1[:, hs], psO1[0:64, 0:64])
        if _REC_STOP <= 5:
            continue
        psU = psum()
        for h in range(2):
            hs = slice(h * 64, (h + 1) * 64)
            P.mm(psU[0:64, hs], TTf[:, hs], X0[:, hs], True, True)
        P.cp("dve", UV[0:64, q, :], psU[0:64, 0:128])
        if _REC_STOP <= 6:
            continue
        psO2 = psum()
        psHn = psum()
        for h in range(2):
            hs = slice(h * 64, (h + 1) * 64)
            P.mm(psO2[0:64, hs], SC[:, h * 128 + 64:(h + 1) * 128], UV[:, q, hs], True, True)
            P.mm(psHn[hs, 0:64], BKP[:, hs], UV[:, q, hs], True, True)
        P.tt("dve", yb[sb_][:, q, :], psO2[0:64, 0:128], O1[:, :], ALU.add)
        P.stt("dve", Hs[:, :], Hs[:, :], e1[:, 63:64], psHn[:, 0:64], ALU.mult, ALU.add)
        if q == QS - 1:
            P.dma("sp", yout[:, sg * QS:(sg + 1) * QS, :], yb[sb_][:])
    P.wait_all("sp", [b.r for b in yb] + [b.r for b in hbuf])
    return P.finish()


def build_odd_post(NT, T):
    tp = TP(T)
    P = tp.P
    x1in = P.dram("xin", [NT, 128, 8, T], F32)
    qin = P.dram("qin", [NT, 4, 128, 7, T], F32)
    xout = P.dram("xout", [NT, 128, 8, T], F32, "ExternalOutput")
    owo = P.dram("owo", [2, 128, 8, 512], BF16)
    w2i = P.dram("w2i", [11, 128, 8, 512], BF16)
    w2o = P.dram("w2o", [4, 128, 22, 256], BF16)
    bd64 = tp.const("bd64", [128, 128])
    pvp = tp.const("pvp", [128, 3, 4])
    tp.xattn_setup()
    qbs = [Buf(P, "qb%d" % i, [128, 7, T], F32) for i in range(2)]
    for i in range(NT):
        P.dma("sp", tp.xT[:, :, :T], x1in[i])
        for c in range(4):
            qb = qbs[c % 2]
            P.dma("sp", qb[:, :, :T], qin[i, c])
            y, lh, g, gg, r, kf, v = [qb[:, z, :T] for z in range(7)]
            ps_m = tp.psum()
            P.mm(ps_m[:, :T], bd64[:, :], y, True, True)
            ysq = tp.scr()
            P.act(ysq[:, :T], y, AF.Square)
            ps_q = tp.psum()
            P.mm(ps_q[:, :T], bd64[:, :], ysq[:, :T], True, True)
            mean = tp.lnm
            rstd = tp.lnr
            msq = tp.scr()
            P.ts("dve", mean[:, :T], ps_m[:, :T], 1.0 / 64, ALU.mult)
            P.tt("pool", msq[:, :T], mean[:, :T], mean[:, :T], ALU.mult)
            P.stt("dve", rstd[:, :T], ps_q[:, :T], 1.0 / 64, msq[:, :T], ALU.mult, ALU.subtract)
            P.ts("dve", rstd[:, :T], rstd[:, :T], GN_EPS, ALU.add)
            P.act(rstd[:, :T], rstd[:, :T], AF.Sqrt)
            P.recip(rstd[:, :T], rstd[:, :T])
            yn = tp.scr()
            P.tt("dve", yn[:, :T], y, mean[:, :T], ALU.subtract)
            P.tt("pool", yn[:, :T], yn[:, :T], rstd[:, :T], ALU.mult)
            P.act(yn[:, :T], yn[:, :T], AF.Identity, bias=pvp[:, 2, c:c + 1], scale=pvp[:, 1, c:c + 1])
            pr = tp.scr()
            P.stt("dve", pr[:, :T], r, pvp[:, 0, c:c + 1], kf, ALU.mult, ALU.mult)
            ps_b = tp.psum()
            P.mm(ps_b[:, :T], bd64[:, :], pr[:, :T], True, True)
            P.tt("dve", pr[:, :T], ps_b[:, :T], v, ALU.mult)
            P.tt("pool", yn[:, :T], yn[:, :T], pr[:, :T], ALU.add)
            P.tt("dve", tp.hT[:, c, :T], yn[:, :T], g, ALU.mult)
            P.tt("pool", tp.hT[:, 4 + c, :T], lh, gg, ALU.mult)
        tp.gemm_fm(owo, 2, 8, 512, lambda k: tp.hT[:, k, :T], T, tp.res_epi(1.0, T))
        tp.layernorm(1, 1.0, T)
        tp.xattn(2, T)
        tp.ffn(w2i, w2o, 3, T)
        P.dma("sp", xout[i], tp.xT[:, :, :T])
    P.wait_all("sp", [tp.xT.r])
    return P.finish()


def blockdiag(w):
    out = np.zeros((128, 4, 128), np.float32)
    for h in range(8):
        c, q = divmod(h, 2)
        out[q * 64:(q + 1) * 64, c, q * 64:(q + 1) * 64] = w[h]
    return out


def tok_major(a):
    S = a.shape[1]
    return np.ascontiguousarray(a.T.reshape(S // 64, 64, 128).transpose(1, 0, 2))


def run_odd(inp, bf, l, x, T):
    o = l // 2
    B, S, _ = x.shape
    maps, cps, tpc = common_maps(inp, bf, l, x, T)
    NT = tpc // T
    bd64 = np.kron(np.eye(2, dtype=np.float32), np.ones((64, 64), np.float32))
    pv = np.zeros((128, 12, 4), np.float32)
    for idx, name in enumerate(["rwkv_w0", "rwkv_a0", "rwkv_k_k", "rwkv_k_a", "lru_conv_b", "lru_b_a", "lru_b_x", "lru_lambda"]):
        pv[:, idx, :] = pvec(inp[name][o])
    for c, m in enumerate(maps):
        start = (c % cps) == 0
        m["w1i"] = ffn_in_blocks(bf["ffn1_w_in"][l])
        m["w1o"] = wblocks(bf["ffn1_w_out"][l], 256)
        m["owi"] = wblocks(bf["odd_w_in"][o], 256)
        m["mu"] = pvec(inp["rwkv_mu"][o])
        m["lora"] = np.ascontiguousarray(np.concatenate([inp["rwkv_w_up"][o], inp["rwkv_a_up"][o]], axis=0))
        m["gup"] = np.ascontiguousarray(inp["rwkv_g_up"][o])
        m["bd64"] = bd64
        m["pv"] = pv
        m["convw"] = np.ascontiguousarray(inp["lru_conv_w"][o].reshape(4, 4, 128).transpose(2, 0, 1))
        m["wabd"] = blockdiag(inp["lru_w_a"][o])
        m["wxbd"] = blockdiag(inp["lru_w_x"][o])
        m["hmask"] = np.full((128, 1), 0.0 if start else 1.0, np.float32)
    res = run(build_odd_pre(NT, T), maps)
    x1 = gather_x(res, "x1out", B, S, cps, tpc)
    Q = np.empty((B, 10, 512, S), np.float32)
    for c in range(NCORES):
        b, t0 = c // cps, (c % cps) * tpc
        qo = np.asarray(res[c]["qout"])
        Q[b, :, :, t0:t0 + tpc] = qo.transpose(1, 3, 2, 0, 4).reshape(10, 512, tpc)
    ii = np.arange(64)
    tri_incl = (ii[:, None] <= ii[None, :]).astype(np.float32)
    tri_strict = (ii[:, None] < ii[None, :]).astype(np.float32)
    tri_gt = (ii[:, None] > ii[None, :]).astype(np.float32)
    consts = {
        "tri2": np.concatenate([tri_incl, tri_strict], 1),
        "trigt2": np.concatenate([tri_gt, tri_gt], 1),
        "mask_sc": np.tile(np.concatenate([tri_strict, tri_incl], 1), (2, 2)),
        "mask_l": np.tile(tri_gt, (1, 2)),
        "ident2": np.tile(np.eye(64, dtype=np.float32), (1, 2)),
    }
    maps2 = []
    nhp = NCORES // B
    for c in range(NCORES):
        b, hp = c // nhp, c % nhp
        rows = slice(hp * 128, (hp + 1) * 128)
        m = dict(consts)
        m["fm"] = np.ascontiguousarray(np.stack([Q[b, 0, rows], Q[b, 4, rows], Q[b, 5, rows], Q[b, 1, rows]]))
        m["ld"] = tok_major(Q[b, 3, rows])
        m["bk"] = np.ascontiguousarray(np.concatenate([tok_major(Q[b, 5, rows]), tok_major(Q[b, 1, rows])], 0))
        m["vv"] = tok_major(Q[b, 2, rows])
        m["la"] = np.ascontiguousarray(Q[b, 6, rows])
        m["lb"] = np.ascontiguousarray(Q[b, 7, rows])
        maps2.append(m)
    res2 = run(build_rec(S), maps2)
    Y = np.empty((B, 512, S), np.float32)
    Hl = np.empty((B, 512, S), np.float32)
    for c in range(NCORES):
        b, hp = c // nhp, c % nhp
        rows = slice(hp * 128, (hp + 1) * 128)
        yo = np.asarray(res2[c]["yout"])
        Y[b, rows] = yo.transpose(1, 0, 2).reshape(S, 128).T
        Hl[b, rows] = np.asarray(res2[c]["hout"])
    maps3, _, _ = common_maps(inp, bf, l, x1, T)
    xattn_maps(maps3, inp, bf, l, cps)
    pvp = np.stack([pvec(inp["rwkv_r_k"][o].reshape(-1)), pvec(inp["rwkv_gn_g"][o]), pvec(inp["rwkv_gn_b"][o])], 1)
    for c, m in enumerate(maps3):
        b, t0 = c // cps, (c % cps) * tpc
        sl = slice(t0, t0 + tpc)
        arrs = np.stack([Y[b][:, sl], Hl[b][:, sl], Q[b, 8][:, sl], Q[b, 9][:, sl], Q[b, 0][:, sl], Q[b, 1][:, sl], Q[b, 2][:, sl]])
        m["qin"] = np.ascontiguousarray(arrs.reshape(7, 4, 128, NT, T).transpose(3, 1, 2, 0, 4))
        del m["xh"]
        m["owo"] = wblocks(bf["odd_w_out"][o], 512)
        m["w2i"] = ffn_in_blocks(bf["ffn2_w_in"][l])
        m["w2o"] = wblocks(bf["ffn2_w_out"][l], 256)
        m["bd64"] = bd64
        m["pvp"] = np.ascontiguousarray(pvp)
    res3 = run(build_odd_post(NT, T), maps3)
    return gather_x(res3, "xout", B, S, cps, tpc), dict(x1=x1, Q=Q, Y=Y, Hl=Hl)


BIG = ["ffn1_w_in", "ffn1_w_out", "ffn2_w_in", "ffn2_w_out", "xattn_w_q", "xattn_w_kv", "xattn_w_o",
       "even_w_in", "even_w_out", "pool_w", "sgu_w", "odd_w_in", "odd_w_out"]


def kernel(**inputs):
    inp = {k: np.ascontiguousarray(np.asarray(v, dtype=np.float32)) for k, v in inputs.items()}
    bf = cast_weights({n: inp[n] for n in BIG})
    x = inp["x"]
    T = 512
    for l in range(DEPTH):
        if l % 2 == 0:
            x = run_even(inp, bf, l, x, T)
        else:
            x, _ = run_odd(inp, bf, l, x, T)
    return x.astype(np.float32)
```

```python
import contextlib
import numpy as np
import ml_dtypes
import concourse.bass as bass
import concourse.mybir as mybir
from concourse.bass_utils import run_bass_kernel_spmd

F32 = mybir.dt.float32
BF16 = mybir.dt.bfloat16
AF = mybir.ActivationFunctionType
ALU = mybir.AluOpType
AX = mybir.AxisListType
NPBF = ml_dtypes.bfloat16

NCORES = 8
D = 1024
DFF = 2816
DEPTH = 4
MEM = 256
ALPHA = (2 * DEPTH) ** 0.25
LN_EPS = 1e-5
GN_EPS = 64e-5
HALO = 16
ENGS = ("pe", "act", "dve", "pool", "sp")


class Reg:
    _n = 0

    def __init__(self, name=""):
        Reg._n += 1
        self.id = Reg._n
        self.name = name
        self.lw = None
        self.rd = {}
        self.dsem = None
        self.dcnt = 0


class V:
    def __init__(self, ap, r):
        self.ap = ap
        self.r = r

    def __getitem__(self, idx):
        return V(self.ap[idx], self.r)

    def re(self, pat, **kw):
        return V(self.ap.rearrange(pat, **kw), self.r)


class Buf:
    def __init__(self, P, name, shape, dt, psum=False, reg=None):
        self.t = (P.ps if psum else P.sb)(name, shape, dt)
        self.r = reg or Reg(name)

    def __getitem__(self, idx):
        return V(self.t[idx], self.r)


def _regs(*vs):
    return [v.r for v in vs if isinstance(v, V)]


def _a(v):
    return v.ap if isinstance(v, V) else v


class Prog:
    def __init__(self):
        self.nc = bass.Bass("TRN2", target_bir_lowering=False)
        self.es = contextlib.ExitStack()
        self.q = {e: [] for e in ENGS}
        self.seq = {e: 0 for e in ENGS}
        self.seen = {e: {} for e in ENGS}
        self.sem = {e: self.es.enter_context(self.nc.semaphore("s_" + e)) for e in ENGS}
        self.ninst = 0

    def sb(self, name, shape, dt):
        return self.es.enter_context(self.nc.sbuf_tensor(name, list(shape), dt))

    def ps(self, name, shape, dt=F32):
        return self.es.enter_context(self.nc.psum_tensor(name, list(shape), dt))

    def dram(self, name, shape, dt, kind="ExternalInput"):
        return self.nc.dram_tensor(name, list(shape), dt, kind=kind).ap()

    def _deps(self, reads, writes):
        toks = {}

        def add(t):
            if t is None:
                return
            k, v = t
            if toks.get(k, 0) < v:
                toks[k] = v
        for r in reads:
            add(r.lw)
        for w in writes:
            add(w.lw)
            for k, v in w.rd.items():
                add((k, v))
        return toks

    def _emit_waits(self, eng, toks, skip_self=False):
        seen = self.seen[eng]
        for k, v in toks.items():
            if skip_self and k == eng:
                continue
            if seen.get(k, 0) >= v:
                continue
            seen[k] = v
            semh = self.sem[k] if isinstance(k, str) else k[1]
            self.q[eng].append(lambda e, s=semh, v=v: e.wait_ge(s, v))

    def _commit(self, tok, reads, writes):
        k, v = tok
        for w in writes:
            w.lw = tok
            w.rd = {}
        for r in reads:
            if r in writes:
                continue
            if r.rd.get(k, 0) < v:
                r.rd[k] = v

    def op(self, eng, fn, reads=(), writes=(), skip_self=False):
        reads = list(reads)
        writes = list(writes)
        self._emit_waits(eng, self._deps(reads, writes), skip_self)
        self.seq[eng] += 1
        v = self.seq[eng]
        semh = self.sem[eng]
        self.q[eng].append(lambda e, fn=fn, s=semh: fn(e).then_inc(s, 1))
        self._commit((eng, v), reads, writes)
        self.ninst += 1

    def dma(self, eng, out, in_, **kw):
        reads = _regs(in_)
        writes = _regs(out)
        self._emit_waits(eng, self._deps(reads, writes))
        r = writes[0] if writes else reads[0]
        if r.dsem is None:
            r.dsem = self.es.enter_context(self.nc.semaphore("d%d" % r.id))
        r.dcnt += 16
        semh = r.dsem
        o, i = _a(out), _a(in_)
        self.q[eng].append(lambda e, o=o, i=i, s=semh, kw=kw: e.dma_start(out=o, in_=i, **kw).then_inc(s, 16))
        self._commit((("d", semh, r.id), r.dcnt), reads, writes)
        self.ninst += 1

    def wait_all(self, eng, regs):
        toks = {}
        for r in regs:
            if r.lw is not None:
                k, v = r.lw
                toks[k] = max(toks.get(k, 0), v)
            for k, v in r.rd.items():
                toks[k] = max(toks.get(k, 0), v)
        self._emit_waits(eng, toks)

    def tt(self, eng, out, a, b, op):
        self.op(eng, lambda e: e.tensor_tensor(out=out.ap, in0=a.ap, in1=b.ap, op=op),
                reads=_regs(a, b), writes=[out.r])

    def ts(self, eng, out, a, s1, op0, s2=None, op1=None, accum=None):
        kw = {}
        if op1 is not None:
            kw["op1"] = op1
        if accum is not None:
            kw["accum_out"] = accum.ap
        self.op(eng, lambda e: e.tensor_scalar(out=out.ap, in0=a.ap, scalar1=_a(s1), scalar2=_a(s2), op0=op0, **kw),
                reads=_regs(a, s1, s2), writes=[out.r] + _regs(accum))

    def stt(self, eng, out, a, s, b, op0, op1):
        self.op(eng, lambda e: e.scalar_tensor_tensor(out=out.ap, in0=a.ap, scalar=_a(s), in1=b.ap, op0=op0, op1=op1),
                reads=_regs(a, s, b), writes=[out.r])

    def act(self, out, a, func, bias=None, scale=None, accum=None):
        kw = {}
        if bias is not None:
            kw["bias"] = _a(bias)
        if scale is not None:
            kw["scale"] = _a(scale)
        if accum is not None:
            kw["accum_out"] = accum.ap
        self.op("act", lambda e: e.activation(out=out.ap, in_=a.ap, func=func, **kw),
                reads=_regs(a, bias, scale), writes=[out.r] + _regs(accum))

    def cp(self, eng, out, a):
        if eng == "act":
            self.op(eng, lambda e: e.copy(out=out.ap, in_=a.ap), reads=[a.r], writes=[out.r])
        else:
            self.op(eng, lambda e: e.tensor_copy(out=out.ap, in_=a.ap), reads=[a.r], writes=[out.r])

    def memset(self, eng, out, val):
        self.op(eng, lambda e: e.memset(out.ap, val), writes=[out.r])

    def mm(self, out, lhsT, rhs, start, stop):
        self.op("pe", lambda e: e.matmul(out.ap, lhsT=lhsT.ap, rhs=rhs.ap, start=start, stop=stop),
                reads=_regs(lhsT, rhs), writes=[out.r], skip_self=True)

    def tr(self, out, a, ident):
        self.op("pe", lambda e: e.transpose(out.ap, a.ap, ident.ap), reads=_regs(a, ident), writes=[out.r],
                skip_self=True)

    def recip(self, out, a):
        self.op("dve", lambda e: e.reciprocal(out=out.ap, in_=a.ap), reads=[a.r], writes=[out.r])

    def finish(self):
        nc = self.nc
        with nc.Block() as block:
            @block.sync
            def _(e):
                for f in self.q["sp"]:
                    f(e)

            @block.scalar
            def _(e):
                for f in self.q["act"]:
                    f(e)

            @block.vector
            def _(e):
                for f in self.q["dve"]:
                    f(e)

            @block.gpsimd
            def _(e):
                for f in self.q["pool"]:
                    f(e)

            @block.tensor
            def _(e):
                for f in self.q["pe"]:
                    f(e)
        self.es.close()
        return nc


def run(nc, in_maps):
    return run_bass_kernel_spmd(nc, in_maps, core_ids=list(range(NCORES))).results


CAST_F = 8192


def build_cast(nt):
    P = Prog()
    src = P.dram("src", [nt, 128, CAST_F], F32)
    dst = P.dram("dst", [nt, 128, CAST_F], BF16, "ExternalOutput")
    ins = [Buf(P, "ci%d" % i, [128, CAST_F], F32) for i in range(2)]
    outs = [Buf(P, "co%d" % i, [128, CAST_F], BF16) for i in range(2)]
    h = CAST_F // 2
    for i in range(nt):
        a, o = ins[i % 2], outs[i % 2]
        P.dma("sp", a[:], src[i])
        P.cp("dve", o[:, :h], a[:, :h])
        P.cp("act", o[:, h:], a[:, h:])
        P.dma("sp", dst[i], o[:])
    P.wait_all("sp", [b.r for b in outs])
    return P.finish()


def cast_weights(arrs):
    names = list(arrs)
    flat = np.concatenate([np.ascontiguousarray(arrs[n]).reshape(-1) for n in names])
    per = NCORES * 128 * CAST_F
    nt = -(-flat.size // per)
    pad = np.zeros(nt * per, np.float32)
    pad[:flat.size] = flat
    src = pad.reshape(NCORES, nt, 128, CAST_F)
    res = run(build_cast(nt), [{"src": src[c]} for c in range(NCORES)])
    out = np.concatenate([np.asarray(r["dst"]).reshape(-1) for r in res])
    d = {}
    off = 0
    for n in names:
        sz = arrs[n].size
        d[n] = out[off:off + sz].reshape(arrs[n].shape)
        off += sz
    return d


def wblocks(w, nb):
    K, N = w.shape
    return np.ascontiguousarray(w.reshape(K // 128, 128, N // nb, nb).transpose(2, 1, 0, 3))


def pvec(v):
    return np.ascontiguousarray(v.reshape(-1, 128).T)


def ffn_in_blocks(w_in):
    g, u = w_in[:, :DFF], w_in[:, DFF:]
    K = w_in.shape[0]
    inter = np.stack([g.reshape(K, 22, 128), u.reshape(K, 22, 128)], axis=2).reshape(K, 2 * DFF)
    return wblocks(inter, 512)


class TP:
    def __init__(self, T):
        P = self.P = Prog()
        self.T = T
        self.xT = Buf(P, "xT", [128, 8, T], F32)
        self.xb = Buf(P, "xb", [128, 8, T], BF16)
        self.zT = Buf(P, "zT", [128, 8, T], F32)
        self.hT = Buf(P, "hT", [128, 22, T], BF16)
        self.wq = [Buf(P, "w%d" % i, [128, 5632], BF16) for i in range(3)]
        self.wi = 0
        self.pq = [Buf(P, "ps%d" % i, [128, 512], F32, psum=True) for i in range(7)]
        self.pi = 0
        self.ptr = Buf(P, "ptr", [128, 1024], BF16, psum=True)
        self.sc = [Buf(P, "sc%d" % i, [128, T], F32) for i in range(6)]
        self.lnm = Buf(P, "lnm", [128, T], F32)
        self.lnr = Buf(P, "lnr", [128, T], F32)
        self.si = 0
        self.sm = [Buf(P, "sm%d" % i, [128, 8], F32) for i in range(12)]
        self.smi = 0
        self.ones = Buf(P, "ones", [128, 128], F32)
        P.memset("pool", self.ones[:], 1.0)
        self.lng = self.const("lng", [128, 4, 8])
        self.lnb = self.const("lnb", [128, 4, 8])

    def const(self, name, shape, dt=F32):
        d = self.P.dram(name, shape, dt)
        b = Buf(self.P, "c_" + name, shape, dt)
        self.P.dma("sp", b[:], d)
        return b

    def psum(self):
        b = self.pq[self.pi % len(self.pq)]
        self.pi += 1
        return b

    def scr(self):
        b = self.sc[self.si % len(self.sc)]
        self.si += 1
        return b

    def small(self):
        b = self.sm[self.smi % len(self.sm)]
        self.smi += 1
        return b

    def wload(self, blk, Kc, NB):
        b = self.wq[self.wi % len(self.wq)]
        self.wi += 1
        v = b[:, :Kc * NB].re("p (k n) -> p k n", k=Kc)
        self.P.dma("sp", v, blk)
        return v

    def gemm_fm(self, wdram, nblk, Kc, NB, rhs_fn, T, epi):
        per = NB // 128
        for b in range(nblk):
            w = self.wload(wdram[b], Kc, NB)
            for c in range(per):
                ps = self.psum()
                for k in range(Kc):
                    self.P.mm(ps[:, :T], w[:, k, c * 128:(c + 1) * 128], rhs_fn(k), k == 0, k == Kc - 1)
                epi(b * per + c, ps)

    def res_epi(self, m, T):
        def epi(oc, ps):
            self.P.stt("dve", self.zT[:, oc, :T], self.xT[:, oc, :T], ALPHA / m, ps[:, :T], ALU.mult, ALU.add)
        return epi

    def gelu(self, out, x, T):
        P = self.P
        a = self.scr()
        b = self.scr()
        P.act(a[:, :T], x, AF.Square)
        P.ts("dve", a[:, :T], a[:, :T], 0.044715, ALU.mult, 1.0, ALU.add)
        P.tt("dve", b[:, :T], a[:, :T], x, ALU.mult)
        P.act(b[:, :T], b[:, :T], AF.Sigmoid, scale=1.5957691216057308)
        P.tt("dve", out, b[:, :T], x, ALU.mult)

    def layernorm(self, j, m, T):
        P = self.P
        ps_s = self.psum()
        ps_q = self.psum()
        for k in range(8):
            P.mm(ps_s[:, :T], self.ones[:, :], self.zT[:, k, :T], k == 0, k == 7)
        sqs = [self.scr(), self.scr()]
        for k in range(8):
            sq = sqs[k % 2]
            P.act(sq[:, :T], self.zT[:, k, :T], AF.Square)
            P.mm(ps_q[:, :T], self.ones[:, :], sq[:, :T], k == 0, k == 7)
        mean = self.lnm
        msq = self.scr()
        rstd = self.lnr
        P.ts("dve", mean[:, :T], ps_s[:, :T], 1.0 / D, ALU.mult)
        P.tt("pool", msq[:, :T], mean[:, :T], mean[:, :T], ALU.mult)
        P.stt("dve", rstd[:, :T], ps_q[:, :T], 1.0 / D, msq[:, :T], ALU.mult, ALU.subtract)
        P.ts("dve", rstd[:, :T], rstd[:, :T], LN_EPS / (m * m), ALU.add)
        P.act(rstd[:, :T], rstd[:, :T], AF.Sqrt)
        P.recip(rstd[:, :T], rstd[:, :T])
        for k in range(8):
            t1 = self.scr()
            e1, e2 = ("dve", "pool") if k % 2 == 0 else ("pool", "dve")
            P.tt(e1, t1[:, :T], self.zT[:, k, :T], mean[:, :T], ALU.subtract)
            P.tt(e2, t1[:, :T], t1[:, :T], rstd[:, :T], ALU.mult)
            P.act(self.xT[:, k, :T], t1[:, :T], AF.Identity, bias=self.lnb[:, j, k:k + 1], scale=self.lng[:, j, k:k + 1])
            P.cp("pool", self.xb[:, k, :T], self.xT[:, k, :T])

    def ffn(self, w_in, w_out, j, T):
        P = self.P
        for b in range(11):
            w = self.wload(w_in[b], 8, 512)
            for pr in range(2):
                jj = 2 * b + pr
                pg = self.psum()
                pu = self.psum()
                for k in range(8):
                    P.mm(pg[:, :T], w[:, k, pr * 256:pr * 256 + 128], self.xb[:, k, :T], k == 0, k == 7)
                for k in range(8):
                    P.mm(pu[:, :T], w[:, k, pr * 256 + 128:pr * 256 + 256], self.xb[:, k, :T], k == 0, k == 7)
                sg = self.scr()
                P.act(sg[:, :T], pg[:, :T], AF.Silu)
                P.tt("dve", self.hT[:, jj, :T], sg[:, :T], pu[:, :T], ALU.mult)
        self.gemm_fm(w_out, 4, 22, 256, lambda k: self.hT[:, k, :T], T, self.res_epi(0.5, T))
        self.layernorm(j, 0.5, T)

    def xattn_setup(self):
        P = self.P
        self.xa_wq = P.dram("xwq", [2, 128, 8, 512], BF16)
        self.xa_wkv = P.dram("xwkv", [4, 128, 8, 512], BF16)
        self.xa_wo = P.dram("xwo", [2, 128, 8, 512], BF16)
        memd = P.dram("memT", [128, 8, MEM], F32)
        memf = self.zT
        P.dma("sp", memf[:, :, :MEM], memd)
        self.ident = self.const("ident", [128, 128], BF16)
        memb = Buf(P, "memb", [128, 8, MEM], BF16)
        P.cp("dve", memb[:], memf[:, :, :MEM])
        self.kT = Buf(P, "kT", [128, 8, MEM], BF16)
        self.vtok = Buf(P, "vtok", [128, 2, D], BF16)
        self.qT = Buf(P, "qT", [128, 8, self.T], BF16)
        self.pTa = Buf(P, "pTa", [128, 2, self.T], BF16)
        self.eb = [Buf(P, "eb%d" % i, [128, MEM], F32) for i in range(4)]
        self.pb = [Buf(P, "pb%d" % i, [128, MEM], BF16) for i in range(4)]

        def kepi(oc, ps):
            P.cp("act", self.kT[:, oc, :], ps[:, :MEM])
        self.gemm_fm(self.xa_wkv, 2, 8, 512, lambda k: memb[:, k, :], MEM, kepi)
        for b in range(2):
            w = self.wload(self.xa_wkv[2 + b], 8, 512)
            for mc in range(2):
                ps = self.psum()
                for k in range(8):
                    P.mm(ps[:, :512], memb[:, k, mc * 128:(mc + 1) * 128], w[:, k, :], k == 0, k == 7)
                P.cp("act", self.vtok[:, mc, b * 512:(b + 1) * 512], ps[:, :512])

    def xattn(self, j, T):
        P = self.P

        def qepi(oc, ps):
            P.cp("act", self.qT[:, oc, :T], ps[:, :T])
        self.gemm_fm(self.xa_wq, 2, 8, 512, lambda k: self.xb[:, k, :T], T, qepi)
        ntc = T // 128
        for h in range(4):
            pss = []
            for tc in range(ntc):
                ps = self.psum()
                for dc in range(2):
                    P.mm(ps[:, :MEM], self.qT[:, 2 * h + dc, tc * 128:(tc + 1) * 128], self.kT[:, 2 * h + dc, :], dc == 0, dc == 1)
                pss.append(ps)
            mxs = [self.small() for _ in range(ntc)]
            for tc in range(ntc):
                P.op("dve", lambda e, o=mxs[tc][:, 0:1].ap, i=pss[tc][:, :MEM].ap: e.tensor_reduce(out=o, in_=i, axis=AX.X, op=ALU.max),
                     reads=[pss[tc].r], writes=[mxs[tc].r])
            for tc in range(ntc):
                P.ts("dve", mxs[tc][:, 1:2], mxs[tc][:, 0:1], -1.0 / 16.0, ALU.mult)
            for tc in range(ntc):
                P.act(self.eb[tc][:, :], pss[tc][:, :MEM], AF.Exp, bias=mxs[tc][:, 1:2], scale=1.0 / 16.0, accum=mxs[tc][:, 2:3])
            for tc in range(ntc):
                P.recip(mxs[tc][:, 3:4], mxs[tc][:, 2:3])
            for tc in range(ntc):
                P.ts("dve", self.pb[tc][:, :], self.eb[tc][:, :], mxs[tc][:, 3:4], ALU.mult)
            for tc in range(ntc):
                for mc in range(2):
                    P.tr(self.ptr[:, tc * 256 + mc * 128:tc * 256 + (mc + 1) * 128], self.pb[tc][:, mc * 128:(mc + 1) * 128], self.ident[:, :])
            for tc in range(ntc):
                P.cp("act" if tc % 2 == 0 else "dve", self.pTa[:, :, tc * 128:(tc + 1) * 128],
                     self.ptr[:, tc * 256:(tc + 1) * 256].re("p (m t) -> p m t", m=2))
            for dc in range(2):
                po = self.psum()
                for mc in range(2):
                    P.mm(po[:, :T], self.vtok[:, mc, h * 256 + dc * 128:h * 256 + dc * 128 + 128], self.pTa[:, mc, :T], mc == 0, mc == 1)
                P.cp("act", self.hT[:, 2 * h + dc, :T], po[:, :T])
        self.gemm_fm(self.xa_wo, 2, 8, 512, lambda k: self.hT[:, k, :T], T, self.res_epi(1.0, T))
        self.layernorm(j, 1.0, T)

    def even_setup(self):
        P = self.P
        T = self.T
        self.e_wi = P.dram("ewi", [3, 128, 8, 512], BF16)
        self.e_wo = P.dram("ewo", [2, 128, 8, 512], BF16)
        self.poolw = self.const("poolw", [128, 4, 128], BF16)
        self.pscale = self.const("pscale", [128, 4])
        self.sgG = self.const("sgG", [128, 512])
        self.sgB = self.const("sgB", [128, 512])
        wsT = self.const("wsT", [128, 4, 128], BF16)
        mask = self.const("trimask", [128, 128], BF16)
        self.wsTm = Buf(P, "wsTm", [128, 4, 128], BF16)
        for h in range(4):
            P.tt("dve", self.wsTm[:, h, :], wsT[:, h, :], mask[:, :], ALU.mult)
        self.BS = self.const("sgBS", [128, 4, T])
        self.corr = self.const("corr", [128, 4, HALO])
        self.hmask = self.const("hmask", [128, 1])
        self.xa = [Buf(P, "xa%d" % i, [128, 4, HALO + T], F32) for i in range(2)]
        self.sA = [Buf(P, "sA%d" % i, [128, HALO + T], F32) for i in range(2)]
        self.pooled = Buf(P, "pooled", [128, 4, T], BF16)
        self.gu = Buf(P, "gu", [128, 4, T], F32)
        self.gv = [Buf(P, "gv%d" % i, [128, 512], F32) for i in range(2)]
        self.vnb = Buf(P, "vnb", [128, T // 128, 512], BF16)
        self.bst = Buf(P, "bst", [128, 8], F32)

    def even_halo(self):
        P = self.P
        w = self.wload(self.e_wi[0], 8, 512)
        for g in range(4):
            ps = self.psum()
            for k in range(8):
                P.mm(ps[:, :HALO], w[:, k, g * 128:(g + 1) * 128], self.xb[:, k, :HALO], k == 0, k == 7)
            P.ts("dve", self.xa[0][:, g, 0:HALO], ps[:, :HALO], self.hmask[:, 0:1], ALU.mult)

    def even_mixer(self, i, j):
        P = self.P
        T = self.T
        xa = self.xa[i % 2]
        if i > 0:
            P.cp("pool", xa[:, :, 0:HALO], self.xa[(i - 1) % 2][:, :, T:T + HALO])
        w = self.wload(self.e_wi[0], 8, 512)
        for g in range(4):
            ps = self.psum()
            for k in range(8):
                P.mm(ps[:, :T], w[:, k, g * 128:(g + 1) * 128], self.xb[:, k, :T], k == 0, k == 7)
            P.cp("act", xa[:, g, HALO:HALO + T], ps[:, :T])
        w = self.wload(self.e_wi[1], 8, 512)
        for c in range(4):
            ps = self.psum()
            for k in range(8):
                P.mm(ps[:, :T], w[:, k, c * 128:(c + 1) * 128], self.xb[:, k, :T], k == 0, k == 7)
            self.gelu(self.gu[:, c, :T], ps[:, :T], T)
        w = self.wload(self.e_wi[2], 8, 512)
        for tc in range(T // 128):
            ps = self.psum()
            for k in range(8):
                P.mm(ps[:, :512], self.xb[:, k, tc * 128:(tc + 1) * 128], w[:, k, :], k == 0, k == 7)
            gv = self.gv[tc % 2]
            self.gelu512(gv, ps)
            st = self.bst
            P.op("dve", lambda e, o=st[:, 0:6].ap, a=gv[:, :].ap: e.bn_stats(out=o, in_=a), reads=[gv.r], writes=[st.r])
            mv = self.small()
            P.op("dve", lambda e, o=mv[:, 0:2].ap, a=st[:, 0:6].ap: e.bn_aggr(out=o, in_=a), reads=[st.r], writes=[mv.r])
            P.ts("dve", mv[:, 2:3], mv[:, 1:2], LN_EPS, ALU.add)
            P.act(mv[:, 2:3], mv[:, 2:3], AF.Sqrt)
            P.recip(mv[:, 3:4], mv[:, 2:3])
            P.ts("dve", gv[:, :], gv[:, :], mv[:, 0:1], ALU.subtract, mv[:, 3:4], ALU.mult)
            P.tt("pool", gv[:, :], gv[:, :], self.sgG[:, :], ALU.mult)
            P.tt("dve", self.vnb[:, tc, :], gv[:, :], self.sgB[:, :], ALU.add)
        for h in range(4):
            ps = self.psum()
            for tc in range(T // 128):
                P.mm(ps[:, tc * 128:(tc + 1) * 128], self.vnb[:, tc, h * 128:(h + 1) * 128], self.wsTm[:, h, :], True, True)
            t = self.scr()
            P.tt("dve", t[:, :T], ps[:, :T], self.BS[:, h, :T], ALU.add)
            P.tt("pool", self.hT[:, 4 + h, :T], t[:, :T], self.gu[:, h, :T], ALU.mult)
        W = HALO + T
        for g, win in enumerate((2, 4, 8, 16)):
            cur = xa[:, g, :]
            src = cur
            lo, sh, step = 0, 1, 0
            while sh < win:
                dst = self.sA[step % 2][:, :]
                lo2 = lo + sh
                P.tt("pool" if step % 2 else "dve", dst[:, lo2:W], src[:, lo2:W], src[:, lo2 - sh:W - sh], ALU.add)
                src, lo, sh, step = dst, lo2, sh * 2, step + 1
            if i == 0:
                P.tt("dve", src[:, HALO:2 * HALO], src[:, HALO:2 * HALO], self.corr[:, g, :], ALU.mult)
            P.stt("dve", self.pooled[:, g, :T], src[:, HALO:W], 1.0 / win, cur[:, HALO:W], ALU.mult, ALU.subtract)
            ps = self.psum()
            P.mm(ps[:, :T], self.poolw[:, g, :], self.pooled[:, g, :T], True, True)
            P.act(self.hT[:, g, :T], ps[:, :T], AF.Identity, scale=self.pscale[:, g:g + 1])
        self.gemm_fm(self.e_wo, 2, 8, 512, lambda k: self.hT[:, k, :T], T, self.res_epi(1.0, T))
        self.layernorm(j, 1.0, T)

    def gelu512(self, out, ps):
        P = self.P
        a = self.scr()
        b = self.scr()
        x = ps[:, :512]
        P.act(a[:, :], x, AF.Square)
        P.ts("dve", a[:, :], a[:, :], 0.044715, ALU.mult, 1.0, ALU.add)
        P.tt("dve", b[:, :], a[:, :], x, ALU.mult)
        P.act(b[:, :], b[:, :], AF.Sigmoid, scale=1.5957691216057308)
        P.tt("dve", out[:, :], b[:, :], x, ALU.mult)


def load_x(tp, src, T):
    P = tp.P
    P.dma("sp", tp.xT[:, :, :T], src)
    P.cp("pool", tp.xb[:, 0:4, :T], tp.xT[:, 0:4, :T])
    P.cp("act", tp.xb[:, 4:8, :T], tp.xT[:, 4:8, :T])


def build_even(NT, T):
    tp = TP(T)
    P = tp.P
    xin = P.dram("xin", [NT, 128, 8, T], F32)
    xh = P.dram("xh", [128, 8, HALO], F32)
    xout = P.dram("xout", [NT, 128, 8, T], F32, "ExternalOutput")
    w1i = P.dram("w1i", [11, 128, 8, 512], BF16)
    w1o = P.dram("w1o", [4, 128, 22, 256], BF16)
    w2i = P.dram("w2i", [11, 128, 8, 512], BF16)
    w2o = P.dram("w2o", [4, 128, 22, 256], BF16)
    tp.xattn_setup()
    tp.even_setup()
    load_x(tp, xh, HALO)
    tp.ffn(w1i, w1o, 0, HALO)
    tp.even_halo()
    for i in range(NT):
        load_x(tp, xin[i], T)
        tp.ffn(w1i, w1o, 0, T)
        tp.even_mixer(i, 1)
        tp.xattn(2, T)
        tp.ffn(w2i, w2o, 3, T)
        P.dma("sp", xout[i], tp.xT[:, :, :T])
    P.wait_all("sp", [tp.xT.r])
    return P.finish()


def to_fm(xs, T):
    n = xs.shape[0] // T
    F = xs.shape[1]
    return np.ascontiguousarray(xs.reshape(n, T, F // 128, 128).transpose(0, 3, 2, 1))


def from_fm(t):
    n, _, Fc, T = t.shape
    return np.ascontiguousarray(np.asarray(t).transpose(0, 3, 2, 1).reshape(n * T, Fc * 128))


def halo_fm(x, b, t0):
    if t0 == 0:
        return np.zeros((128, 8, HALO), np.float32)
    h = x[b, t0 - HALO:t0]
    return np.ascontiguousarray(h.reshape(HALO, 8, 128).transpose(2, 1, 0))


def common_maps(inp, bf, l, x, T):
    B, S, _ = x.shape
    cps = NCORES // B
    tpc = S // cps
    maps = []
    for c in range(NCORES):
        b, t0 = c // cps, (c % cps) * tpc
        m = {
            "xin": to_fm(x[b, t0:t0 + tpc], T),
            "xh": halo_fm(x, b, t0),
            "lng": np.ascontiguousarray(inp["ln_g"][l].reshape(4, 8, 128).transpose(2, 0, 1)),
            "lnb": np.ascontiguousarray(inp["ln_b"][l].reshape(4, 8, 128).transpose(2, 0, 1)),
        }
        maps.append(m)
    return maps, cps, tpc


def xattn_maps(maps, inp, bf, l, cps):
    for c, m in enumerate(maps):
        b = c // cps
        m["xwq"] = wblocks(bf["xattn_w_q"][l], 512)
        m["xwkv"] = wblocks(bf["xattn_w_kv"][l], 512)
        m["xwo"] = wblocks(bf["xattn_w_o"][l], 512)
        m["memT"] = np.ascontiguousarray(inp["mem"][b].reshape(MEM, 8, 128).transpose(2, 1, 0))
        m["ident"] = np.eye(128, dtype=np.float32).astype(NPBF)


def gather_x(res, key, B, S, cps, tpc):
    out = np.empty((B, S, D), np.float32)
    for c in range(NCORES):
        b, t0 = c // cps, (c % cps) * tpc
        out[b, t0:t0 + tpc] = from_fm(res[c][key])
    return out


def run_even(inp, bf, l, x, T):
    e = l // 2
    maps, cps, tpc = common_maps(inp, bf, l, x, T)
    xattn_maps(maps, inp, bf, l, cps)
    tri = (np.arange(128)[:, None] <= np.arange(128)[None, :]).astype(np.float32).astype(NPBF)
    for c, m in enumerate(maps):
        start = (c % cps) == 0
        m["w1i"] = ffn_in_blocks(bf["ffn1_w_in"][l])
        m["w1o"] = wblocks(bf["ffn1_w_out"][l], 256)
        m["w2i"] = ffn_in_blocks(bf["ffn2_w_in"][l])
        m["w2o"] = wblocks(bf["ffn2_w_out"][l], 256)
        m["ewi"] = wblocks(bf["even_w_in"][e], 512)
        m["ewo"] = wblocks(bf["even_w_out"][e], 512)
        m["poolw"] = np.ascontiguousarray(bf["pool_w"][e].transpose(1, 0, 2))
        m["pscale"] = pvec(inp["pool_scale"][e])
        m["sgG"] = np.ascontiguousarray(np.broadcast_to(inp["sgu_ln_g"][e][None, :], (128, 512)))
        m["sgB"] = np.ascontiguousarray(np.broadcast_to(inp["sgu_ln_b"][e][None, :], (128, 512)))
        m["wsT"] = np.ascontiguousarray(bf["sgu_w"][e].transpose(2, 0, 1))
        m["trimask"] = tri
        m["sgBS"] = np.ascontiguousarray(np.tile(inp["sgu_b"][e][None, :, :], (128, 1, T // 128)))
        corr = np.ones((128, 4, HALO), np.float32)
        if start:
            pos = np.arange(1, HALO + 1, dtype=np.float32)
            for g, win in enumerate((2, 4, 8, 16)):
                corr[:, g, :] = win / np.minimum(pos, win)
        m["corr"] = corr
        m["hmask"] = np.full((128, 1), 0.0 if start else 1.0, np.float32)
    B, S, _ = x.shape
    res = run(build_even(tpc // T, T), maps)
    return gather_x(res, "xout", B, S, cps, tpc)


EXPM05 = float(np.exp(-0.5))


def build_odd_pre(NT, T):
    tp = TP(T)
    P = tp.P
    W = HALO + T
    xin = P.dram("xin", [NT, 128, 8, T], F32)
    xh = P.dram("xh", [128, 8, HALO], F32)
    x1out = P.dram("x1out", [NT, 128, 8, T], F32, "ExternalOutput")
    qout = P.dram("qout", [NT, 10, 128, 4, T], F32, "ExternalOutput")
    w1i = P.dram("w1i", [11, 128, 8, 512], BF16)
    w1o = P.dram("w1o", [4, 128, 22, 256], BF16)
    owi = P.dram("owi", [11, 128, 8, 256], BF16)
    mu = tp.const("mu", [128, 14])
    lora = tp.const("lora", [128, 512])
    gup = tp.const("gup", [128, 512])
    bd64 = tp.const("bd64", [128, 128])
    pv = tp.const("pv", [128, 12, 4])
    W0, A0, KK, KA, CB, BA, BX, LAM = range(8)
    cw = tp.const("convw", [128, 4, 4])
    wabd = tp.const("wabd", [128, 4, 128])
    wxbd = tp.const("wxbd", [128, 4, 128])
    hmask = tp.const("hmask", [128, 1])
    omka = Buf(P, "omka", [128, 4], F32)
    P.ts("dve", omka[:, :], pv[:, KA, :], -1.0, ALU.mult, 1.0, ALU.add)
    m8sp = Buf(P, "m8sp", [128, 4], F32)
    P.act(m8sp[:, :], pv[:, LAM, :], AF.Exp, scale=-1.0)
    P.ts("dve", m8sp[:, :], m8sp[:, :], 1.0, ALU.add)
    P.act(m8sp[:, :], m8sp[:, :], AF.Ln)
    P.ts("dve", m8sp[:, :], m8sp[:, :], -8.0, ALU.mult)
    hb = Buf(P, "hb", [128, 22, W], F32)
    hh = Buf(P, "hh", [128, 22, 3], F32)
    abuf = Buf(P, "abuf", [128, 4, T], F32)
    osts = [Buf(P, "ost%d" % i, [128, 4, T], F32) for i in range(3)]
    ocnt = [0]

    def ost():
        b = osts[ocnt[0] % 3]
        ocnt[0] += 1
        return b

    def hgemm(Tn):
        def epi(oc, ps):
            P.cp("act", hb[:, oc, HALO:HALO + Tn], ps[:, :Tn])
        tp.gemm_fm(owi, 11, 8, 256, lambda k: tp.xb[:, k, :Tn], Tn, epi)

    load_x(tp, xh, HALO)
    tp.ffn(w1i, w1o, 0, HALO)
    hgemm(HALO)
    P.ts("dve", hh[:, :, :], hb[:, :, 2 * HALO - 3:2 * HALO], hmask[:, 0:1], ALU.mult)

    for i in range(NT):
        load_x(tp, xin[i], T)
        tp.ffn(w1i, w1o, 0, T)
        P.dma("sp", x1out[i], tp.xT[:, :, :T])
        hgemm(T)
        P.cp("pool", hb[:, :, HALO - 3:HALO], hh[:, :, :])
        P.cp("pool", hh[:, :, :], hb[:, :, W - 3:W])
        for c in range(14):
            d = tp.scr()
            e1 = "dve" if c % 2 == 0 else "pool"
            P.tt(e1, d[:, :T], hb[:, c, HALO - 1:W - 1], hb[:, c, HALO:W], ALU.subtract)
            P.stt("dve", hb[:, c, HALO:W], d[:, :T], mu[:, c:c + 1], hb[:, c, HALO:W], ALU.mult, ALU.add)
        o_r, o_v = ost(), ost()
        P.cp("pool", o_r[:, :, :T], hb[:, 0:4, HALO:W])
        P.dma("sp", qout[i, 0], o_r[:, :, :T])
        P.cp("pool", o_v[:, :, :T], hb[:, 8:12, HALO:W])
        P.dma("sp", qout[i, 2], o_v[:, :, :T])
        tw = tp.scr()
        P.act(tw[0:64, :T], hb[0:64, 12, HALO:W], AF.Tanh)
        sgd = tp.scr()
        P.act(sgd[:, :T], hb[:, 13, HALO:W], AF.Sigmoid)
        o_ld = ost()
        o_g = ost()
        for c in range(4):
            ps = tp.psum()
            P.mm(ps[:, :T], lora[0:64, c * 128:(c + 1) * 128], tw[0:64, :T], True, True)
            P.act(o_ld[:, c, :T], ps[:, :T], AF.Sigmoid, bias=pv[:, W0, c:c + 1])
            P.ts("pool", o_ld[:, c, :T], o_ld[:, c, :T], -EXPM05, ALU.mult)
            ps = tp.psum()
            P.mm(ps[:, :T], lora[64:128, c * 128:(c + 1) * 128], hb[64:128, 12, HALO:W], True, True)
            P.act(abuf[:, c, :T], ps[:, :T], AF.Sigmoid, bias=pv[:, A0, c:c + 1])
            ps = tp.psum()
            P.mm(ps[:, :T], gup[:, c * 128:(c + 1) * 128], sgd[:, :T], True, True)
            P.cp("act", o_g[:, c, :T], ps[:, :T])
        P.dma("sp", qout[i, 3], o_ld[:, :, :T])
        P.dma("sp", qout[i, 8], o_g[:, :, :T])
        o_al, o_be = ost(), ost()
        for c in range(4):
            k = hb[:, 4 + c, HALO:W]
            kr = tp.scr()
            sq = tp.scr()
            P.ts("dve", kr[:, :T], k, pv[:, KK, c:c + 1], ALU.mult)
            P.act(sq[:, :T], kr[:, :T], AF.Square)
            ps = tp.psum()
            P.mm(ps[:, :T], bd64[:, :], sq[:, :T], True, True)
            P.ts("dve", sq[:, :T], ps[:, :T], 1e-24, ALU.max)
            P.act(sq[:, :T], sq[:, :T], AF.Sqrt)
            P.recip(sq[:, :T], sq[:, :T])
            P.tt("pool", kr[:, :T], kr[:, :T], sq[:, :T], ALU.mult)
            P.ts("pool", o_al[:, c, :T], kr[:, :T], -1.0, ALU.mult)
            P.tt("dve", o_be[:, c, :T], kr[:, :T], abuf[:, c, :T], ALU.mult)
        P.dma("sp", qout[i, 4], o_al[:, :, :T])
        P.dma("sp", qout[i, 5], o_be[:, :, :T])
        o_k = ost()
        for c in range(4):
            t = tp.scr()
            P.ts("dve", t[:, :T], abuf[:, c, :T], pv[:, KA, c:c + 1], ALU.mult, omka[:, c:c + 1], ALU.add)
            P.tt("pool", o_k[:, c, :T], t[:, :T], hb[:, 4 + c, HALO:W], ALU.mult)
        P.dma("sp", qout[i, 1], o_k[:, :, :T])
        o_a, o_bx, o_gg = ost(), ost(), ost()
        for c in range(4):
            xc = tp.scr()
            P.ts("dve", xc[:, :T], hb[:, 18 + c, HALO - 3:W - 3], cw[:, 0, c:c + 1], ALU.mult)
            for tap in range(1, 4):
                P.stt("dve", xc[:, :T], hb[:, 18 + c, HALO - 3 + tap:W - 3 + tap], cw[:, tap, c:c + 1], xc[:, :T], ALU.mult, ALU.add)
            P.ts("pool", xc[:, :T], xc[:, :T], pv[:, CB, c:c + 1], ALU.add)
            ps = tp.psum()
            P.mm(ps[:, :T], wabd[:, c, :], xc[:, :T], True, True)
            la = tp.scr()
            P.act(la[:, :T], ps[:, :T], AF.Sigmoid, bias=pv[:, BA, c:c + 1])
            P.ts("pool", la[:, :T], la[:, :T], m8sp[:, c:c + 1], ALU.mult)
            ps = tp.psum()
            P.mm(ps[:, :T], wxbd[:, c, :], xc[:, :T], True, True)
            ix = tp.scr()
            P.act(ix[:, :T], ps[:, :T], AF.Sigmoid, bias=pv[:, BX, c:c + 1])
            P.tt("pool", ix[:, :T], ix[:, :T], xc[:, :T], ALU.mult)
            P.act(o_a[:, c, :T], la[:, :T], AF.Exp)
            th = tp.scr()
            P.act(th[:, :T], la[:, :T], AF.Tanh)
            a2 = xc
            P.tt("dve", a2[:, :T], o_a[:, c, :T], o_a[:, c, :T], ALU.mult)
            P.stt("dve", th[:, :T], a2[:, :T], 1.0, th[:, :T], ALU.add, ALU.mult)
            P.ts("dve", th[:, :T], th[:, :T], -1.0, ALU.mult, 0.0, ALU.max)
            P.act(th[:, :T], th[:, :T], AF.Sqrt)
            P.tt("dve", o_bx[:, c, :T], th[:, :T], ix[:, :T], ALU.mult)
            tp.gelu(o_gg[:, c, :T], hb[:, 14 + c, HALO:W], T)
        P.dma("sp", qout[i, 6], o_a[:, :, :T])
        P.dma("sp", qout[i, 7], o_bx[:, :, :T])
        P.dma("sp", qout[i, 9], o_gg[:, :, :T])
    P.wait_all("sp", [b.r for b in osts] + [tp.xT.r])
    return P.finish()


_REC_STOP = 99
_REC_VAR = 0


def build_rec(S, SEG=512):
    P = Prog()
    NCH = S // 64
    QS = SEG // 64
    NSEG = S // SEG
    fm = P.dram("fm", [4, 128, S], F32)
    ld = P.dram("ld", [64, NCH, 128], F32)
    bk = P.dram("bk", [128, NCH, 128], F32)
    vv = P.dram("vv", [64, NCH, 128], F32)
    la = P.dram("la", [128, S], F32)
    lb = P.dram("lb", [128, S], F32)
    yout = P.dram("yout", [64, NCH, 128], F32, "ExternalOutput")
    hout = P.dram("hout", [128, S], F32, "ExternalOutput")

    def const(name, shape):
        d = P.dram(name, shape, F32)
        b = Buf(P, "c_" + name, shape, F32)
        P.dma("sp", b[:], d)
        return b
    tri2 = const("tri2", [64, 128])
    trigt2 = const("trigt2", [64, 128])
    mask_sc = const("mask_sc", [128, 256])
    mask_l = const("mask_l", [64, 128])
    ident2 = const("ident2", [64, 128])

    LSEG = min(S, 1024)
    abuf = [Buf(P, "la%d" % i, [128, LSEG], F32) for i in range(2)]
    bbuf = [Buf(P, "lb%d" % i, [128, LSEG], F32) for i in range(2)]
    hbuf = [Buf(P, "lh%d" % i, [128, LSEG], F32) for i in range(2)]
    for s in range(S // LSEG):
        a_, b_, h_ = abuf[s % 2], bbuf[s % 2], hbuf[s % 2]
        P.dma("sp", a_[:], la[:, s * LSEG:(s + 1) * LSEG])
        P.dma("sp", b_[:], lb[:, s * LSEG:(s + 1) * LSEG])
        if s == 0:
            P.op("dve", lambda e, o=h_[:].ap, a=a_[:].ap, b=b_[:].ap: e.tensor_tensor_scan(
                out=o, data0=a, data1=b, initial=0.0, op0=ALU.mult, op1=ALU.add), reads=[a_.r, b_.r], writes=[h_.r])
        else:
            pr = hbuf[(s - 1) % 2]
            P.op("dve", lambda e, o=h_[:].ap, a=a_[:].ap, b=b_[:].ap, i=pr[:, LSEG - 1:LSEG].ap: e.tensor_tensor_scan(
                out=o, data0=a, data1=b, initial=i, op0=ALU.mult, op1=ALU.add), reads=[a_.r, b_.r, pr.r], writes=[h_.r])
        P.dma("sp", hout[:, s * LSEG:(s + 1) * LSEG], h_[:])

    pq = [Buf(P, "ps%d" % i, [128, 512], F32, psum=True) for i in range(8)]
    pi = [0]

    def psum():
        b = pq[pi[0] % 8]
        pi[0] += 1
        return b
    fmb = [Buf(P, "fmb%d" % i, [128, 4, SEG], F32) for i in range(2)]
    ldb = [Buf(P, "ldb%d" % i, [64, QS, 128], F32) for i in range(2)]
    bkb = [Buf(P, "bkb%d" % i, [128, QS, 128], F32) for i in range(2)]
    uvb = [Buf(P, "uvb%d" % i, [128, QS, 128], F32) for i in range(2)]
    yb = [Buf(P, "yb%d" % i, [64, QS, 128], F32) for i in range(2)]
    Hsb = Buf(P, "Hs", [128, 128], F32)
    P.memset("pool", Hsb[:], 0.0)
    Hs = Hsb[:, 0:64]

    W = 10

    def mk(name, shape, n=W):
        return [Buf(P, "%s%d" % (name, i), shape, F32) for i in range(n)]
    e1s, e2s, e3s = mk("e1", [128, 128]), mk("e2", [128, 64]), mk("e3", [128, 128])
    ARs, BKts, BKPs = mk("AR", [128, 128]), mk("BKt", [128, 128]), mk("BKP", [128, 128])
    SCs, L0s = mk("SC", [128, 256]), mk("L0", [64, 128])
    NTs, Lbs = mk("NT", [64, 256], 2 * W), mk("Lb", [64, 128], 2 * W)
    TTfs, XVs, X0s, O1s = mk("TTf", [64, 128]), mk("XV", [64, 128]), mk("X0", [64, 128], 2), mk("O1", [64, 128], 2)

    def h2(v, w):
        return v.re("p (h c) -> p h c", h=2)

    def chunk(n):
        sg, q = divmod(n, QS)
        sb_ = sg % 2
        if q == 0:
            t0 = sg * SEG
            P.dma("sp", fmb[sb_][:], fm[:, :, t0:t0 + SEG].rearrange("f p t -> p f t"))
            P.dma("sp", ldb[sb_][:], ld[:, sg * QS:(sg + 1) * QS, :])
            P.dma("sp", bkb[sb_][:], bk[:, sg * QS:(sg + 1) * QS, :])
            P.dma("sp", uvb[sb_][64:128, :, :], vv[:, sg * QS:(sg + 1) * QS, :])
        F = fmb[sb_]
        tsl = slice(q * 64, (q + 1) * 64)
        LD2 = ldb[sb_][:, q, :]
        UV = uvb[sb_]
        pb = n % W
        e1, e2, e3, AR, BKt, BKP, SC, L0 = e1s[pb], e2s[pb], e3s[pb], ARs[pb], BKts[pb], BKPs[pb], SCs[pb], L0s[pb]
        TTf, XV, X0, O1 = TTfs[pb], XVs[pb], X0s[n % 2], O1s[n % 2]
        NTc, Lbc = NTs[2 * pb:2 * pb + 2], Lbs[2 * pb:2 * pb + 2]
        ps1 = psum()
        P.mm(ps1[:, 0:128], LD2, tri2[:, :], True, True)
        ps2 = psum()
        P.mm(ps2[:, 0:128], trigt2[:, :], LD2, True, True)
        P.act(e1[:, :], ps1[:, 0:128], AF.Exp)
        P.act(e2[:, :], ps1[:, 0:64], AF.Exp, scale=-1.0)
        P.act(e3[:, :], ps2[:, 0:128], AF.Exp)
        P.tt("dve", AR[:, 0:64], F[:, 1, tsl], e1[:, 64:128], ALU.mult)
        P.tt("pool", AR[:, 64:128], F[:, 0, tsl], e1[:, 0:64], ALU.mult)
        P.tt("dve", BKt[:, 0:64], F[:, 2, tsl], e2[:, :], ALU.mult)
        P.tt("pool", BKt[:, 64:128], F[:, 3, tsl], e2[:, :], ALU.mult)
        P.tt("pool", BKP[:, :], bkb[sb_][:, q, :], e3[:, :], ALU.mult)
        yield
        for h in range(2):
            hs = slice(h * 64, (h + 1) * 64)
            psS = psum()
            psL = psum()
            P.mm(psS[:, 0:128], BKt[hs, :], AR[hs, :], True, True)
            P.mm(psL[0:64, 0:64], AR[hs, 0:64], BKt[hs, 0:64], True, True)
            P.tt("dve", SC[:, h * 128:(h + 1) * 128], psS[:, 0:128], mask_sc[:, 0:128], ALU.mult)
            P.tt("dve", L0[:, hs], psL[0:64, 0:64], mask_l[:, 0:64], ALU.mult)
        yield
        SCh = h2(SC[0:64, :], 128)
        NT1, L1 = NTc[0], Lbc[0]
        psA = psum()
        psB = psum()
        for h in range(2):
            hs = slice(h * 64, (h + 1) * 64)
            P.mm(psA[0:64, hs], L0[:, hs], SC[0:64, h * 128:h * 128 + 64], True, True)
            P.mm(psB[0:64, hs], SC[0:64, h * 128:h * 128 + 64], L0[:, hs], True, True)
        P.cp("act", h2(NT1[:, :], 128)[:, :, 0:64], h2(psA[0:64, 0:128], 64))
        P.tt("dve", h2(NT1[:, :], 128)[:, :, 64:128], h2(ident2[:, :], 64), SCh[:, :, 0:64], ALU.add)
        P.cp("act", L1[:, :], psB[0:64, 0:128])
        yield
        for k in range(1, 5):
            cur, nxt, Lc, Ln = NTc[(k - 1) % 2], NTc[k % 2], Lbc[(k - 1) % 2], Lbc[k % 2]
            psA = psum()
            psB = psum()
            for h in range(2):
                hs = slice(h * 64, (h + 1) * 64)
                P.mm(psA[0:64, h * 128:(h + 1) * 128], Lc[:, hs], cur[:, h * 128:(h + 1) * 128], True, True)
                P.mm(psB[0:64, hs], cur[:, h * 128:h * 128 + 64], Lc[:, hs], True, True)
            P.cp("act", h2(nxt[:, :], 128)[:, :, 0:64], h2(psA[0:64, 0:256], 128)[:, :, 0:64])
            P.tt("dve", h2(nxt[:, :], 128)[:, :, 64:128], h2(cur[:, :], 128)[:, :, 64:128],
                 h2(psA[0:64, 0:256], 128)[:, :, 64:128], ALU.add)
            P.cp("act", Ln[:, :], psB[0:64, 0:128])
            yield
        cur, Lc = NTc[0], Lbc[0]
        psA = psum()
        for h in range(2):
            hs = slice(h * 64, (h + 1) * 64)
            P.mm(psA[0:64, hs], Lc[:, hs], cur[:, h * 128 + 64:(h + 1) * 128], True, True)
        P.tt("dve", h2(TTf[:, :], 64), h2(cur[:, :], 128)[:, :, 64:128], h2(psA[0:64, 0:128], 64), ALU.add)
        psX = psum()
        for h in range(2):
            hs = slice(h * 64, (h + 1) * 64)
            P.mm(psX[0:64, hs], SC[64:128, h * 128:h * 128 + 64], UV[64:128, q, hs], True, True)
        P.cp("act", XV[:, :], psX[0:64, 0:128])
        yield
        for h in range(2):
            hs = slice(h * 64, (h + 1) * 64)
            psH = psum()
            psO1 = psum()
            P.mm(psH[0:64, 0:64], AR[hs, 0:64], Hs[hs, :], True, True)
            P.mm(psO1[0:64, 0:64], AR[hs, 64:128], Hs[hs, :], True, True)
            P.tt("dve", X0[:, hs], psH[0:64, 0:64], XV[:, hs], ALU.add)
            P.cp("act", O1[:, hs], psO1[0:64, 0:64])
        psU = psum()
        for h in range(2):
            hs = slice(h * 64, (h + 1) * 64)
            P.mm(psU[0:64, hs], TTf[:, hs], X0[:, hs], True, True)
        P.cp("dve", UV[0:64, q, :], psU[0:64, 0:128])
        psO2 = psum()
        psHn = psum()
        for h in range(2):
            hs = slice(h * 64, (h + 1) * 64)
            P.mm(psO2[0:64, hs], SC[:, h * 128 + 64:(h + 1) * 128], UV[:, q, hs], True, True)
            P.mm(psHn[hs, 0:64], BKP[:, hs], UV[:, q, hs], True, True)
        P.tt("pool" if False else "dve", yb[sb_][:, q, :], psO2[0:64, 0:128], O1[:, :], ALU.add)
        P.stt("dve", Hs[:, :], Hs[:, :], e1[:, 63:64], psHn[:, 0:64], ALU.mult, ALU.add)
        if q == QS - 1:
            P.dma("sp", yout[:, sg * QS:(sg + 1) * QS, :], yb[sb_][:])

    active = []
    nxt_n = 0
    while nxt_n < NCH or active:
        if nxt_n < NCH:
            active.append(chunk(nxt_n))
            nxt_n += 1
        for g in list(active):
            try:
                next(g)
            except StopIteration:
                active.remove(g)
    P.wait_all("sp", [b.r for b in yb] + [b.r for b in hbuf])
    return P.finish()


def build_odd_post(NT, T):
    tp = TP(T)
    P = tp.P
    x1in = P.dram("xin", [NT, 128, 8, T], F32)
    qin = P.dram("qin", [NT, 4, 128, 7, T], F32)
    xout = P.dram("xout", [NT, 128, 8, T], F32, "ExternalOutput")
    owo = P.dram("owo", [2, 128, 8, 512], BF16)
    w2i = P.dram("w2i", [11, 128, 8, 512], BF16)
    w2o = P.dram("w2o", [4, 128, 22, 256], BF16)
    bd64 = tp.const("bd64", [128, 128])
    pvp = tp.const("pvp", [128, 3, 4])
    tp.xattn_setup()
    qbs = [Buf(P, "qb%d" % i, [128, 7, T], F32) for i in range(2)]
    for i in range(NT):
        P.dma("sp", tp.xT[:, :, :T], x1in[i])
        for c in range(4):
            qb = qbs[c % 2]
            P.dma("sp", qb[:, :, :T], qin[i, c])
            y, lh, g, gg, r, kf, v = [qb[:, z, :T] for z in range(7)]
            ps_m = tp.psum()
            P.mm(ps_m[:, :T], bd64[:, :], y, True, True)
            ysq = tp.scr()
            P.act(ysq[:, :T], y, AF.Square)
            ps_q = tp.psum()
            P.mm(ps_q[:, :T], bd64[:, :], ysq[:, :T], True, True)
            mean = tp.lnm
            rstd = tp.lnr
            msq = tp.scr()
            P.ts("dve", mean[:, :T], ps_m[:, :T], 1.0 / 64, ALU.mult)
            P.tt("pool", msq[:, :T], mean[:, :T], mean[:, :T], ALU.mult)
            P.stt("dve", rstd[:, :T], ps_q[:, :T], 1.0 / 64, msq[:, :T], ALU.mult, ALU.subtract)
            P.ts("dve", rstd[:, :T], rstd[:, :T], GN_EPS, ALU.add)
            P.act(rstd[:, :T], rstd[:, :T], AF.Sqrt)
            P.recip(rstd[:, :T], rstd[:, :T])
            yn = tp.scr()
            P.tt("dve", yn[:, :T], y, mean[:, :T], ALU.subtract)
            P.tt("pool", yn[:, :T], yn[:, :T], rstd[:, :T], ALU.mult)
            P.act(yn[:, :T], yn[:, :T], AF.Identity, bias=pvp[:, 2, c:c + 1], scale=pvp[:, 1, c:c + 1])
            pr = tp.scr()
            P.stt("dve", pr[:, :T], r, pvp[:, 0, c:c + 1], kf, ALU.mult, ALU.mult)
            ps_b = tp.psum()
            P.mm(ps_b[:, :T], bd64[:, :], pr[:, :T], True, True)
            P.tt("dve", pr[:, :T], ps_b[:, :T], v, ALU.mult)
            P.tt("pool", yn[:, :T], yn[:, :T], pr[:, :T], ALU.add)
            P.tt("dve", tp.hT[:, c, :T], yn[:, :T], g, ALU.mult)
            P.tt("pool", tp.hT[:, 4 + c, :T], lh, gg, ALU.mult)
        tp.gemm_fm(owo, 2, 8, 512, lambda k: tp.hT[:, k, :T], T, tp.res_epi(1.0, T))
        tp.layernorm(1, 1.0, T)
        tp.xattn(2, T)
        tp.ffn(w2i, w2o, 3, T)
        P.dma("sp", xout[i], tp.xT[:, :, :T])
    P.wait_all("sp", [tp.xT.r])
    return P.finish()


def blockdiag(w):
    out = np.zeros((128, 4, 128), np.float32)
    for h in range(8):
        c, q = divmod(h, 2)
        out[q * 64:(q + 1) * 64, c, q * 64:(q + 1) * 64] = w[h]
    return out


def tok_major(a):
    S = a.shape[1]
    return np.ascontiguousarray(a.T.reshape(S // 64, 64, 128).transpose(1, 0, 2))


def run_odd(inp, bf, l, x, T):
    o = l // 2
    B, S, _ = x.shape
    maps, cps, tpc = common_maps(inp, bf, l, x, T)
    NT = tpc // T
    bd64 = np.kron(np.eye(2, dtype=np.float32), np.ones((64, 64), np.float32))
    pv = np.zeros((128, 12, 4), np.float32)
    for idx, name in enumerate(["rwkv_w0", "rwkv_a0", "rwkv_k_k", "rwkv_k_a", "lru_conv_b", "lru_b_a", "lru_b_x", "lru_lambda"]):
        pv[:, idx, :] = pvec(inp[name][o])
    for c, m in enumerate(maps):
        start = (c % cps) == 0
        m["w1i"] = ffn_in_blocks(bf["ffn1_w_in"][l])
        m["w1o"] = wblocks(bf["ffn1_w_out"][l], 256)
        m["owi"] = wblocks(bf["odd_w_in"][o], 256)
        m["mu"] = pvec(inp["rwkv_mu"][o])
        m["lora"] = np.ascontiguousarray(np.concatenate([inp["rwkv_w_up"][o], inp["rwkv_a_up"][o]], axis=0))
        m["gup"] = np.ascontiguousarray(inp["rwkv_g_up"][o])
        m["bd64"] = bd64
        m["pv"] = pv
        m["convw"] = np.ascontiguousarray(inp["lru_conv_w"][o].reshape(4, 4, 128).transpose(2, 0, 1))
        m["wabd"] = blockdiag(inp["lru_w_a"][o])
        m["wxbd"] = blockdiag(inp["lru_w_x"][o])
        m["hmask"] = np.full((128, 1), 0.0 if start else 1.0, np.float32)
    res = run(build_odd_pre(NT, T), maps)
    x1 = gather_x(res, "x1out", B, S, cps, tpc)
    Q = np.empty((B, 10, 512, S), np.float32)
    for c in range(NCORES):
        b, t0 = c // cps, (c % cps) * tpc
        qo = np.asarray(res[c]["qout"])
        Q[b, :, :, t0:t0 + tpc] = qo.transpose(1, 3, 2, 0, 4).reshape(10, 512, tpc)
    ii = np.arange(64)
    tri_incl = (ii[:, None] <= ii[None, :]).astype(np.float32)
    tri_strict = (ii[:, None] < ii[None, :]).astype(np.float32)
    tri_gt = (ii[:, None] > ii[None, :]).astype(np.float32)
    consts = {
        "tri2": np.concatenate([tri_incl, tri_strict], 1),
        "trigt2": np.concatenate([tri_gt, tri_gt], 1),
        "mask_sc": np.tile(np.concatenate([tri_strict, tri_incl], 1), (2, 2)),
        "mask_l": np.tile(tri_gt, (1, 2)),
        "ident2": np.tile(np.eye(64, dtype=np.float32), (1, 2)),
    }
    maps2 = []
    nhp = NCORES // B
    for c in range(NCORES):
        b, hp = c // nhp, c % nhp
        rows = slice(hp * 128, (hp + 1) * 128)
        m = dict(consts)
        m["fm"] = np.ascontiguousarray(np.stack([Q[b, 0, rows], Q[b, 4, rows], Q[b, 5, rows], Q[b, 1, rows]]))
        m["ld"] = tok_major(Q[b, 3, rows])
        m["bk"] = np.ascontiguousarray(np.concatenate([tok_major(Q[b, 5, rows]), tok_major(Q[b, 1, rows])], 0))
        m["vv"] = tok_major(Q[b, 2, rows])
        m["la"] = np.ascontiguousarray(Q[b, 6, rows])
        m["lb"] = np.ascontiguousarray(Q[b, 7, rows])
        maps2.append(m)
    res2 = run(build_rec(S), maps2)
    Y = np.empty((B, 512, S), np.float32)
    Hl = np.empty((B, 512, S), np.float32)
    for c in range(NCORES):
        b, hp = c // nhp, c % nhp
        rows = slice(hp * 128, (hp + 1) * 128)
        yo = np.asarray(res2[c]["yout"])
        Y[b, rows] = yo.transpose(1, 0, 2).reshape(S, 128).T
        Hl[b, rows] = np.asarray(res2[c]["hout"])
    maps3, _, _ = common_maps(inp, bf, l, x1, T)
    xattn_maps(maps3, inp, bf, l, cps)
    pvp = np.stack([pvec(inp["rwkv_r_k"][o].reshape(-1)), pvec(inp["rwkv_gn_g"][o]), pvec(inp["rwkv_gn_b"][o])], 1)
    for c, m in enumerate(maps3):
        b, t0 = c // cps, (c % cps) * tpc
        sl = slice(t0, t0 + tpc)
        arrs = np.stack([Y[b][:, sl], Hl[b][:, sl], Q[b, 8][:, sl], Q[b, 9][:, sl], Q[b, 0][:, sl], Q[b, 1][:, sl], Q[b, 2][:, sl]])
        m["qin"] = np.ascontiguousarray(arrs.reshape(7, 4, 128, NT, T).transpose(3, 1, 2, 0, 4))
        del m["xh"]
        m["owo"] = wblocks(bf["odd_w_out"][o], 512)
        m["w2i"] = ffn_in_blocks(bf["ffn2_w_in"][l])
        m["w2o"] = wblocks(bf["ffn2_w_out"][l], 256)
        m["bd64"] = bd64
        m["pvp"] = np.ascontiguousarray(pvp)
    res3 = run(build_odd_post(NT, T), maps3)
    return gather_x(res3, "xout", B, S, cps, tpc), dict(x1=x1, Q=Q, Y=Y, Hl=Hl)


BIG = ["ffn1_w_in", "ffn1_w_out", "ffn2_w_in", "ffn2_w_out", "xattn_w_q", "xattn_w_kv", "xattn_w_o",
       "even_w_in", "even_w_out", "pool_w", "sgu_w", "odd_w_in", "odd_w_out"]


def kernel(**inputs):
    inp = {k: np.ascontiguousarray(np.asarray(v, dtype=np.float32)) for k, v in inputs.items()}
    bf = cast_weights({n: inp[n] for n in BIG})
    x = inp["x"]
    T = 512
    for l in range(DEPTH):
        if l % 2 == 0:
            x = run_even(inp, bf, l, x, T)
        else:
            x, _ = run_odd(inp, bf, l, x, T)
    return x.astype(np.float32)
```

```python
import contextlib
import numpy as np
import ml_dtypes
import concourse.bass as bass
import concourse.mybir as mybir
from concourse.bass_utils import run_bass_kernel_spmd

F32 = mybir.dt.float32
BF16 = mybir.dt.bfloat16
AF = mybir.ActivationFunctionType
ALU = mybir.AluOpType
AX = mybir.AxisListType
NPBF = ml_dtypes.bfloat16

NCORES = 8
D = 1024
DFF = 2816
DEPTH = 4
MEM = 256
ALPHA = (2 * DEPTH) ** 0.25
LN_EPS = 1e-5
GN_EPS = 64e-5
HALO = 16
ENGS = ("pe", "act", "dve", "pool", "sp")


class Reg:
    _n = 0

    def __init__(self, name=""):
        Reg._n += 1
        self.id = Reg._n
        self.name = name
        self.lw = None
        self.rd = {}
        self.dsem = None
        self.dcnt = 0


class V:
    def __init__(self, ap, r):
        self.ap = ap
        self.r = r

    def __getitem__(self, idx):
        return V(self.ap[idx], self.r)

    def re(self, pat, **kw):
        return V(self.ap.rearrange(pat, **kw), self.r)


class Buf:
    def __init__(self, P, name, shape, dt, psum=False, reg=None):
        self.t = (P.ps if psum else P.sb)(name, shape, dt)
        self.r = reg or Reg(name)

    def __getitem__(self, idx):
        return V(self.t[idx], self.r)


def _regs(*vs):
    return [v.r for v in vs if isinstance(v, V)]


def _a(v):
    return v.ap if isinstance(v, V) else v


class Prog:
    def __init__(self):
        self.nc = bass.Bass("TRN2", target_bir_lowering=False)
        self.es = contextlib.ExitStack()
        self.q = {e: [] for e in ENGS}
        self.seq = {e: 0 for e in ENGS}
        self.seen = {e: {} for e in ENGS}
        self.sem = {e: self.es.enter_context(self.nc.semaphore("s_" + e)) for e in ENGS}
        self.ninst = 0

    def sb(self, name, shape, dt):
        return self.es.enter_context(self.nc.sbuf_tensor(name, list(shape), dt))

    def ps(self, name, shape, dt=F32):
        return self.es.enter_context(self.nc.psum_tensor(name, list(shape), dt))

    def dram(self, name, shape, dt, kind="ExternalInput"):
        return self.nc.dram_tensor(name, list(shape), dt, kind=kind).ap()

    def _deps(self, reads, writes):
        toks = {}

        def add(t):
            if t is None:
                return
            k, v = t
            if toks.get(k, 0) < v:
                toks[k] = v
        for r in reads:
            add(r.lw)
        for w in writes:
            add(w.lw)
            for k, v in w.rd.items():
                add((k, v))
        return toks

    def _emit_waits(self, eng, toks, skip_self=False):
        seen = self.seen[eng]
        for k, v in toks.items():
            if skip_self and k == eng:
                continue
            if seen.get(k, 0) >= v:
                continue
            seen[k] = v
            semh = self.sem[k] if isinstance(k, str) else k[1]
            self.q[eng].append(lambda e, s=semh, v=v: e.wait_ge(s, v))

    def _commit(self, tok, reads, writes):
        k, v = tok
        for w in writes:
            w.lw = tok
            w.rd = {}
        for r in reads:
            if r in writes:
                continue
            if r.rd.get(k, 0) < v:
                r.rd[k] = v

    def op(self, eng, fn, reads=(), writes=(), skip_self=False):
        reads = list(reads)
        writes = list(writes)
        self._emit_waits(eng, self._deps(reads, writes), skip_self)
        self.seq[eng] += 1
        v = self.seq[eng]
        semh = self.sem[eng]
        self.q[eng].append(lambda e, fn=fn, s=semh: fn(e).then_inc(s, 1))
        self._commit((eng, v), reads, writes)
        self.ninst += 1

    def dma(self, eng, out, in_, **kw):
        reads = _regs(in_)
        writes = _regs(out)
        self._emit_waits(eng, self._deps(reads, writes))
        r = writes[0] if writes else reads[0]
        if r.dsem is None:
            r.dsem = self.es.enter_context(self.nc.semaphore("d%d" % r.id))
        r.dcnt += 16
        semh = r.dsem
        o, i = _a(out), _a(in_)
        self.q[eng].append(lambda e, o=o, i=i, s=semh, kw=kw: e.dma_start(out=o, in_=i, **kw).then_inc(s, 16))
        self._commit((("d", semh, r.id), r.dcnt), reads, writes)
        self.ninst += 1

    def wait_all(self, eng, regs):
        toks = {}
        for r in regs:
            if r.lw is not None:
                k, v = r.lw
                toks[k] = max(toks.get(k, 0), v)
            for k, v in r.rd.items():
                toks[k] = max(toks.get(k, 0), v)
        self._emit_waits(eng, toks)

    def tt(self, eng, out, a, b, op):
        self.op(eng, lambda e: e.tensor_tensor(out=out.ap, in0=a.ap, in1=b.ap, op=op),
                reads=_regs(a, b), writes=[out.r])

    def ts(self, eng, out, a, s1, op0, s2=None, op1=None, accum=None):
        kw = {}
        if op1 is not None:
            kw["op1"] = op1
        if accum is not None:
            kw["accum_out"] = accum.ap
        self.op(eng, lambda e: e.tensor_scalar(out=out.ap, in0=a.ap, scalar1=_a(s1), scalar2=_a(s2), op0=op0, **kw),
                reads=_regs(a, s1, s2), writes=[out.r] + _regs(accum))

    def stt(self, eng, out, a, s, b, op0, op1):
        self.op(eng, lambda e: e.scalar_tensor_tensor(out=out.ap, in0=a.ap, scalar=_a(s), in1=b.ap, op0=op0, op1=op1),
                reads=_regs(a, s, b), writes=[out.r])

    def act(self, out, a, func, bias=None, scale=None, accum=None):
        kw = {}
        if bias is not None:
            kw["bias"] = _a(bias)
        if scale is not None:
            kw["scale"] = _a(scale)
        if accum is not None:
            kw["accum_out"] = accum.ap
        self.op("act", lambda e: e.activation(out=out.ap, in_=a.ap, func=func, **kw),
                reads=_regs(a, bias, scale), writes=[out.r] + _regs(accum))

    def cp(self, eng, out, a):
        if eng == "act":
            self.op(eng, lambda e: e.copy(out=out.ap, in_=a.ap), reads=[a.r], writes=[out.r])
        else:
            self.op(eng, lambda e: e.tensor_copy(out=out.ap, in_=a.ap), reads=[a.r], writes=[out.r])

    def memset(self, eng, out, val):
        self.op(eng, lambda e: e.memset(out.ap, val), writes=[out.r])

    def mm(self, out, lhsT, rhs, start, stop):
        self.op("pe", lambda e: e.matmul(out.ap, lhsT=lhsT.ap, rhs=rhs.ap, start=start, stop=stop),
                reads=_regs(lhsT, rhs), writes=[out.r], skip_self=True)

    def tr(self, out, a, ident):
        self.op("pe", lambda e: e.transpose(out.ap, a.ap, ident.ap), reads=_regs(a, ident), writes=[out.r],
                skip_self=True)

    def recip(self, out, a):
        self.op("dve", lambda e: e.reciprocal(out=out.ap, in_=a.ap), reads=[a.r], writes=[out.r])

    def finish(self):
        nc = self.nc
        with nc.Block() as block:
            @block.sync
            def _(e):
                for f in self.q["sp"]:
                    f(e)

            @block.scalar
            def _(e):
                for f in self.q["act"]:
                    f(e)

            @block.vector
            def _(e):
                for f in self.q["dve"]:
                    f(e)

            @block.gpsimd
            def _(e):
                for f in self.q["pool"]:
                    f(e)

            @block.tensor
            def _(e):
                for f in self.q["pe"]:
                    f(e)
        self.es.close()
        return nc


def run(nc, in_maps):
    return run_bass_kernel_spmd(nc, in_maps, core_ids=list(range(NCORES))).results


CAST_F = 8192


def build_cast(nt):
    P = Prog()
    src = P.dram("src", [nt, 128, CAST_F], F32)
    dst = P.dram("dst", [nt, 128, CAST_F], BF16, "ExternalOutput")
    ins = [Buf(P, "ci%d" % i, [128, CAST_F], F32) for i in range(2)]
    outs = [Buf(P, "co%d" % i, [128, CAST_F], BF16) for i in range(2)]
    h = CAST_F // 2
    for i in range(nt):
        a, o = ins[i % 2], outs[i % 2]
        P.dma("sp", a[:], src[i])
        P.cp("dve", o[:, :h], a[:, :h])
        P.cp("act", o[:, h:], a[:, h:])
        P.dma("sp", dst[i], o[:])
    P.wait_all("sp", [b.r for b in outs])
    return P.finish()


def cast_weights(arrs):
    names = list(arrs)
    flat = np.concatenate([np.ascontiguousarray(arrs[n]).reshape(-1) for n in names])
    per = NCORES * 128 * CAST_F
    nt = -(-flat.size // per)
    pad = np.zeros(nt * per, np.float32)
    pad[:flat.size] = flat
    src = pad.reshape(NCORES, nt, 128, CAST_F)
    res = run(build_cast(nt), [{"src": src[c]} for c in range(NCORES)])
    out = np.concatenate([np.asarray(r["dst"]).reshape(-1) for r in res])
    d = {}
    off = 0
    for n in names:
        sz = arrs[n].size
        d[n] = out[off:off + sz].reshape(arrs[n].shape)
        off += sz
    return d


def wblocks(w, nb):
    K, N = w.shape
    return np.ascontiguousarray(w.reshape(K // 128, 128, N // nb, nb).transpose(2, 1, 0, 3))


def pvec(v):
    return np.ascontiguousarray(v.reshape(-1, 128).T)


def ffn_in_blocks(w_in):
    g, u = w_in[:, :DFF], w_in[:, DFF:]
    K = w_in.shape[0]
    inter = np.stack([g.reshape(K, 22, 128), u.reshape(K, 22, 128)], axis=2).reshape(K, 2 * DFF)
    return wblocks(inter, 512)


class TP:
    def __init__(self, T, nsc=6):
        P = self.P = Prog()
        self.T = T
        self.xT = Buf(P, "xT", [128, 8, T], F32)
        self.xb = Buf(P, "xb", [128, 8, T], BF16)
        self.zT = Buf(P, "zT", [128, 8, T], F32)
        self.hT = Buf(P, "hT", [128, 22, T], BF16)
        self.wq = [Buf(P, "w%d" % i, [128, 5632], BF16) for i in range(3)]
        self.wi = 0
        self.pq = [Buf(P, "ps%d" % i, [128, 512], F32, psum=True) for i in range(7)]
        self.pi = 0
        self.ptr = Buf(P, "ptr", [128, 1024], BF16, psum=True)
        self.sc = [Buf(P, "sc%d" % i, [128, T], F32) for i in range(nsc)]
        self.lnm = Buf(P, "lnm", [128, T], F32)
        self.lnr = Buf(P, "lnr", [128, T], F32)
        self.si = 0
        self.pool2 = None
        self.use2 = False
        self.si2 = 0
        self.sm = [Buf(P, "sm%d" % i, [128, 8], F32) for i in range(12)]
        self.smi = 0
        self.ones = Buf(P, "ones", [128, 128], F32)
        P.memset("pool", self.ones[:], 1.0)
        self.lng = self.const("lng", [128, 4, 8])
        self.lnb = self.const("lnb", [128, 4, 8])

    def const(self, name, shape, dt=F32):
        d = self.P.dram(name, shape, dt)
        b = Buf(self.P, "c_" + name, shape, dt)
        self.P.dma("sp", b[:], d)
        return b

    def psum(self):
        b = self.pq[self.pi % len(self.pq)]
        self.pi += 1
        return b

    def scr(self):
        if self.pool2 is not None and self.use2:
            b = self.pool2[self.si2 % len(self.pool2)]
            self.si2 += 1
            return b
        b = self.sc[self.si % len(self.sc)]
        self.si += 1
        return b

    def small(self):
        b = self.sm[self.smi % len(self.sm)]
        self.smi += 1
        return b

    def wload(self, blk, Kc, NB):
        b = self.wq[self.wi % len(self.wq)]
        self.wi += 1
        v = b[:, :Kc * NB].re("p (k n) -> p k n", k=Kc)
        self.P.dma("sp", v, blk)
        return v

    def gemm_fm(self, wdram, nblk, Kc, NB, rhs_fn, T, epi):
        for _ in self.gemm_fm_gen(wdram, nblk, Kc, NB, rhs_fn, T, epi):
            pass

    def gemm_fm_gen(self, wdram, nblk, Kc, NB, rhs_fn, T, epi):
        per = NB // 128
        for b in range(nblk):
            w = self.wload(wdram[b], Kc, NB)
            for c in range(per):
                ps = self.psum()
                for k in range(Kc):
                    self.P.mm(ps[:, :T], w[:, k, c * 128:(c + 1) * 128], rhs_fn(k), k == 0, k == Kc - 1)
                epi(b * per + c, ps)
            yield

    def res_epi(self, m, T):
        def epi(oc, ps):
            self.P.stt("dve", self.zT[:, oc, :T], self.xT[:, oc, :T], ALPHA / m, ps[:, :T], ALU.mult, ALU.add)
        return epi

    def gelu(self, out, x, T):
        P = self.P
        a = self.scr()
        b = self.scr()
        P.act(a[:, :T], x, AF.Square)
        P.ts("dve", a[:, :T], a[:, :T], 0.044715, ALU.mult, 1.0, ALU.add)
        P.tt("dve", b[:, :T], a[:, :T], x, ALU.mult)
        P.act(b[:, :T], b[:, :T], AF.Sigmoid, scale=1.5957691216057308)
        P.tt("dve", out, b[:, :T], x, ALU.mult)

    def layernorm(self, j, m, T):
        P = self.P
        ps_s = self.psum()
        ps_q = self.psum()
        for k in range(8):
            P.mm(ps_s[:, :T], self.ones[:, :], self.zT[:, k, :T], k == 0, k == 7)
        sqs = [self.scr(), self.scr()]
        for k in range(8):
            sq = sqs[k % 2]
            P.act(sq[:, :T], self.zT[:, k, :T], AF.Square)
            P.mm(ps_q[:, :T], self.ones[:, :], sq[:, :T], k == 0, k == 7)
        mean = self.lnm
        msq = self.scr()
        rstd = self.lnr
        P.ts("dve", mean[:, :T], ps_s[:, :T], 1.0 / D, ALU.mult)
        P.tt("pool", msq[:, :T], mean[:, :T], mean[:, :T], ALU.mult)
        P.stt("dve", rstd[:, :T], ps_q[:, :T], 1.0 / D, msq[:, :T], ALU.mult, ALU.subtract)
        P.ts("dve", rstd[:, :T], rstd[:, :T], LN_EPS / (m * m), ALU.add)
        P.act(rstd[:, :T], rstd[:, :T], AF.Sqrt)
        P.recip(rstd[:, :T], rstd[:, :T])
        for k in range(8):
            t1 = self.scr()
            e1, e2 = ("dve", "pool") if k % 2 == 0 else ("pool", "dve")
            P.tt(e1, t1[:, :T], self.zT[:, k, :T], mean[:, :T], ALU.subtract)
            P.tt(e2, t1[:, :T], t1[:, :T], rstd[:, :T], ALU.mult)
            P.act(self.xT[:, k, :T], t1[:, :T], AF.Identity, bias=self.lnb[:, j, k:k + 1], scale=self.lng[:, j, k:k + 1])
            P.cp("pool", self.xb[:, k, :T], self.xT[:, k, :T])

    def ffn(self, w_in, w_out, j, T):
        for _ in self.ffn_gen(w_in, w_out, j, T):
            pass

    def ffn_gen(self, w_in, w_out, j, T):
        P = self.P
        for b in range(11):
            w = self.wload(w_in[b], 8, 512)
            for pr in range(2):
                jj = 2 * b + pr
                pg = self.psum()
                pu = self.psum()
                for k in range(8):
                    P.mm(pg[:, :T], w[:, k, pr * 256:pr * 256 + 128], self.xb[:, k, :T], k == 0, k == 7)
                for k in range(8):
                    P.mm(pu[:, :T], w[:, k, pr * 256 + 128:pr * 256 + 256], self.xb[:, k, :T], k == 0, k == 7)
                sg = self.scr()
                P.act(sg[:, :T], pg[:, :T], AF.Silu)
                P.tt("dve", self.hT[:, jj, :T], sg[:, :T], pu[:, :T], ALU.mult)
            yield
        yield from self.gemm_fm_gen(w_out, 4, 22, 256, lambda k: self.hT[:, k, :T], T, self.res_epi(0.5, T))
        self.layernorm(j, 0.5, T)
        yield

    def xattn_setup(self):
        P = self.P
        self.xa_wq = P.dram("xwq", [2, 128, 8, 512], BF16)
        self.xa_wkv = P.dram("xwkv", [4, 128, 8, 512], BF16)
        self.xa_wo = P.dram("xwo", [2, 128, 8, 512], BF16)
        memd = P.dram("memT", [128, 8, MEM], F32)
        memf = self.zT
        P.dma("sp", memf[:, :, :MEM], memd)
        self.ident = self.const("ident", [128, 128], BF16)
        memb = Buf(P, "memb", [128, 8, MEM], BF16)
        P.cp("dve", memb[:], memf[:, :, :MEM])
        self.kT = Buf(P, "kT", [128, 8, MEM], BF16)
        self.vtok = Buf(P, "vtok", [128, 2, D], BF16)
        self.qT = Buf(P, "qT", [128, 8, self.T], BF16)
        self.pTa = Buf(P, "pTa", [128, 2, self.T], BF16)
        self.eb = [Buf(P, "eb%d" % i, [128, MEM], F32) for i in range(4)]
        self.pb = [Buf(P, "pb%d" % i, [128, MEM], BF16) for i in range(4)]

        def kepi(oc, ps):
            P.cp("act", self.kT[:, oc, :], ps[:, :MEM])
        self.gemm_fm(self.xa_wkv, 2, 8, 512, lambda k: memb[:, k, :], MEM, kepi)
        for b in range(2):
            w = self.wload(self.xa_wkv[2 + b], 8, 512)
            for mc in range(2):
                ps = self.psum()
                for k in range(8):
                    P.mm(ps[:, :512], memb[:, k, mc * 128:(mc + 1) * 128], w[:, k, :], k == 0, k == 7)
                P.cp("act", self.vtok[:, mc, b * 512:(b + 1) * 512], ps[:, :512])

    def xattn(self, j, T):
        P = self.P

        def qepi(oc, ps):
            P.cp("act", self.qT[:, oc, :T], ps[:, :T])
        self.gemm_fm(self.xa_wq, 2, 8, 512, lambda k: self.xb[:, k, :T], T, qepi)
        ntc = T // 128
        for h in range(4):
            pss = []
            for tc in range(ntc):
                ps = self.psum()
                for dc in range(2):
                    P.mm(ps[:, :MEM], self.qT[:, 2 * h + dc, tc * 128:(tc + 1) * 128], self.kT[:, 2 * h + dc, :], dc == 0, dc == 1)
                pss.append(ps)
            mxs = [self.small() for _ in range(ntc)]
            for tc in range(ntc):
                P.op("dve", lambda e, o=mxs[tc][:, 0:1].ap, i=pss[tc][:, :MEM].ap: e.tensor_reduce(out=o, in_=i, axis=AX.X, op=ALU.max),
                     reads=[pss[tc].r], writes=[mxs[tc].r])
            for tc in range(ntc):
                P.ts("dve", mxs[tc][:, 1:2], mxs[tc][:, 0:1], -1.0 / 16.0, ALU.mult)
            for tc in range(ntc):
                P.act(self.eb[tc][:, :], pss[tc][:, :MEM], AF.Exp, bias=mxs[tc][:, 1:2], scale=1.0 / 16.0, accum=mxs[tc][:, 2:3])
            for tc in range(ntc):
                P.recip(mxs[tc][:, 3:4], mxs[tc][:, 2:3])
            for tc in range(ntc):
                P.ts("dve", self.pb[tc][:, :], self.eb[tc][:, :], mxs[tc][:, 3:4], ALU.mult)
            for tc in range(ntc):
                for mc in range(2):
                    P.tr(self.ptr[:, tc * 256 + mc * 128:tc * 256 + (mc + 1) * 128], self.pb[tc][:, mc * 128:(mc + 1) * 128], self.ident[:, :])
            for tc in range(ntc):
                P.cp("act" if tc % 2 == 0 else "dve", self.pTa[:, :, tc * 128:(tc + 1) * 128],
                     self.ptr[:, tc * 256:(tc + 1) * 256].re("p (m t) -> p m t", m=2))
            for dc in range(2):
                po = self.psum()
                for mc in range(2):
                    P.mm(po[:, :T], self.vtok[:, mc, h * 256 + dc * 128:h * 256 + dc * 128 + 128], self.pTa[:, mc, :T], mc == 0, mc == 1)
                P.cp("act", self.hT[:, 2 * h + dc, :T], po[:, :T])
        self.gemm_fm(self.xa_wo, 2, 8, 512, lambda k: self.hT[:, k, :T], T, self.res_epi(1.0, T))
        self.layernorm(j, 1.0, T)

    def even_setup(self):
        P = self.P
        T = self.T
        self.e_wi = P.dram("ewi", [3, 128, 8, 512], BF16)
        self.e_wo = P.dram("ewo", [2, 128, 8, 512], BF16)
        self.poolw = self.const("poolw", [128, 4, 128], BF16)
        self.pscale = self.const("pscale", [128, 4])
        self.sgG = self.const("sgG", [128, 512])
        self.sgB = self.const("sgB", [128, 512])
        wsT = self.const("wsT", [128, 4, 128], BF16)
        mask = self.const("trimask", [128, 128], BF16)
        self.wsTm = Buf(P, "wsTm", [128, 4, 128], BF16)
        for h in range(4):
            P.tt("dve", self.wsTm[:, h, :], wsT[:, h, :], mask[:, :], ALU.mult)
        self.BS = self.const("sgBS", [128, 4, T])
        self.corr = self.const("corr", [128, 4, HALO])
        self.hmask = self.const("hmask", [128, 1])
        self.xa = [Buf(P, "xa%d" % i, [128, 4, HALO + T], F32) for i in range(2)]
        self.sA = [Buf(P, "sA%d" % i, [128, HALO + T], F32) for i in range(2)]
        self.pooled = Buf(P, "pooled", [128, 4, T], BF16)
        self.gu = Buf(P, "gu", [128, 4, T], F32)
        self.gv = [Buf(P, "gv%d" % i, [128, 512], F32) for i in range(2)]
        self.vnb = Buf(P, "vnb", [128, T // 128, 512], BF16)
        self.bst = Buf(P, "bst", [128, 8], F32)

    def even_halo(self):
        P = self.P
        w = self.wload(self.e_wi[0], 8, 512)
        for g in range(4):
            ps = self.psum()
            for k in range(8):
                P.mm(ps[:, :HALO], w[:, k, g * 128:(g + 1) * 128], self.xb[:, k, :HALO], k == 0, k == 7)
            P.ts("dve", self.xa[0][:, g, 0:HALO], ps[:, :HALO], self.hmask[:, 0:1], ALU.mult)

    def even_mixer(self, i, j):
        P = self.P
        T = self.T
        xa = self.xa[i % 2]
        if i > 0:
            P.cp("pool", xa[:, :, 0:HALO], self.xa[(i - 1) % 2][:, :, T:T + HALO])
        w = self.wload(self.e_wi[0], 8, 512)
        for g in range(4):
            ps = self.psum()
            for k in range(8):
                P.mm(ps[:, :T], w[:, k, g * 128:(g + 1) * 128], self.xb[:, k, :T], k == 0, k == 7)
            P.cp("act", xa[:, g, HALO:HALO + T], ps[:, :T])
        w = self.wload(self.e_wi[1], 8, 512)
        for c in range(4):
            ps = self.psum()
            for k in range(8):
                P.mm(ps[:, :T], w[:, k, c * 128:(c + 1) * 128], self.xb[:, k, :T], k == 0, k == 7)
            self.gelu(self.gu[:, c, :T], ps[:, :T], T)
        w = self.wload(self.e_wi[2], 8, 512)
        for tc in range(T // 128):
            ps = self.psum()
            for k in range(8):
                P.mm(ps[:, :512], self.xb[:, k, tc * 128:(tc + 1) * 128], w[:, k, :], k == 0, k == 7)
            gv = self.gv[tc % 2]
            self.gelu512(gv, ps)
            st = self.bst
            P.op("dve", lambda e, o=st[:, 0:6].ap, a=gv[:, :].ap: e.bn_stats(out=o, in_=a), reads=[gv.r], writes=[st.r])
            mv = self.small()
            P.op("dve", lambda e, o=mv[:, 0:2].ap, a=st[:, 0:6].ap: e.bn_aggr(out=o, in_=a), reads=[st.r], writes=[mv.r])
            P.ts("dve", mv[:, 2:3], mv[:, 1:2], LN_EPS, ALU.add)
            P.act(mv[:, 2:3], mv[:, 2:3], AF.Sqrt)
            P.recip(mv[:, 3:4], mv[:, 2:3])
            P.ts("dve", gv[:, :], gv[:, :], mv[:, 0:1], ALU.subtract, mv[:, 3:4], ALU.mult)
            P.tt("pool", gv[:, :], gv[:, :], self.sgG[:, :], ALU.mult)
            P.tt("dve", self.vnb[:, tc, :], gv[:, :], self.sgB[:, :], ALU.add)
        for h in range(4):
            ps = self.psum()
            for tc in range(T // 128):
                P.mm(ps[:, tc * 128:(tc + 1) * 128], self.vnb[:, tc, h * 128:(h + 1) * 128], self.wsTm[:, h, :], True, True)
            t = self.scr()
            P.tt("dve", t[:, :T], ps[:, :T], self.BS[:, h, :T], ALU.add)
            P.tt("pool", self.hT[:, 4 + h, :T], t[:, :T], self.gu[:, h, :T], ALU.mult)
        W = HALO + T
        for g, win in enumerate((2, 4, 8, 16)):
            cur = xa[:, g, :]
            src = cur
            lo, sh, step = 0, 1, 0
            while sh < win:
                dst = self.sA[step % 2][:, :]
                lo2 = lo + sh
                P.tt("pool" if step % 2 else "dve", dst[:, lo2:W], src[:, lo2:W], src[:, lo2 - sh:W - sh], ALU.add)
                src, lo, sh, step = dst, lo2, sh * 2, step + 1
            if i == 0:
                P.tt("dve", src[:, HALO:2 * HALO], src[:, HALO:2 * HALO], self.corr[:, g, :], ALU.mult)
            P.stt("dve", self.pooled[:, g, :T], src[:, HALO:W], 1.0 / win, cur[:, HALO:W], ALU.mult, ALU.subtract)
            ps = self.psum()
            P.mm(ps[:, :T], self.poolw[:, g, :], self.pooled[:, g, :T], True, True)
            P.act(self.hT[:, g, :T], ps[:, :T], AF.Identity, scale=self.pscale[:, g:g + 1])
        self.gemm_fm(self.e_wo, 2, 8, 512, lambda k: self.hT[:, k, :T], T, self.res_epi(1.0, T))
        self.layernorm(j, 1.0, T)

    def gelu512(self, out, ps):
        P = self.P
        a = self.scr()
        b = self.scr()
        x = ps[:, :512]
        P.act(a[:, :], x, AF.Square)
        P.ts("dve", a[:, :], a[:, :], 0.044715, ALU.mult, 1.0, ALU.add)
        P.tt("dve", b[:, :], a[:, :], x, ALU.mult)
        P.act(b[:, :], b[:, :], AF.Sigmoid, scale=1.5957691216057308)
        P.tt("dve", out[:, :], b[:, :], x, ALU.mult)


def load_x(tp, src, T):
    P = tp.P
    P.dma("sp", tp.xT[:, :, :T], src)
    P.cp("pool", tp.xb[:, 0:4, :T], tp.xT[:, 0:4, :T])
    P.cp("act", tp.xb[:, 4:8, :T], tp.xT[:, 4:8, :T])


def build_even(NT, T):
    tp = TP(T)
    P = tp.P
    xin = P.dram("xin", [NT, 128, 8, T], F32)
    xh = P.dram("xh", [128, 8, HALO], F32)
    xout = P.dram("xout", [NT, 128, 8, T], F32, "ExternalOutput")
    w1i = P.dram("w1i", [11, 128, 8, 512], BF16)
    w1o = P.dram("w1o", [4, 128, 22, 256], BF16)
    w2i = P.dram("w2i", [11, 128, 8, 512], BF16)
    w2o = P.dram("w2o", [4, 128, 22, 256], BF16)
    tp.xattn_setup()
    tp.even_setup()
    load_x(tp, xh, HALO)
    tp.ffn(w1i, w1o, 0, HALO)
    tp.even_halo()
    for i in range(NT):
        load_x(tp, xin[i], T)
        tp.ffn(w1i, w1o, 0, T)
        tp.even_mixer(i, 1)
        tp.xattn(2, T)
        tp.ffn(w2i, w2o, 3, T)
        P.dma("sp", xout[i], tp.xT[:, :, :T])
    P.wait_all("sp", [tp.xT.r])
    return P.finish()


def to_fm(xs, T):
    n = xs.shape[0] // T
    F = xs.shape[1]
    return np.ascontiguousarray(xs.reshape(n, T, F // 128, 128).transpose(0, 3, 2, 1))


def from_fm(t):
    n, _, Fc, T = t.shape
    return np.ascontiguousarray(np.asarray(t).transpose(0, 3, 2, 1).reshape(n * T, Fc * 128))


def halo_fm(x, b, t0):
    if t0 == 0:
        return np.zeros((128, 8, HALO), np.float32)
    h = x[b, t0 - HALO:t0]
    return np.ascontiguousarray(h.reshape(HALO, 8, 128).transpose(2, 1, 0))


def common_maps(inp, bf, l, x, T):
    B, S, _ = x.shape
    cps = NCORES // B
    tpc = S // cps
    maps = []
    for c in range(NCORES):
        b, t0 = c // cps, (c % cps) * tpc
        m = {
            "xin": to_fm(x[b, t0:t0 + tpc], T),
            "xh": halo_fm(x, b, t0),
            "lng": np.ascontiguousarray(inp["ln_g"][l].reshape(4, 8, 128).transpose(2, 0, 1)),
            "lnb": np.ascontiguousarray(inp["ln_b"][l].reshape(4, 8, 128).transpose(2, 0, 1)),
        }
        maps.append(m)
    return maps, cps, tpc


def xattn_maps(maps, inp, bf, l, cps):
    for c, m in enumerate(maps):
        b = c // cps
        m["xwq"] = wblocks(bf["xattn_w_q"][l], 512)
        m["xwkv"] = wblocks(bf["xattn_w_kv"][l], 512)
        m["xwo"] = wblocks(bf["xattn_w_o"][l], 512)
        m["memT"] = np.ascontiguousarray(inp["mem"][b].reshape(MEM, 8, 128).transpose(2, 1, 0))
        m["ident"] = np.eye(128, dtype=np.float32).astype(NPBF)


def gather_x(res, key, B, S, cps, tpc):
    out = np.empty((B, S, D), np.float32)
    for c in range(NCORES):
        b, t0 = c // cps, (c % cps) * tpc
        out[b, t0:t0 + tpc] = from_fm(res[c][key])
    return out


def run_even(inp, bf, l, x, T):
    e = l // 2
    maps, cps, tpc = common_maps(inp, bf, l, x, T)
    xattn_maps(maps, inp, bf, l, cps)
    tri = (np.arange(128)[:, None] <= np.arange(128)[None, :]).astype(np.float32).astype(NPBF)
    for c, m in enumerate(maps):
        start = (c % cps) == 0
        m["w1i"] = ffn_in_blocks(bf["ffn1_w_in"][l])
        m["w1o"] = wblocks(bf["ffn1_w_out"][l], 256)
        m["w2i"] = ffn_in_blocks(bf["ffn2_w_in"][l])
        m["w2o"] = wblocks(bf["ffn2_w_out"][l], 256)
        m["ewi"] = wblocks(bf["even_w_in"][e], 512)
        m["ewo"] = wblocks(bf["even_w_out"][e], 512)
        m["poolw"] = np.ascontiguousarray(bf["pool_w"][e].transpose(1, 0, 2))
        m["pscale"] = pvec(inp["pool_scale"][e])
        m["sgG"] = np.ascontiguousarray(np.broadcast_to(inp["sgu_ln_g"][e][None, :], (128, 512)))
        m["sgB"] = np.ascontiguousarray(np.broadcast_to(inp["sgu_ln_b"][e][None, :], (128, 512)))
        m["wsT"] = np.ascontiguousarray(bf["sgu_w"][e].transpose(2, 0, 1))
        m["trimask"] = tri
        m["sgBS"] = np.ascontiguousarray(np.tile(inp["sgu_b"][e][None, :, :], (128, 1, T // 128)))
        corr = np.ones((128, 4, HALO), np.float32)
        if start:
            pos = np.arange(1, HALO + 1, dtype=np.float32)
            for g, win in enumerate((2, 4, 8, 16)):
                corr[:, g, :] = win / np.minimum(pos, win)
        m["corr"] = corr
        m["hmask"] = np.full((128, 1), 0.0 if start else 1.0, np.float32)
    B, S, _ = x.shape
    res = run(build_even(tpc // T, T), maps)
    return gather_x(res, "xout", B, S, cps, tpc)


EXPM05 = float(np.exp(-0.5))


def build_odd_pre(NT, T):
    tp = TP(T, nsc=4)
    P = tp.P
    W = HALO + T
    xin = P.dram("xin", [NT, 128, 8, T], F32)
    xh = P.dram("xh", [128, 8, HALO], F32)
    x1out = P.dram("x1out", [NT, 128, 8, T], F32, "ExternalOutput")
    qout = P.dram("qout", [NT, 10, 128, 4, T], F32, "ExternalOutput")
    w1i = P.dram("w1i", [11, 128, 8, 512], BF16)
    w1o = P.dram("w1o", [4, 128, 22, 256], BF16)
    owi = P.dram("owi", [11, 128, 8, 256], BF16)
    mu = tp.const("mu", [128, 14])
    lora = tp.const("lora", [128, 512])
    gup = tp.const("gup", [128, 512])
    bd64 = tp.const("bd64", [128, 128])
    pv = tp.const("pv", [128, 12, 4])
    W0, A0, KK, KA, CB, BA, BX, LAM = range(8)
    cw = tp.const("convw", [128, 4, 4])
    wabd = tp.const("wabd", [128, 4, 128])
    wxbd = tp.const("wxbd", [128, 4, 128])
    hmask = tp.const("hmask", [128, 1])
    omka = Buf(P, "omka", [128, 4], F32)
    P.ts("dve", omka[:, :], pv[:, KA, :], -1.0, ALU.mult, 1.0, ALU.add)
    m8sp = Buf(P, "m8sp", [128, 4], F32)
    P.act(m8sp[:, :], pv[:, LAM, :], AF.Exp, scale=-1.0)
    P.ts("dve", m8sp[:, :], m8sp[:, :], 1.0, ALU.add)
    P.act(m8sp[:, :], m8sp[:, :], AF.Ln)
    P.ts("dve", m8sp[:, :], m8sp[:, :], -8.0, ALU.mult)
    hb = Buf(P, "hb", [128, 22, W], F32)
    hh = Buf(P, "hh", [128, 22, 3], F32)
    abuf = Buf(P, "abuf", [128, 4, T], F32)
    osts = [Buf(P, "ost%d" % i, [128, 4, T], F32) for i in range(3)]
    ocnt = [0]

    def ost():
        b = osts[ocnt[0] % 3]
        ocnt[0] += 1
        return b

    def hgemm(Tn):
        def epi(oc, ps):
            P.cp("act", hb[:, oc, HALO:HALO + Tn], ps[:, :Tn])
        tp.gemm_fm(owi, 11, 8, 256, lambda k: tp.xb[:, k, :Tn], Tn, epi)

    load_x(tp, xh, HALO)
    tp.ffn(w1i, w1o, 0, HALO)
    hgemm(HALO)
    P.ts("dve", hh[:, :, :], hb[:, :, 2 * HALO - 3:2 * HALO], hmask[:, 0:1], ALU.mult)

    tp.pool2 = [Buf(P, "sc2_%d" % i, [128, T], F32) for i in range(6)]

    def front(i):
        load_x(tp, xin[i], T)
        yield from tp.ffn_gen(w1i, w1o, 0, T)
        P.dma("sp", x1out[i], tp.xT[:, :, :T])

    def hstage():
        hgemm(T)
        P.cp("pool", hb[:, :, HALO - 3:HALO], hh[:, :, :])
        P.cp("pool", hh[:, :, :], hb[:, :, W - 3:W])

    def back(i):
            for c in range(14):
                d = tp.scr()
                e1 = "dve" if c % 2 == 0 else "pool"
                P.tt(e1, d[:, :T], hb[:, c, HALO - 1:W - 1], hb[:, c, HALO:W], ALU.subtract)
                P.stt("dve", hb[:, c, HALO:W], d[:, :T], mu[:, c:c + 1], hb[:, c, HALO:W], ALU.mult, ALU.add)
                if c % 4 == 3:
                    yield
            o_r, o_v = ost(), ost()
            P.cp("pool", o_r[:, :, :T], hb[:, 0:4, HALO:W])
            P.dma("sp", qout[i, 0], o_r[:, :, :T])
            P.cp("pool", o_v[:, :, :T], hb[:, 8:12, HALO:W])
            P.dma("sp", qout[i, 2], o_v[:, :, :T])
            yield
            tw = tp.scr()
            P.act(tw[0:64, :T], hb[0:64, 12, HALO:W], AF.Tanh)
            sgd = tp.scr()
            P.act(sgd[:, :T], hb[:, 13, HALO:W], AF.Sigmoid)
            o_ld = ost()
            o_g = ost()
            for c in range(4):
                ps = tp.psum()
                P.mm(ps[:, :T], lora[0:64, c * 128:(c + 1) * 128], tw[0:64, :T], True, True)
                P.act(o_ld[:, c, :T], ps[:, :T], AF.Sigmoid, bias=pv[:, W0, c:c + 1])
                P.ts("pool", o_ld[:, c, :T], o_ld[:, c, :T], -EXPM05, ALU.mult)
                ps = tp.psum()
                P.mm(ps[:, :T], lora[64:128, c * 128:(c + 1) * 128], hb[64:128, 12, HALO:W], True, True)
                P.act(abuf[:, c, :T], ps[:, :T], AF.Sigmoid, bias=pv[:, A0, c:c + 1])
                ps = tp.psum()
                P.mm(ps[:, :T], gup[:, c * 128:(c + 1) * 128], sgd[:, :T], True, True)
                P.cp("act", o_g[:, c, :T], ps[:, :T])
                yield
            P.dma("sp", qout[i, 3], o_ld[:, :, :T])
            P.dma("sp", qout[i, 8], o_g[:, :, :T])
            yield
            o_al, o_be = ost(), ost()
            for c in range(4):
                k = hb[:, 4 + c, HALO:W]
                kr = tp.scr()
                sq = tp.scr()
                P.ts("dve", kr[:, :T], k, pv[:, KK, c:c + 1], ALU.mult)
                P.act(sq[:, :T], kr[:, :T], AF.Square)
                ps = tp.psum()
                P.mm(ps[:, :T], bd64[:, :], sq[:, :T], True, True)
                P.ts("dve", sq[:, :T], ps[:, :T], 1e-24, ALU.max)
                P.act(sq[:, :T], sq[:, :T], AF.Sqrt)
                P.recip(sq[:, :T], sq[:, :T])
                P.tt("pool", kr[:, :T], kr[:, :T], sq[:, :T], ALU.mult)
                P.ts("pool", o_al[:, c, :T], kr[:, :T], -1.0, ALU.mult)
                P.tt("dve", o_be[:, c, :T], kr[:, :T], abuf[:, c, :T], ALU.mult)
                yield
            P.dma("sp", qout[i, 4], o_al[:, :, :T])
            P.dma("sp", qout[i, 5], o_be[:, :, :T])
            yield
            o_k = ost()
            for c in range(4):
                t = tp.scr()
                P.ts("dve", t[:, :T], abuf[:, c, :T], pv[:, KA, c:c + 1], ALU.mult, omka[:, c:c + 1], ALU.add)
                P.tt("pool", o_k[:, c, :T], t[:, :T], hb[:, 4 + c, HALO:W], ALU.mult)
            P.dma("sp", qout[i, 1], o_k[:, :, :T])
            yield
            o_a, o_bx, o_gg = ost(), ost(), ost()
            for c in range(4):
                xc = tp.scr()
                P.ts("dve", xc[:, :T], hb[:, 18 + c, HALO - 3:W - 3], cw[:, 0, c:c + 1], ALU.mult)
                for tap in range(1, 4):
                    P.stt("dve", xc[:, :T], hb[:, 18 + c, HALO - 3 + tap:W - 3 + tap], cw[:, tap, c:c + 1], xc[:, :T], ALU.mult, ALU.add)
                P.ts("pool", xc[:, :T], xc[:, :T], pv[:, CB, c:c + 1], ALU.add)
                ps = tp.psum()
                P.mm(ps[:, :T], wabd[:, c, :], xc[:, :T], True, True)
                la = tp.scr()
                P.act(la[:, :T], ps[:, :T], AF.Sigmoid, bias=pv[:, BA, c:c + 1])
                P.ts("pool", la[:, :T], la[:, :T], m8sp[:, c:c + 1], ALU.mult)
                ps = tp.psum()
                P.mm(ps[:, :T], wxbd[:, c, :], xc[:, :T], True, True)
                ix = tp.scr()
                P.act(ix[:, :T], ps[:, :T], AF.Sigmoid, bias=pv[:, BX, c:c + 1])
                P.tt("pool", ix[:, :T], ix[:, :T], xc[:, :T], ALU.mult)
                yield
                P.act(o_a[:, c, :T], la[:, :T], AF.Exp)
                th = tp.scr()
                P.act(th[:, :T], la[:, :T], AF.Tanh)
                a2 = xc
                P.tt("dve", a2[:, :T], o_a[:, c, :T], o_a[:, c, :T], ALU.mult)
                P.stt("dve", th[:, :T], a2[:, :T], 1.0, th[:, :T], ALU.add, ALU.mult)
                P.ts("dve", th[:, :T], th[:, :T], -1.0, ALU.mult, 0.0, ALU.max)
                P.act(th[:, :T], th[:, :T], AF.Sqrt)
                P.tt("dve", o_bx[:, c, :T], th[:, :T], ix[:, :T], ALU.mult)
                yield
                tp.gelu(o_gg[:, c, :T], hb[:, 14 + c, HALO:W], T)
                yield
            P.dma("sp", qout[i, 6], o_a[:, :, :T])
            P.dma("sp", qout[i, 7], o_bx[:, :, :T])
            P.dma("sp", qout[i, 9], o_gg[:, :, :T])

    def step(g, use2):
        tp.use2 = use2
        try:
            next(g)
            return True
        except StopIteration:
            return False
        finally:
            tp.use2 = False

    f = front(0)
    while step(f, False):
        pass
    hstage()
    for i in range(NT):
        b = back(i)
        f = front(i + 1) if i + 1 < NT else None
        bl, fl = True, f is not None
        while bl or fl:
            if bl:
                bl = step(b, True)
            if fl:
                fl = step(f, False)
        if f is not None:
            hstage()
    P.wait_all("sp", [b.r for b in osts] + [tp.xT.r])
    return P.finish()


_REC_STOP = 99
_REC_VAR = 0


def build_rec(S, SEG=512):
    P = Prog()
    NCH = S // 64
    QS = SEG // 64
    NSEG = S // SEG
    fm = P.dram("fm", [4, 128, S], F32)
    ld = P.dram("ld", [64, NCH, 128], F32)
    bk = P.dram("bk", [128, NCH, 128], F32)
    vv = P.dram("vv", [64, NCH, 128], F32)
    la = P.dram("la", [128, S], F32)
    lb = P.dram("lb", [128, S], F32)
    yout = P.dram("yout", [64, NCH, 128], F32, "ExternalOutput")
    hout = P.dram("hout", [128, S], F32, "ExternalOutput")

    def const(name, shape):
        d = P.dram(name, shape, F32)
        b = Buf(P, "c_" + name, shape, F32)
        P.dma("sp", b[:], d)
        return b
    tri2 = const("tri2", [64, 128])
    trigt2 = const("trigt2", [64, 128])
    mask_sc = const("mask_sc", [128, 256])
    mask_l = const("mask_l", [64, 128])
    ident2 = const("ident2", [64, 128])

    LSEG = min(S, 1024)
    abuf = [Buf(P, "la%d" % i, [128, LSEG], F32) for i in range(2)]
    bbuf = [Buf(P, "lb%d" % i, [128, LSEG], F32) for i in range(2)]
    hbuf = [Buf(P, "lh%d" % i, [128, LSEG], F32) for i in range(2)]
    for s in range(S // LSEG):
        a_, b_, h_ = abuf[s % 2], bbuf[s % 2], hbuf[s % 2]
        P.dma("sp", a_[:], la[:, s * LSEG:(s + 1) * LSEG])
        P.dma("sp", b_[:], lb[:, s * LSEG:(s + 1) * LSEG])
        if s == 0:
            P.op("dve", lambda e, o=h_[:].ap, a=a_[:].ap, b=b_[:].ap: e.tensor_tensor_scan(
                out=o, data0=a, data1=b, initial=0.0, op0=ALU.mult, op1=ALU.add), reads=[a_.r, b_.r], writes=[h_.r])
        else:
            pr = hbuf[(s - 1) % 2]
            P.op("dve", lambda e, o=h_[:].ap, a=a_[:].ap, b=b_[:].ap, i=pr[:, LSEG - 1:LSEG].ap: e.tensor_tensor_scan(
                out=o, data0=a, data1=b, initial=i, op0=ALU.mult, op1=ALU.add), reads=[a_.r, b_.r, pr.r], writes=[h_.r])
        P.dma("sp", hout[:, s * LSEG:(s + 1) * LSEG], h_[:])

    pq = [Buf(P, "ps%d" % i, [128, 512], F32, psum=True) for i in range(8)]
    pi = [0]

    def psum():
        b = pq[pi[0] % 8]
        pi[0] += 1
        return b
    fmb = [Buf(P, "fmb%d" % i, [128, 4, SEG], F32) for i in range(2)]
    ldb = [Buf(P, "ldb%d" % i, [64, QS, 128], F32) for i in range(2)]
    bkb = [Buf(P, "bkb%d" % i, [128, QS, 128], F32) for i in range(2)]
    uvb = [Buf(P, "uvb%d" % i, [128, QS, 128], F32) for i in range(2)]
    yb = [Buf(P, "yb%d" % i, [64, QS, 128], F32) for i in range(2)]
    Hsb = Buf(P, "Hs", [128, 128], F32)
    P.memset("pool", Hsb[:], 0.0)
    Hs = Hsb[:, 0:64]

    W = 10

    def mk(name, shape, n=W):
        return [Buf(P, "%s%d" % (name, i), shape, F32) for i in range(n)]
    e1s, e2s, e3s = mk("e1", [128, 128]), mk("e2", [128, 64]), mk("e3", [128, 128])
    ARs, BKts, BKPs = mk("AR", [128, 128]), mk("BKt", [128, 128]), mk("BKP", [128, 128])
    SCs, L0s = mk("SC", [128, 256]), mk("L0", [64, 128])
    NTs, Lbs = mk("NT", [64, 256], 2 * W), mk("Lb", [64, 128], 2 * W)
    TTfs, XVs, X0s, O1s = mk("TTf", [64, 128]), mk("XV", [64, 128]), mk("X0", [64, 128], 2), mk("O1", [64, 128], 2)

    def h2(v, w):
        return v.re("p (h c) -> p h c", h=2)

    def chunk(n):
        sg, q = divmod(n, QS)
        sb_ = sg % 2
        if q == 0:
            t0 = sg * SEG
            P.dma("sp", fmb[sb_][:], fm[:, :, t0:t0 + SEG].rearrange("f p t -> p f t"))
            P.dma("sp", ldb[sb_][:], ld[:, sg * QS:(sg + 1) * QS, :])
            P.dma("sp", bkb[sb_][:], bk[:, sg * QS:(sg + 1) * QS, :])
            P.dma("sp", uvb[sb_][64:128, :, :], vv[:, sg * QS:(sg + 1) * QS, :])
        F = fmb[sb_]
        tsl = slice(q * 64, (q + 1) * 64)
        LD2 = ldb[sb_][:, q, :]
        UV = uvb[sb_]
        pb = n % W
        e1, e2, e3, AR, BKt, BKP, SC, L0 = e1s[pb], e2s[pb], e3s[pb], ARs[pb], BKts[pb], BKPs[pb], SCs[pb], L0s[pb]
        TTf, XV, X0, O1 = TTfs[pb], XVs[pb], X0s[n % 2], O1s[n % 2]
        NTc, Lbc = NTs[2 * pb:2 * pb + 2], Lbs[2 * pb:2 * pb + 2]
        ps1 = psum()
        P.mm(ps1[:, 0:128], LD2, tri2[:, :], True, True)
        ps2 = psum()
        P.mm(ps2[:, 0:128], trigt2[:, :], LD2, True, True)
        P.act(e1[:, :], ps1[:, 0:128], AF.Exp)
        P.act(e2[:, :], ps1[:, 0:64], AF.Exp, scale=-1.0)
        P.act(e3[:, :], ps2[:, 0:128], AF.Exp)
        P.tt("dve", AR[:, 0:64], F[:, 1, tsl], e1[:, 64:128], ALU.mult)
        P.tt("pool", AR[:, 64:128], F[:, 0, tsl], e1[:, 0:64], ALU.mult)
        P.tt("dve", BKt[:, 0:64], F[:, 2, tsl], e2[:, :], ALU.mult)
        P.tt("pool", BKt[:, 64:128], F[:, 3, tsl], e2[:, :], ALU.mult)
        P.tt("pool", BKP[:, :], bkb[sb_][:, q, :], e3[:, :], ALU.mult)
        yield
        for h in range(2):
            hs = slice(h * 64, (h + 1) * 64)
            psS = psum()
            psL = psum()
            P.mm(psS[:, 0:128], BKt[hs, :], AR[hs, :], True, True)
            P.mm(psL[0:64, 0:64], AR[hs, 0:64], BKt[hs, 0:64], True, True)
            P.tt("dve", SC[:, h * 128:(h + 1) * 128], psS[:, 0:128], mask_sc[:, 0:128], ALU.mult)
            P.tt("dve", L0[:, hs], psL[0:64, 0:64], mask_l[:, 0:64], ALU.mult)
        yield
        SCh = h2(SC[0:64, :], 128)
        NT1, L1 = NTc[0], Lbc[0]
        psA = psum()
        psB = psum()
        for h in range(2):
            hs = slice(h * 64, (h + 1) * 64)
            P.mm(psA[0:64, hs], L0[:, hs], SC[0:64, h * 128:h * 128 + 64], True, True)
            P.mm(psB[0:64, hs], SC[0:64, h * 128:h * 128 + 64], L0[:, hs], True, True)
        P.cp("act", h2(NT1[:, :], 128)[:, :, 0:64], h2(psA[0:64, 0:128], 64))
        P.tt("dve", h2(NT1[:, :], 128)[:, :, 64:128], h2(ident2[:, :], 64), SCh[:, :, 0:64], ALU.add)
        P.cp("act", L1[:, :], psB[0:64, 0:128])
        yield
        for k in range(1, 5):
            cur, nxt, Lc, Ln = NTc[(k - 1) % 2], NTc[k % 2], Lbc[(k - 1) % 2], Lbc[k % 2]
            psA = psum()
            psB = psum()
            for h in range(2):
                hs = slice(h * 64, (h + 1) * 64)
                P.mm(psA[0:64, h * 128:(h + 1) * 128], Lc[:, hs], cur[:, h * 128:(h + 1) * 128], True, True)
                P.mm(psB[0:64, hs], cur[:, h * 128:h * 128 + 64], Lc[:, hs], True, True)
            P.cp("act", h2(nxt[:, :], 128)[:, :, 0:64], h2(psA[0:64, 0:256], 128)[:, :, 0:64])
            P.tt("dve", h2(nxt[:, :], 128)[:, :, 64:128], h2(cur[:, :], 128)[:, :, 64:128],
                 h2(psA[0:64, 0:256], 128)[:, :, 64:128], ALU.add)
            P.cp("act", Ln[:, :], psB[0:64, 0:128])
            yield
        cur, Lc = NTc[0], Lbc[0]
        psA = psum()
        for h in range(2):
            hs = slice(h * 64, (h + 1) * 64)
            P.mm(psA[0:64, hs], Lc[:, hs], cur[:, h * 128 + 64:(h + 1) * 128], True, True)
        P.tt("dve", h2(TTf[:, :], 64), h2(cur[:, :], 128)[:, :, 64:128], h2(psA[0:64, 0:128], 64), ALU.add)
        psX = psum()
        for h in range(2):
            hs = slice(h * 64, (h + 1) * 64)
            P.mm(psX[0:64, hs], SC[64:128, h * 128:h * 128 + 64], UV[64:128, q, hs], True, True)
        P.cp("act", XV[:, :], psX[0:64, 0:128])
        yield
        for h in range(2):
            hs = slice(h * 64, (h + 1) * 64)
            psH = psum()
            psO1 = psum()
            P.mm(psH[0:64, 0:64], AR[hs, 0:64], Hs[hs, :], True, True)
            P.mm(psO1[0:64, 0:64], AR[hs, 64:128], Hs[hs, :], True, True)
            P.tt("dve", X0[:, hs], psH[0:64, 0:64], XV[:, hs], ALU.add)
            P.cp("act", O1[:, hs], psO1[0:64, 0:64])
        psU = psum()
        for h in range(2):
            hs = slice(h * 64, (h + 1) * 64)
            P.mm(psU[0:64, hs], TTf[:, hs], X0[:, hs], True, True)
        P.cp("dve", UV[0:64, q, :], psU[0:64, 0:128])
        psO2 = psum()
        psHn = psum()
        for h in range(2):
            hs = slice(h * 64, (h + 1) * 64)
            P.mm(psO2[0:64, hs], SC[:, h * 128 + 64:(h + 1) * 128], UV[:, q, hs], True, True)
            P.mm(psHn[hs, 0:64], BKP[:, hs], UV[:, q, hs], True, True)
        P.tt("pool" if False else "dve", yb[sb_][:, q, :], psO2[0:64, 0:128], O1[:, :], ALU.add)
        P.stt("dve", Hs[:, :], Hs[:, :], e1[:, 63:64], psHn[:, 0:64], ALU.mult, ALU.add)
        if q == QS - 1:
            P.dma("sp", yout[:, sg * QS:(sg + 1) * QS, :], yb[sb_][:])

    active = []
    nxt_n = 0
    while nxt_n < NCH or active:
        if nxt_n < NCH:
            active.append(chunk(nxt_n))
            nxt_n += 1
        for g in list(active):
            try:
                next(g)
            except StopIteration:
                active.remove(g)
    P.wait_all("sp", [b.r for b in yb] + [b.r for b in hbuf])
    return P.finish()


def build_odd_post(NT, T):
    tp = TP(T)
    P = tp.P
    x1in = P.dram("xin", [NT, 128, 8, T], F32)
    qin = P.dram("qin", [NT, 4, 128, 7, T], F32)
    xout = P.dram("xout", [NT, 128, 8, T], F32, "ExternalOutput")
    owo = P.dram("owo", [2, 128, 8, 512], BF16)
    w2i = P.dram("w2i", [11, 128, 8, 512], BF16)
    w2o = P.dram("w2o", [4, 128, 22, 256], BF16)
    bd64 = tp.const("bd64", [128, 128])
    pvp = tp.const("pvp", [128, 3, 4])
    tp.xattn_setup()
    qbs = [Buf(P, "qb%d" % i, [128, 7, T], F32) for i in range(2)]
    for i in range(NT):
        P.dma("sp", tp.xT[:, :, :T], x1in[i])
        for c in range(4):
            qb = qbs[c % 2]
            P.dma("sp", qb[:, :, :T], qin[i, c])
            y, lh, g, gg, r, kf, v = [qb[:, z, :T] for z in range(7)]
            ps_m = tp.psum()
            P.mm(ps_m[:, :T], bd64[:, :], y, True, True)
            ysq = tp.scr()
            P.act(ysq[:, :T], y, AF.Square)
            ps_q = tp.psum()
            P.mm(ps_q[:, :T], bd64[:, :], ysq[:, :T], True, True)
            mean = tp.lnm
            rstd = tp.lnr
            msq = tp.scr()
            P.ts("dve", mean[:, :T], ps_m[:, :T], 1.0 / 64, ALU.mult)
            P.tt("pool", msq[:, :T], mean[:, :T], mean[:, :T], ALU.mult)
            P.stt("dve", rstd[:, :T], ps_q[:, :T], 1.0 / 64, msq[:, :T], ALU.mult, ALU.subtract)
            P.ts("dve", rstd[:, :T], rstd[:, :T], GN_EPS, ALU.add)
            P.act(rstd[:, :T], rstd[:, :T], AF.Sqrt)
            P.recip(rstd[:, :T], rstd[:, :T])
            yn = tp.scr()
            P.tt("dve", yn[:, :T], y, mean[:, :T], ALU.subtract)
            P.tt("pool", yn[:, :T], yn[:, :T], rstd[:, :T], ALU.mult)
            P.act(yn[:, :T], yn[:, :T], AF.Identity, bias=pvp[:, 2, c:c + 1], scale=pvp[:, 1, c:c + 1])
            pr = tp.scr()
            P.stt("dve", pr[:, :T], r, pvp[:, 0, c:c + 1], kf, ALU.mult, ALU.mult)
            ps_b = tp.psum()
            P.mm(ps_b[:, :T], bd64[:, :], pr[:, :T], True, True)
            P.tt("dve", pr[:, :T], ps_b[:, :T], v, ALU.mult)
            P.tt("pool", yn[:, :T], yn[:, :T], pr[:, :T], ALU.add)
            P.tt("dve", tp.hT[:, c, :T], yn[:, :T], g, ALU.mult)
            P.tt("pool", tp.hT[:, 4 + c, :T], lh, gg, ALU.mult)
        tp.gemm_fm(owo, 2, 8, 512, lambda k: tp.hT[:, k, :T], T, tp.res_epi(1.0, T))
        tp.layernorm(1, 1.0, T)
        tp.xattn(2, T)
        tp.ffn(w2i, w2o, 3, T)
        P.dma("sp", xout[i], tp.xT[:, :, :T])
    P.wait_all("sp", [tp.xT.r])
    return P.finish()


def blockdiag(w):
    out = np.zeros((128, 4, 128), np.float32)
    for h in range(8):
        c, q = divmod(h, 2)
        out[q * 64:(q + 1) * 64, c, q * 64:(q + 1) * 64] = w[h]
    return out


def tok_major(a):
    S = a.shape[1]
    return np.ascontiguousarray(a.T.reshape(S // 64, 64, 128).transpose(1, 0, 2))


def run_odd(inp, bf, l, x, T):
    o = l // 2
    B, S, _ = x.shape
    maps, cps, tpc = common_maps(inp, bf, l, x, T)
    NT = tpc // T
    bd64 = np.kron(np.eye(2, dtype=np.float32), np.ones((64, 64), np.float32))
    pv = np.zeros((128, 12, 4), np.float32)
    for idx, name in enumerate(["rwkv_w0", "rwkv_a0", "rwkv_k_k", "rwkv_k_a", "lru_conv_b", "lru_b_a", "lru_b_x", "lru_lambda"]):
        pv[:, idx, :] = pvec(inp[name][o])
    for c, m in enumerate(maps):
        start = (c % cps) == 0
        m["w1i"] = ffn_in_blocks(bf["ffn1_w_in"][l])
        m["w1o"] = wblocks(bf["ffn1_w_out"][l], 256)
        m["owi"] = wblocks(bf["odd_w_in"][o], 256)
        m["mu"] = pvec(inp["rwkv_mu"][o])
        m["lora"] = np.ascontiguousarray(np.concatenate([inp["rwkv_w_up"][o], inp["rwkv_a_up"][o]], axis=0))
        m["gup"] = np.ascontiguousarray(inp["rwkv_g_up"][o])
        m["bd64"] = bd64
        m["pv"] = pv
        m["convw"] = np.ascontiguousarray(inp["lru_conv_w"][o].reshape(4, 4, 128).transpose(2, 0, 1))
        m["wabd"] = blockdiag(inp["lru_w_a"][o])
        m["wxbd"] = blockdiag(inp["lru_w_x"][o])
        m["hmask"] = np.full((128, 1), 0.0 if start else 1.0, np.float32)
    res = run(build_odd_pre(NT, T), maps)
    x1 = gather_x(res, "x1out", B, S, cps, tpc)
    Q = np.empty((B, 10, 512, S), np.float32)
    for c in range(NCORES):
        b, t0 = c // cps, (c % cps) * tpc
        qo = np.asarray(res[c]["qout"])
        Q[b, :, :, t0:t0 + tpc] = qo.transpose(1, 3, 2, 0, 4).reshape(10, 512, tpc)
    ii = np.arange(64)
    tri_incl = (ii[:, None] <= ii[None, :]).astype(np.float32)
    tri_strict = (ii[:, None] < ii[None, :]).astype(np.float32)
    tri_gt = (ii[:, None] > ii[None, :]).astype(np.float32)
    consts = {
        "tri2": np.concatenate([tri_incl, tri_strict], 1),
        "trigt2": np.concatenate([tri_gt, tri_gt], 1),
        "mask_sc": np.tile(np.concatenate([tri_strict, tri_incl], 1), (2, 2)),
        "mask_l": np.tile(tri_gt, (1, 2)),
        "ident2": np.tile(np.eye(64, dtype=np.float32), (1, 2)),
    }
    maps2 = []
    nhp = NCORES // B
    for c in range(NCORES):
        b, hp = c // nhp, c % nhp
        rows = slice(hp * 128, (hp + 1) * 128)
        m = dict(consts)
        m["fm"] = np.ascontiguousarray(np.stack([Q[b, 0, rows], Q[b, 4, rows], Q[b, 5, rows], Q[b, 1, rows]]))
        m["ld"] = tok_major(Q[b, 3, rows])
        m["bk"] = np.ascontiguousarray(np.concatenate([tok_major(Q[b, 5, rows]), tok_major(Q[b, 1, rows])], 0))
        m["vv"] = tok_major(Q[b, 2, rows])
        m["la"] = np.ascontiguousarray(Q[b, 6, rows])
        m["lb"] = np.ascontiguousarray(Q[b, 7, rows])
        maps2.append(m)
    res2 = run(build_rec(S), maps2)
    Y = np.empty((B, 512, S), np.float32)
    Hl = np.empty((B, 512, S), np.float32)
    for c in range(NCORES):
        b, hp = c // nhp, c % nhp
        rows = slice(hp * 128, (hp + 1) * 128)
        yo = np.asarray(res2[c]["yout"])
        Y[b, rows] = yo.transpose(1, 0, 2).reshape(S, 128).T
        Hl[b, rows] = np.asarray(res2[c]["hout"])
    maps3, _, _ = common_maps(inp, bf, l, x1, T)
    xattn_maps(maps3, inp, bf, l, cps)
    pvp = np.stack([pvec(inp["rwkv_r_k"][o].reshape(-1)), pvec(inp["rwkv_gn_g"][o]), pvec(inp["rwkv_gn_b"][o])], 1)
    for c, m in enumerate(maps3):
        b, t0 = c // cps, (c % cps) * tpc
        sl = slice(t0, t0 + tpc)
        arrs = np.stack([Y[b][:, sl], Hl[b][:, sl], Q[b, 8][:, sl], Q[b, 9][:, sl], Q[b, 0][:, sl], Q[b, 1][:, sl], Q[b, 2][:, sl]])
        m["qin"] = np.ascontiguousarray(arrs.reshape(7, 4, 128, NT, T).transpose(3, 1, 2, 0, 4))
        del m["xh"]
        m["owo"] = wblocks(bf["odd_w_out"][o], 512)
        m["w2i"] = ffn_in_blocks(bf["ffn2_w_in"][l])
        m["w2o"] = wblocks(bf["ffn2_w_out"][l], 256)
        m["bd64"] = bd64
        m["pvp"] = np.ascontiguousarray(pvp)
    res3 = run(build_odd_post(NT, T), maps3)
    return gather_x(res3, "xout", B, S, cps, tpc), dict(x1=x1, Q=Q, Y=Y, Hl=Hl)


BIG = ["ffn1_w_in", "ffn1_w_out", "ffn2_w_in", "ffn2_w_out", "xattn_w_q", "xattn_w_kv", "xattn_w_o",
       "even_w_in", "even_w_out", "pool_w", "sgu_w", "odd_w_in", "odd_w_out"]


def kernel(**inputs):
    inp = {k: np.ascontiguousarray(np.asarray(v, dtype=np.float32)) for k, v in inputs.items()}
    bf = cast_weights({n: inp[n] for n in BIG})
    x = inp["x"]
    T = 512
    for l in range(DEPTH):
        if l % 2 == 0:
            x = run_even(inp, bf, l, x, T)
        else:
            x, _ = run_odd(inp, bf, l, x, T)
    return x.astype(np.float32)
```
